# Optimizing a Trainium2 kernel written in Bass

```python
import math
import jax
import jax.numpy as jnp
from jax import lax
import numpy as np

D_MODEL = 1024
BATCH = 8
SEQ = 4096
DEPTH = 2

N_EVEN = (DEPTH + 1) // 2
N_ODD = DEPTH // 2
EPS = 1e-5

SSD_HEADS = 16
SSD_HEAD_DIM = 64
SSD_INNER = SSD_HEADS * SSD_HEAD_DIM
SSD_GROUPS = 4
SSD_STATE = 128
SSD_CONV = 4
SSD_CHUNK = 128
SSD_XBC = SSD_INNER + 2 * SSD_GROUPS * SSD_STATE

CONF_DIM = 1024
CONF_KERNEL = 31

EVEN_IN = SSD_INNER + SSD_XBC + SSD_HEADS + 2 * CONF_DIM
EVEN_MIX = SSD_INNER + CONF_DIM

S5_DIM = 512
S5_GROUP = 16
S5_GROUPS = S5_DIM // S5_GROUP
S5_STATE = 64

RET_HEADS = 4
RET_KEY_DIM = 256
RET_VAL_DIM = 256
RET_CHUNK = 128
RET_QK = RET_HEADS * RET_KEY_DIM
RET_V = RET_HEADS * RET_VAL_DIM
ROPE_BASE = 10000.0

ODD_IN = S5_DIM + 2 * RET_QK + 2 * RET_V
ODD_MIX = S5_DIM + RET_V

FFN_HIDDEN = 256 * math.ceil(8 * D_MODEL / 3 / 256)

kernel_name = 'hybrid_ssd_conformer_s5_retention_block'


def rmsnorm(x, g):
    xf = x.astype(jnp.float32)
    y = xf * lax.rsqrt(jnp.mean(xf * xf, axis=-1, keepdims=True) + EPS)
    return (y * g.astype(jnp.float32)).astype(x.dtype)


def layernorm(x, g, b):
    xf = x.astype(jnp.float32)
    mu = jnp.mean(xf, axis=-1, keepdims=True)
    var = jnp.mean(jnp.square(xf - mu), axis=-1, keepdims=True)
    y = (xf - mu) * lax.rsqrt(var + EPS)
    return (y * g.astype(jnp.float32) + b.astype(jnp.float32)).astype(x.dtype)


def causal_depthwise_conv(x, w, bias):
    k, ch = w.shape
    y = lax.conv_general_dilated(x, w[:, None, :].astype(x.dtype), window_strides=(1,),
                                 padding=[(k - 1, 0)], dimension_numbers=('NWC', 'WIO', 'NWC'),
                                 feature_group_count=ch)
    return y + bias.astype(x.dtype)


def segsum_exp(a):
    t = a.shape[-1]
    cs = jnp.cumsum(a, axis=-1)
    diff = cs[..., :, None] - cs[..., None, :]
    mask = jnp.tril(jnp.ones((t, t), dtype=bool))
    return jnp.exp(jnp.where(mask, diff, -jnp.inf))


def ssd_chunked(xh, dt, a_head, bm, cm):
    b, s, h, p = xh.shape
    g, n = bm.shape[2], bm.shape[3]
    r = h // g
    c, l = s // SSD_CHUNK, SSD_CHUNK
    a = jnp.moveaxis((dt * a_head).reshape(b, c, l, g, r), 2, -1)
    a_cs = jnp.cumsum(a, axis=-1)
    xdt = (xh * dt[..., None]).reshape(b, c, l, g, r, p)
    bc = bm.reshape(b, c, l, g, n)
    cc = cm.reshape(b, c, l, g, n)
    cb = jnp.einsum('bclgn,bcsgn->bcgls', cc, bc)
    scores = cb[:, :, :, None] * segsum_exp(a)
    y_diag = jnp.einsum('bcgrls,bcsgrp->bclgrp', scores, xdt)
    decay_states = jnp.moveaxis(jnp.exp(a_cs[..., -1:] - a_cs), -1, 2)
    states = jnp.einsum('bclgn,bclgrp->bcgrpn', bc, xdt * decay_states[..., None])
    states = jnp.concatenate([jnp.zeros_like(states[:, :1]), states], axis=1)
    chunk_a = jnp.pad(jnp.moveaxis(a_cs[..., -1], 1, -1), [(0, 0), (0, 0), (0, 0), (1, 0)])
    decay_chunk = segsum_exp(chunk_a)
    prev = jnp.einsum('bgrzc,bcgrpn->bzgrpn', decay_chunk, states)[:, :-1]
    state_decay_out = jnp.moveaxis(jnp.exp(a_cs), -1, 2)
    y_off = jnp.einsum('bclgn,bcgrpn->bclgrp', cc, prev) * state_decay_out[..., None]
    return (y_diag + y_off).reshape(b, s, h, p)


def mamba2_mixer(z, xbc, dt_raw, conv_w, conv_b, dt_bias, a_log, d_skip, norm_g):
    b, s, _ = z.shape
    xbc = jax.nn.silu(causal_depthwise_conv(xbc, conv_w, conv_b)).astype(jnp.float32)
    gn = SSD_GROUPS * SSD_STATE
    xs = xbc[..., :SSD_INNER].reshape(b, s, SSD_HEADS, SSD_HEAD_DIM)
    bm = xbc[..., SSD_INNER:SSD_INNER + gn].reshape(b, s, SSD_GROUPS, SSD_STATE)
    cm = xbc[..., SSD_INNER + gn:].reshape(b, s, SSD_GROUPS, SSD_STATE)
    dt = jax.nn.softplus(dt_raw.astype(jnp.float32) + dt_bias.astype(jnp.float32))
    a_head = -jnp.exp(a_log.astype(jnp.float32))
    y = ssd_chunked(xs, dt, a_head, bm, cm) + d_skip.astype(jnp.float32)[:, None] * xs
    y = y.reshape(b, s, SSD_INNER) * jax.nn.silu(z.astype(jnp.float32))
    yg = y.reshape(b, s, SSD_GROUPS, SSD_INNER // SSD_GROUPS)
    yg = yg * lax.rsqrt(jnp.mean(yg * yg, axis=-1, keepdims=True) + EPS)
    return (yg.reshape(b, s, SSD_INNER) * norm_g.astype(jnp.float32)).astype(z.dtype)


def conformer_conv_mixer(u, conv_w, conv_b, ln_g, ln_b):
    a, gate = u[..., :CONF_DIM], u[..., CONF_DIM:]
    h = a * jax.nn.sigmoid(gate)
    h = causal_depthwise_conv(h, conv_w, conv_b)
    h = layernorm(h, ln_g, ln_b)
    return jax.nn.silu(h)


def s5_mixer(u, lam_re, lam_im, b_re, b_im, c_re, c_im, log_dt, d_skip, glu_w, glu_b):
    b, s, _ = u.shape
    f32 = jnp.float32
    uf = u.astype(f32).reshape(b, s, S5_GROUPS, S5_GROUP)
    lam = lax.complex(lam_re.astype(f32), lam_im.astype(f32))
    dt = jnp.exp(log_dt.astype(f32))[:, None]
    lam_bar = jnp.exp(lam * dt)
    b_bar = ((lam_bar - 1.0) / lam)[..., None] * lax.complex(b_re.astype(f32), b_im.astype(f32))
    bu = jnp.einsum('bsgc,gpc->bsgp', uf, b_bar)
    a_el = jnp.broadcast_to(lam_bar, bu.shape)

    def combine(e1, e2):
        a1, x1 = e1
        a2, x2 = e2
        return a2 * a1, a2 * x1 + x2

    _, states = lax.associative_scan(combine, (a_el, bu), axis=1)
    cmat = lax.complex(c_re.astype(f32), c_im.astype(f32))
    y = jnp.einsum('bsgp,gcp->bsgc', states, cmat).real
    y = y + d_skip.astype(f32).reshape(S5_GROUPS, S5_GROUP) * uf
    y = jax.nn.gelu(y.reshape(b, s, S5_DIM))
    y = y * jax.nn.sigmoid(y @ glu_w.astype(f32) + glu_b.astype(f32))
    return y.astype(u.dtype)


def rotary(t, cos, sin):
    half = t.shape[-1] // 2
    t1, t2 = t[..., :half], t[..., half:]
    return jnp.concatenate([t1 * cos - t2 * sin, t1 * sin + t2 * cos], axis=-1)


def retention_mixer(q, k, v, gate, cos, sin, gn_g, gn_b):
    b, s, _ = q.shape
    f32 = jnp.float32
    c, l = s // RET_CHUNK, RET_CHUNK
    q = rotary(q.astype(f32).reshape(b, s, RET_HEADS, RET_KEY_DIM), cos, sin)
    k = rotary(k.astype(f32).reshape(b, s, RET_HEADS, RET_KEY_DIM), cos, sin) * RET_KEY_DIM ** -0.5
    v = v.astype(f32).reshape(b, s, RET_HEADS, RET_VAL_DIM)
    gamma = 1.0 - jnp.exp(jnp.linspace(math.log(1.0 / 32), math.log(1.0 / 512), RET_HEADS, dtype=f32))
    log_g = jnp.log(gamma)
    qc = q.reshape(b, c, l, RET_HEADS, RET_KEY_DIM)
    kc = k.reshape(b, c, l, RET_HEADS, RET_KEY_DIM)
    vc = v.reshape(b, c, l, RET_HEADS, RET_VAL_DIM)
    idx = jnp.arange(l, dtype=f32)
    diff = idx[:, None] - idx[None, :]
    dmat = jnp.where(diff >= 0, jnp.exp(log_g[:, None, None] * jnp.maximum(diff, 0.0)), 0.0)
    scores = jnp.einsum('bclhd,bcshd->bchls', qc, kc) * dmat
    intra = jnp.einsum('bchls,bcshv->bclhv', scores, vc)
    k_dec = jnp.exp(log_g[None, :] * (l - 1.0 - idx)[:, None])
    kv = jnp.einsum('bclhd,bclhv->bchdv', kc * k_dec[:, :, None], vc)
    chunk_g = jnp.exp(log_g * l)[None, :, None, None]

    def step(state, kv_i):
        return chunk_g * state + kv_i, state

    init = jnp.zeros((b, RET_HEADS, RET_KEY_DIM, RET_VAL_DIM), f32)
    _, prev = lax.scan(step, init, jnp.moveaxis(kv, 1, 0))
    prev = jnp.moveaxis(prev, 0, 1)
    q_dec = jnp.exp(log_g[None, :] * (idx + 1.0)[:, None])
    cross = jnp.einsum('bclhd,bchdv->bclhv', qc * q_dec[:, :, None], prev)
    o = (intra + cross).reshape(b, s, RET_HEADS, RET_VAL_DIM)
    mu = jnp.mean(o, axis=-1, keepdims=True)
    var = jnp.mean(jnp.square(o - mu), axis=-1, keepdims=True)
    o = ((o - mu) * lax.rsqrt(var + EPS)).reshape(b, s, RET_V)
    o = o * gn_g.astype(f32) + gn_b.astype(f32)
    return (jax.nn.silu(gate.astype(f32)) * o).astype(gate.dtype)


def even_layer(x, norm_g, w_in, conv_w, conv_b, dt_bias, a_log, d_skip, ssd_norm,
               cf_w, cf_b, cf_g, cf_beta, w_out):
    u = rmsnorm(x, norm_g) @ w_in
    o1 = SSD_INNER
    o2 = o1 + SSD_XBC
    o3 = o2 + SSD_HEADS
    ya = mamba2_mixer(u[..., :o1], u[..., o1:o2], u[..., o2:o3], conv_w, conv_b,
                      dt_bias, a_log, d_skip, ssd_norm)
    yb = conformer_conv_mixer(u[..., o3:], cf_w, cf_b, cf_g, cf_beta)
    return x + jnp.concatenate([ya, yb], axis=-1) @ w_out


def odd_layer(x, cos, sin, norm_g, w_in, lam_re, lam_im, b_re, b_im, c_re, c_im, log_dt,
              s5_d, glu_w, glu_b, gn_g, gn_b, w_out):
    u = rmsnorm(x, norm_g) @ w_in
    o1 = S5_DIM
    o2 = o1 + RET_QK
    o3 = o2 + RET_QK
    o4 = o3 + RET_V
    yc = s5_mixer(u[..., :o1], lam_re, lam_im, b_re, b_im, c_re, c_im, log_dt, s5_d, glu_w, glu_b)
    yd = retention_mixer(u[..., o1:o2], u[..., o2:o3], u[..., o3:o4], u[..., o4:], cos, sin, gn_g, gn_b)
    return x + jnp.concatenate([yc, yd], axis=-1) @ w_out


def swiglu_ffn(x, g, w_gate, w_up, w_down):
    h = rmsnorm(x, g)
    return x + (jax.nn.silu(h @ w_gate) * (h @ w_up)) @ w_down


def _normal(k, shape, scale):
    return scale * jax.random.normal(k, shape, jnp.float32)


def setup_inputs(seed: int = 0) -> dict:
    key = jax.random.key(seed)
    ks = iter(jax.random.split(key, 48))
    E, O, D, F = N_EVEN, N_ODD, D_MODEL, FFN_HIDDEN
    G, P = S5_GROUPS, S5_STATE
    inp = {}
    inp['x'] = _normal(next(ks), (BATCH, SEQ, D), 1.0)
    inp['ev_norm'] = 1.0 + _normal(next(ks), (E, D), 0.02)
    inp['ev_w_in'] = _normal(next(ks), (E, D, EVEN_IN), D ** -0.5)
    inp['ev_conv_w'] = _normal(next(ks), (E, SSD_CONV, SSD_XBC), SSD_CONV ** -0.5)
    inp['ev_conv_b'] = _normal(next(ks), (E, SSD_XBC), 0.02)
    dt0 = jnp.exp(jax.random.uniform(next(ks), (E, SSD_HEADS), jnp.float32, math.log(1e-3), math.log(1e-1)))
    inp['ev_dt_bias'] = dt0 + jnp.log(-jnp.expm1(-dt0))
    inp['ev_a_log'] = jnp.log(jax.random.uniform(next(ks), (E, SSD_HEADS), jnp.float32, 1.0, 16.0))
    inp['ev_d'] = 1.0 + _normal(next(ks), (E, SSD_HEADS), 0.02)
    inp['ev_ssd_norm'] = 1.0 + _normal(next(ks), (E, SSD_INNER), 0.02)
    inp['ev_cf_conv_w'] = _normal(next(ks), (E, CONF_KERNEL, CONF_DIM), CONF_KERNEL ** -0.5)
    inp['ev_cf_conv_b'] = _normal(next(ks), (E, CONF_DIM), 0.02)
    inp['ev_cf_ln_g'] = 1.0 + _normal(next(ks), (E, CONF_DIM), 0.02)
    inp['ev_cf_ln_b'] = _normal(next(ks), (E, CONF_DIM), 0.02)
    inp['ev_w_out'] = _normal(next(ks), (E, EVEN_MIX, D), EVEN_MIX ** -0.5)
    inp['od_norm'] = 1.0 + _normal(next(ks), (O, D), 0.02)
    inp['od_w_in'] = _normal(next(ks), (O, D, ODD_IN), D ** -0.5)
    n_idx = jnp.arange(P, dtype=jnp.float32)
    inp['od_lam_re'] = -0.5 + _normal(next(ks), (O, G, P), 0.01)
    inp['od_lam_im'] = math.pi * n_idx + _normal(next(ks), (O, G, P), 0.01)
    inp['od_b_re'] = _normal(next(ks), (O, G, P, S5_GROUP), (2 * S5_GROUP) ** -0.5)
    inp['od_b_im'] = _normal(next(ks), (O, G, P, S5_GROUP), (2 * S5_GROUP) ** -0.5)
    inp['od_c_re'] = _normal(next(ks), (O, G, S5_GROUP, P), P ** -0.5)
    inp['od_c_im'] = _normal(next(ks), (O, G, S5_GROUP, P), P ** -0.5)
    inp['od_log_dt'] = jax.random.uniform(next(ks), (O, G), jnp.float32, math.log(1e-3), math.log(1e-1))
    inp['od_s5_d'] = _normal(next(ks), (O, S5_DIM), 1.0)
    inp['od_glu_w'] = _normal(next(ks), (O, S5_DIM, S5_DIM), S5_DIM ** -0.5)
    inp['od_glu_b'] = _normal(next(ks), (O, S5_DIM), 0.02)
    inp['od_gn_g'] = 1.0 + _normal(next(ks), (O, RET_V), 0.02)
    inp['od_gn_b'] = _normal(next(ks), (O, RET_V), 0.02)
    inp['od_w_out'] = _normal(next(ks), (O, ODD_MIX, D), ODD_MIX ** -0.5)
    inp['ffn_norm'] = 1.0 + _normal(next(ks), (DEPTH, D), 0.02)
    inp['ffn_w_gate'] = _normal(next(ks), (DEPTH, D, F), D ** -0.5)
    inp['ffn_w_up'] = _normal(next(ks), (DEPTH, D, F), D ** -0.5)
    inp['ffn_w_down'] = _normal(next(ks), (DEPTH, F, D), F ** -0.5)
    inp['final_norm'] = 1.0 + _normal(next(ks), (D,), 0.02)
    return inp


def reference(x, ev_norm, ev_w_in, ev_conv_w, ev_conv_b, ev_dt_bias, ev_a_log, ev_d, ev_ssd_norm,
              ev_cf_conv_w, ev_cf_conv_b, ev_cf_ln_g, ev_cf_ln_b, ev_w_out,
              od_norm, od_w_in, od_lam_re, od_lam_im, od_b_re, od_b_im, od_c_re, od_c_im, od_log_dt,
              od_s5_d, od_glu_w, od_glu_b, od_gn_g, od_gn_b, od_w_out,
              ffn_norm, ffn_w_gate, ffn_w_up, ffn_w_down, final_norm):
    s = x.shape[1]
    inv_freq = ROPE_BASE ** (-jnp.arange(0, RET_KEY_DIM, 2, dtype=jnp.float32) / RET_KEY_DIM)
    ang = jnp.arange(s, dtype=jnp.float32)[:, None] * inv_freq[None, :]
    cos = jnp.cos(ang)[:, None, :]
    sin = jnp.sin(ang)[:, None, :]
    for layer in range(DEPTH):
        i = layer // 2
        if layer % 2 == 0:
            x = even_layer(x, ev_norm[i], ev_w_in[i], ev_conv_w[i], ev_conv_b[i], ev_dt_bias[i],
                           ev_a_log[i], ev_d[i], ev_ssd_norm[i], ev_cf_conv_w[i], ev_cf_conv_b[i],
                           ev_cf_ln_g[i], ev_cf_ln_b[i], ev_w_out[i])
        else:
            x = odd_layer(x, cos, sin, od_norm[i], od_w_in[i], od_lam_re[i], od_lam_im[i],
                          od_b_re[i], od_b_im[i], od_c_re[i], od_c_im[i], od_log_dt[i], od_s5_d[i],
                          od_glu_w[i], od_glu_b[i], od_gn_g[i], od_gn_b[i], od_w_out[i])
        x = swiglu_ffn(x, ffn_norm[layer], ffn_w_gate[layer], ffn_w_up[layer], ffn_w_down[layer])
    return rmsnorm(x, final_norm)
```

```python
import math
import os
import numpy as np
from contextlib import ExitStack
import concourse.bass as bass
import concourse.mybir as mybir
from concourse.bass_utils import run_bass_kernel_spmd

F32 = mybir.dt.float32
BF16 = mybir.dt.bfloat16
ALU = mybir.AluOpType
AF = mybir.ActivationFunctionType
AX = mybir.AxisListType

ENGS = ("pe", "act", "dve", "pool", "sp")
SAME_ENG_SYNC = True
SAME_ENG_WINDOW = 1
FFN_WQ = "pool"

S = 4096
D = 1024
FF = 2816
EPS = 1e-5
NB = 512
NBLK = S // NB
ARENA = 51 * 1024


class Res:
    __slots__ = ("name", "w", "rs")

    def __init__(self, name=""):
        self.name = name
        self.w = None
        self.rs = []


class Op:
    __slots__ = ("eng", "fn", "deps", "dma_key", "signal", "ev", "waits", "is_dma", "dma_cnt", "idx", "phase")


class Prog:
    def __init__(self, nc, arena_words):
        self.nc = nc
        self.stack = ExitStack()
        self.ops = {e: [] for e in ENGS}
        self.dma_counts = {}
        self.dma_last = {}
        self.arena = self.stack.enter_context(nc.sbuf_tensor("arena", [128, arena_words], F32))
        self.arena_words = arena_words
        self.off = 0
        self.keep = 0
        self.phase = 0

    def alloc(self, nwords):
        o = self.off
        self.off += nwords
        assert self.off <= self.arena_words, ("arena overflow", self.off, self.arena_words)
        return o

    def tile(self, shape, dt, name=None):
        n = 1
        for s_ in shape[1:]:
            n *= s_
        if dt == BF16:
            words = (n + 1) // 2
        else:
            words = n
        o = self.alloc(words)
        ap = self.arena[0:shape[0], o:o + words]
        if dt == BF16:
            ap = ap.bitcast(BF16)
            if n % 2:
                ap = ap[:, 0:n]
        if len(shape) == 3:
            ap = ap.rearrange("p (a b) -> p a b", a=shape[1])
        elif len(shape) == 4:
            ap = ap.rearrange("p (a b c) -> p a b c", a=shape[1], b=shape[2])
        return ap

    def reset(self):
        self.off = self.keep

    def ps(self, name, shape, dt):
        return self.stack.enter_context(self.nc.psum_tensor(name, list(shape), dt))

    def op(self, eng, fn, reads=(), writes=(), dma_key=None):
        o = Op()
        o.eng = eng
        o.fn = fn
        o.dma_key = dma_key
        o.is_dma = dma_key is not None
        o.signal = o.is_dma
        o.ev = None
        o.idx = len(self.ops[eng])
        o.phase = self.phase
        raw = {}
        oth = {}
        for r in reads:
            if r.w is not None:
                raw[id(r.w)] = r.w
        for w in writes:
            if w.w is not None:
                oth[id(w.w)] = w.w
            last = {}
            for rd in w.rs:
                k = rd.dma_key if rd.is_dma else rd.eng
                last[k] = rd
            for rd in last.values():
                oth[id(rd)] = rd
        dl = []
        seen = set()
        for is_raw, dd in ((True, raw), (False, oth)):
            for d in dd.values():
                if d is o or id(d) in seen:
                    continue
                if (not d.is_dma) and d.eng == eng and not o.is_dma:
                    if eng == "pe":
                        continue
                    if (not is_raw) or (o.idx - d.idx > SAME_ENG_WINDOW):
                        continue
                seen.add(id(d))
                if d.is_dma:
                    dl.append((d, self.dma_counts[d.dma_key]))
                else:
                    d.signal = True
                    dl.append((d, None))
        o.deps = dl
        if o.is_dma:
            self.dma_counts[dma_key] = self.dma_counts.get(dma_key, 0) + 1
            o.dma_cnt = self.dma_counts[dma_key]
            self.dma_last[dma_key] = o
        for r in reads:
            r.rs.append(o)
        for w in writes:
            w.w = o
            w.rs = []
        self.ops[eng].append(o)
        return o

    def barrier(self):
        lasts = []
        for e in ENGS:
            for o in reversed(self.ops[e]):
                if (not o.is_dma) and o.fn is not None:
                    o.signal = True
                    lasts.append(o)
                    break
        for e in ENGS:
            w = Op()
            w.eng = e
            w.fn = None
            w.is_dma = False
            w.signal = False
            w.dma_key = None
            w.ev = None
            w.idx = len(self.ops[e])
            w.phase = self.phase
            w.deps = [(o, None) for o in lasts if (o.eng != e or SAME_ENG_SYNC)]
            w.deps += [(o, self.dma_counts[k]) for k, o in self.dma_last.items()]
            self.ops[e].append(w)
        self.phase += 1

    def emit(self):
        nc = self.nc
        st = self.stack
        sems = {}
        dsems = {k: st.enter_context(nc.semaphore("d_" + k)) for k in self.dma_counts}
        self.sem_max = 0
        for e in ENGS:
            c = {}
            for o in self.ops[e]:
                if o.is_dma:
                    o.ev = (dsems[o.dma_key], 16 * o.dma_cnt)
                    self.sem_max = max(self.sem_max, 16 * o.dma_cnt)
                elif o.signal:
                    k = (e, o.phase)
                    if k not in sems:
                        sems[k] = st.enter_context(nc.semaphore("s_%s%d" % k))
                    c[k] = c.get(k, 0) + 1
                    o.ev = (sems[k], c[k])
                    self.sem_max = max(self.sem_max, c[k])
        assert self.sem_max < 4000, self.sem_max
        nwait = 0
        for e in ENGS:
            seen = {}
            for o in self.ops[e]:
                w = {}
                for d, cnt in o.deps:
                    if d.is_dma:
                        s, v = dsems[d.dma_key], 16 * cnt
                    else:
                        s, v = d.ev
                    k = id(s)
                    if seen.get(k, 0) >= v:
                        continue
                    if k not in w or w[k][1] < v:
                        w[k] = (s, v)
                for k, (s, v) in w.items():
                    seen[k] = v
                o.waits = list(w.values())
                nwait += len(o.waits)
        self.nwait = nwait
        block = st.enter_context(nc.Block())

        def run(eng_name):
            def body(e):
                for o in self.ops[eng_name]:
                    for s, v in o.waits:
                        e.wait_ge(s, v)
                    if o.fn is None:
                        continue
                    ins = o.fn(e)
                    if o.signal:
                        ins.then_inc(o.ev[0], 16 if o.is_dma else 1)
            return body

        block.tensor(run("pe"))
        block.scalar(run("act"))
        block.vector(run("dve"))
        block.gpsimd(run("pool"))
        block.sync(run("sp"))


class Ctx:
    pass


def setup_common(P, C):
    nc = P.nc
    C.identf = P.tile([128, 128], F32)
    C.ident = P.tile([128, 128], BF16)
    C.onesf = P.tile([128, 128], F32)
    C.r_const = Res("const")
    P.op("dve", lambda e: e.memset(C.identf, 0.0), writes=[C.r_const])
    P.op("pool", lambda e: e.affine_select(out=C.identf, in_=C.identf, pattern=[[-1, 128]],
                                           compare_op=ALU.not_equal, fill=1.0, base=0,
                                           channel_multiplier=1),
         reads=[C.r_const], writes=[C.r_const])
    P.op("dve", lambda e: e.tensor_copy(out=C.ident, in_=C.identf), reads=[C.r_const], writes=[C.r_const])
    P.op("dve", lambda e: e.memset(C.onesf, 1.0), reads=[C.r_const], writes=[C.r_const])
    C.banks = [P.ps("pb%d" % i, [128, 512], F32) for i in range(8)]
    C.rb = [Res("pb%d" % i) for i in range(8)]
    P.keep = P.off


def bank_bf(C, i):
    return C.banks[i][:].bitcast(BF16)


def load_bcast_vec(P, dst, src_vec, res, key):
    n = dst.shape[1]
    P.op("sp", lambda e: e.dma_start(out=dst, in_=src_vec.rearrange("(o n) -> o n", o=1).broadcast_to([128, n])),
         writes=[res], dma_key=key)


def rmsnorm_tiles(P, C, xts, r_xts, g_bc, r_g, hT, r_hT, tp_banks, scr, r_scr):
    nt = len(xts)
    ss = scr["ss"]
    for i in range(nt):
        P.op("act", lambda e, i=i: e.activation(out=scr["junk"], in_=xts[i], func=AF.Square,
                                                accum_out=ss[:, i:i + 1]),
             reads=[r_xts[i]], writes=[r_scr["junk"], r_scr["ss"]])
    P.op("act", lambda e: e.activation(out=ss[:, 0:nt], in_=ss[:, 0:nt], func=AF.Sqrt, scale=1.0 / D, bias=EPS),
         reads=[r_scr["ss"]], writes=[r_scr["ss"]])
    P.op("dve", lambda e: e.reciprocal(out=ss[:, 0:nt], in_=ss[:, 0:nt]), reads=[r_scr["ss"]], writes=[r_scr["ss"]])
    for i in range(nt):
        hb = scr["hb"][i % 2]
        r_hb = r_scr["hb"][i % 2]
        P.op("dve", lambda e, i=i, hb=hb: e.scalar_tensor_tensor(out=hb, in0=xts[i], scalar=ss[:, i:i + 1], in1=g_bc,
                                                               op0=ALU.mult, op1=ALU.mult),
             reads=[r_xts[i], r_scr["ss"], r_g], writes=[r_hb])
        b = tp_banks[i % 2]
        tb = bank_bf(C, b)
        for k in range(8):
            P.op("pe", lambda e, k=k, hb=hb, tb=tb: e.transpose(out=tb[:, k * 128:(k + 1) * 128],
                                                               in_=hb[:, k * 128:(k + 1) * 128], identity=C.ident),
                 reads=[r_hb, C.r_const], writes=[C.rb[b]])
        P.op("act", lambda e, hT=hT, i=i, tb=tb: e.copy(out=hT[:, :, i * 128:(i + 1) * 128],
                                                in_=tb.rearrange("p (k n) -> p k n", k=8)),
             reads=[C.rb[b]], writes=[r_hT])


def rmsnorm_a(P, C, xts, r_xts, g_bc, r_g, scr, r_scr, hb4, r_hb4):
    ss = scr["ss"]
    for i in range(4):
        P.op("act", lambda e, i=i: e.activation(out=scr["junk"], in_=xts[i], func=AF.Square, accum_out=ss[:, i:i + 1]),
             reads=[r_xts[i]], writes=[r_scr["junk"], r_scr["ss"]])
    P.op("act", lambda e: e.activation(out=ss[:, 0:4], in_=ss[:, 0:4], func=AF.Sqrt, scale=1.0 / D, bias=EPS),
         reads=[r_scr["ss"]], writes=[r_scr["ss"]])
    P.op("dve", lambda e: e.reciprocal(out=ss[:, 0:4], in_=ss[:, 0:4]), reads=[r_scr["ss"]], writes=[r_scr["ss"]])
    for i in range(4):
        P.op("dve", lambda e, i=i: e.scalar_tensor_tensor(out=hb4[i], in0=xts[i], scalar=ss[:, i:i + 1], in1=g_bc,
                                                          op0=ALU.mult, op1=ALU.mult),
             reads=[r_xts[i], r_scr["ss"], r_g], writes=[r_hb4[i]])


def rmsnorm_b(P, C, hb4, r_hb4, hT, r_hT, tp_banks):
    for i in range(4):
        b = tp_banks[i % 2]
        tb = bank_bf(C, b)
        for k in range(8):
            P.op("pe", lambda e, k=k, i=i, tb=tb: e.transpose(out=tb[:, k * 128:(k + 1) * 128],
                                                             in_=hb4[i][:, k * 128:(k + 1) * 128], identity=C.ident),
                 reads=[r_hb4[i], C.r_const], writes=[C.rb[b]])
        P.op("act", lambda e, hT=hT, i=i, tb=tb: e.copy(out=hT[:, :, i * 128:(i + 1) * 128],
                                                        in_=tb.rearrange("p (k n) -> p k n", k=8)),
             reads=[C.rb[b]], writes=[r_hT])


def load_x_tiles(P, xin, r_xin, xt, r_xt, blk, keyfmt):
    t0 = blk * NB
    for i in range(4):
        P.op("sp", lambda e, i=i, t0=t0: e.dma_start(out=xt[i], in_=xin[t0 + i * 128:t0 + (i + 1) * 128, :]),
             reads=[r_xin[blk * 4 + i]], writes=[r_xt[i]], dma_key=keyfmt % i)


def phase_ffn(P, C, WS, layer, xin, r_xin, xout, r_xout, g_norm, g_final=None, nblk=NBLK, tag="f", hook=None):
    P.reset()
    g_bc = P.tile([128, D], F32)
    r_g = Res("g")
    load_bcast_vec(P, g_bc, g_norm, r_g, tag + "g")
    if g_final is not None:
        gf_bc = P.tile([128, D], F32)
        load_bcast_vec(P, gf_bc, g_final, r_g, tag + "g")
    xt = [[P.tile([128, D], F32) for _ in range(4)] for _ in range(2)]
    r_xt = [[Res("xt") for _ in range(4)] for _ in range(2)]
    hT = [P.tile([128, 8, NB], BF16) for _ in range(2)]
    r_hT = [Res("hT") for _ in range(2)]
    aT = P.tile([128, 22, NB], BF16)
    r_aT = [Res("aT%d" % f) for f in range(22)]
    NW = 4
    wgu = [P.tile([128, 8, 512], BF16) for _ in range(NW)]
    r_wgu = [Res("wgu") for _ in range(NW)]
    wd = [P.tile([128, 22, 512], BF16) for _ in range(2)]
    r_wd = [Res("wd") for _ in range(2)]
    sg = [P.tile([128, 512], F32) for _ in range(2)]
    r_sg = [Res("sg") for _ in range(2)]
    scr = {"ss": P.tile([128, 8], F32), "junk": P.tile([128, D], BF16),
           "hb": [P.tile([128, D], BF16) for _ in range(2)]}
    r_scr = {"ss": Res("ss"), "junk": Res("junk"), "hb": [Res("hb0"), Res("hb1")]}
    ss2 = P.tile([128, 8], F32)
    r_ss2 = Res("ss2")

    panels = [(c0, min(512, FF - c0)) for c0 in range(0, FF, 512)]
    wcnt = 0
    gcnt = 0
    dcnt = 0
    wdcnt = 0
    load_x_tiles(P, xin, r_xin, xt[0], r_xt[0], 0, tag + "x0%d")
    rmsnorm_tiles(P, C, xt[0], r_xt[0], g_bc, r_g, hT[0], r_hT[0], (0, 1), scr, r_scr)
    for blk in range(nblk):
        sl = blk % 2
        t0 = blk * NB
        if hook is not None:
            hook(blk)
        for pi, (c0, cw) in enumerate(panels):
            slots = []
            for nm in ("fg", "fu"):
                ws = wcnt % NW
                wcnt += 1
                WS.load(P, "%s%d_%d" % (nm, layer, pi), wgu[ws], r_wgu[ws], "%sw%d" % (tag, ws), eng=FFN_WQ)
                slots.append(ws)
            for cc in range(cw // 128):
                f = c0 // 128 + cc
                pg = 2 + 2 * (gcnt % 2)
                pu = pg + 1
                sgi = gcnt % 2
                gcnt += 1
                for (pb, ws) in ((pg, slots[0]), (pu, slots[1])):
                    for k in range(8):
                        P.op("pe", lambda e, hT=hT, pb=pb, ws=ws, k=k, cc=cc, sl=sl: e.matmul(
                            C.banks[pb][:], lhsT=wgu[ws][:, k, cc * 128:(cc + 1) * 128], rhs=hT[sl][:, k, :],
                            start=(k == 0), stop=(k == 7)),
                            reads=[r_wgu[ws], r_hT[sl]], writes=[C.rb[pb]])
                P.op("act", lambda e, pg=pg, sgi=sgi: e.activation(out=sg[sgi], in_=C.banks[pg][:], func=AF.Silu),
                     reads=[C.rb[pg]], writes=[r_sg[sgi]])
                P.op("dve", lambda e, hT=hT, pu=pu, sgi=sgi, f=f: e.tensor_tensor(out=aT[:, f, :], in0=sg[sgi], in1=C.banks[pu][:],
                                                                         op=ALU.mult),
                     reads=[C.rb[pu], r_sg[sgi]], writes=[r_aT[f]])
        if blk + 1 < nblk:
            nsl = (blk + 1) % 2
            load_x_tiles(P, xin, r_xin, xt[nsl], r_xt[nsl], blk + 1, tag + "x" + str(nsl) + "%d")
            rmsnorm_tiles(P, C, xt[nsl], r_xt[nsl], g_bc, r_g, hT[nsl], r_hT[nsl], (0, 1), scr, r_scr)
        for half in range(2):
            ws = wdcnt % 2
            wdcnt += 1
            WS.load(P, "fd%d_%d" % (layer, half), wd[ws], r_wd[ws], "%sd%d" % (tag, ws), eng=FFN_WQ)
            for i in range(4):
                pb = 6 + (dcnt % 2)
                dcnt += 1
                for f in range(22):
                    P.op("pe", lambda e, pb=pb, f=f, i=i, ws=ws: e.matmul(
                        C.banks[pb][:], lhsT=aT[:, f, i * 128:(i + 1) * 128], rhs=wd[ws][:, f, :],
                        start=(f == 0), stop=(f == 21)),
                        reads=[r_aT[f], r_wd[ws]], writes=[C.rb[pb]])
                xs_ = xt[sl][i][:, half * 512:(half + 1) * 512]
                P.op("dve", lambda e, pb=pb, xs_=xs_: e.tensor_tensor(out=xs_, in0=xs_, in1=C.banks[pb][:], op=ALU.add),
                     reads=[C.rb[pb], r_xt[sl][i]], writes=[r_xt[sl][i]])
                if half == 1:
                    if g_final is not None:
                        P.op("act", lambda e, i=i, sl=sl: e.activation(out=scr["junk"], in_=xt[sl][i], func=AF.Square,
                                                                       accum_out=ss2[:, i:i + 1]),
                             reads=[r_xt[sl][i]], writes=[r_scr["junk"], r_ss2])
                        P.op("act", lambda e, i=i: e.activation(out=ss2[:, i:i + 1], in_=ss2[:, i:i + 1], func=AF.Sqrt,
                                                                scale=1.0 / D, bias=EPS),
                             reads=[r_ss2], writes=[r_ss2])
                        P.op("dve", lambda e, i=i: e.reciprocal(out=ss2[:, i:i + 1], in_=ss2[:, i:i + 1]),
                             reads=[r_ss2], writes=[r_ss2])
                        P.op("dve", lambda e, i=i, sl=sl: e.scalar_tensor_tensor(
                            out=xt[sl][i], in0=xt[sl][i], scalar=ss2[:, i:i + 1], in1=gf_bc, op0=ALU.mult, op1=ALU.mult),
                            reads=[r_ss2, r_g, r_xt[sl][i]], writes=[r_xt[sl][i]])
                    P.op("sp", lambda e, i=i, sl=sl, t0=t0: e.dma_start(out=xout[t0 + i * 128:t0 + (i + 1) * 128, :],
                                                                        in_=xt[sl][i]),
                         reads=[r_xt[sl][i]], writes=[r_xout[blk * 4 + i]], dma_key=tag + "o")


WNAMES_FFN = ["ffn_norm", "ffn_w_gate", "ffn_w_up", "ffn_w_down", "final_norm"]


def build_ffn_only(layer=0, final=False, nblk=NBLK):
    nc = bass.Bass("TRN2", target_bir_lowering=False)
    xin = nc.dram_tensor("x", [S, D], F32, kind="ExternalInput").ap()
    g = nc.dram_tensor("ffn_norm", [2, D], F32, kind="ExternalInput").ap()
    wg = nc.dram_tensor("ffn_w_gate", [2, D, FF], F32, kind="ExternalInput").ap()
    wu = nc.dram_tensor("ffn_w_up", [2, D, FF], F32, kind="ExternalInput").ap()
    wdn = nc.dram_tensor("ffn_w_down", [2, FF, D], F32, kind="ExternalInput").ap()
    gf = nc.dram_tensor("final_norm", [D], F32, kind="ExternalInput").ap()
    out = nc.dram_tensor("out", [S, D], F32, kind="ExternalOutput").ap()
    P = Prog(nc, ARENA)
    C = Ctx()
    setup_common(P, C)
    r_in = [Res("xin") for _ in range(32)]
    r_out = [Res("xout") for _ in range(32)]
    WS = WStore(nc)
    define_panels(WS, Wf={"ffn_w_gate": wg, "ffn_w_up": wu, "ffn_w_down": wdn})
    WS.emit_group(P, "C" if layer == 0 else "E")
    phase_ffn(P, C, WS, layer, xin, r_in, out, r_out, g[layer], g_final=(gf if final else None), nblk=nblk)
    P.op("sp", None, reads=r_out)
    P.emit()
    return nc, P


def load_cols(P, dst, vec, res, key, nchunk):
    P.op("sp", lambda e: e.dma_start(out=dst, in_=vec.rearrange("(c p) -> p c", p=128), allow_slow_non_contiguous=True),
         writes=[res], dma_key=key)


def stream_w(P, slot_ap, r_slot, key, W, c0, cw, kc=8):
    P.op("pool", lambda e: e.dma_start(out=slot_ap[:, 0:kc, 0:cw], in_=W[:, c0:c0 + cw].rearrange("(k p) n -> p k n", p=128)),
         writes=r_slot if isinstance(r_slot, list) else [r_slot], dma_key=key)


class WStore:
    def __init__(self, nc):
        self.nc = nc
        self.specs = {}
        self.groups = {}

    def add(self, group, key, W, r0, nrows, c0, cw):
        kc = nrows // 128
        t = self.nc.dram_tensor("wb_" + key, [128, kc * cw], BF16, kind="Internal").ap()
        self.specs[key] = (t, Res(key), kc, cw, W, r0, nrows, c0, group)
        self.groups.setdefault(group, []).append(key)

    def emit_group(self, P, group):
        for key in self.groups.get(group, []):
            t, res, kc, cw, W, r0, nrows, c0, _ = self.specs[key]
            P.op("pool", lambda e, t=t, kc=kc, cw=cw, W=W, r0=r0, nrows=nrows, c0=c0: e.dma_start(
                out=t.rearrange("p (k n) -> p k n", k=kc), in_=W[r0:r0 + nrows, c0:c0 + cw].rearrange("(k p) n -> p k n", p=128)),
                writes=[res], dma_key="cv" + group)

    def load(self, P, key, slot_ap, r_slot, dma_key, eng="pool"):
        t, res, kc, cw, _, _, _, _, _ = self.specs[key]
        P.op(eng, lambda e, t=t, kc=kc, cw=cw: e.dma_start(out=slot_ap[:, 0:kc, 0:cw], in_=t.rearrange("p (k n) -> p k n", k=kc)),
             reads=[res], writes=r_slot if isinstance(r_slot, list) else [r_slot], dma_key=dma_key)


def define_panels(WS, We=None, Wo=None, Wf=None):
    if We is not None:
        for i in range(2):
            WS.add("A", "ea%d" % i, We["ev_w_in"], 0, 1024, 3088 + i * 512, 512)
            WS.add("A", "eg%d" % i, We["ev_w_in"], 0, 1024, 4112 + i * 512, 512)
        for i in range(2):
            WS.add("B", "ez%d" % i, We["ev_w_in"], 0, 1024, i * 512, 512)
        for i in range(4):
            WS.add("B", "ex%d" % i, We["ev_w_in"], 0, 1024, 1024 + i * 512, 512)
        for half in range(2):
            for part in range(2):
                WS.add("B", "eo%d%d" % (half, part), We["ev_w_out"], part * 1024, 1024, half * 512, 512)
    if Wf is not None:
        for l, grp in ((0, "C"), (1, "E")):
            for i, c0 in enumerate(range(0, FF, 512)):
                cw = min(512, FF - c0)
                WS.add(grp, "fg%d_%d" % (l, i), Wf["ffn_w_gate"][l], 0, 1024, c0, cw)
                WS.add(grp, "fu%d_%d" % (l, i), Wf["ffn_w_up"][l], 0, 1024, c0, cw)
            for half in range(2):
                WS.add(grp, "fd%d_%d" % (l, half), Wf["ffn_w_down"][l], 0, FF, half * 512, 512)
    if Wo is not None:
        WS.add("D", "ou", Wo["od_w_in"], 0, 1024, 0, 512)
        for nm, c0 in (("oq", 512), ("ok", 1536), ("ov", 2560), ("og", 3584)):
            for i in range(2):
                WS.add("D", "%s%d" % (nm, i), Wo["od_w_in"], 0, 1024, c0 + i * 512, 512)
        for half in range(2):
            WS.add("D", "oo%d0" % half, Wo["od_w_out"], 0, 1024, half * 512, 512)
            WS.add("D", "oo%d1" % half, Wo["od_w_out"], 1024, 512, half * 512, 512)


def phase_e1(P, C, WS, xin, r_xin, W, ybT, r_yb, nblk=NBLK, hook=None):
    P.reset()
    A0 = 3088
    G0 = 4112
    g_bc = P.tile([128, D], F32)
    r_g = Res("g")
    load_bcast_vec(P, g_bc, W["ev_norm"], r_g, "e1g")
    wnat = P.tile([34, D], F32)
    r_wnat = Res("wnat")
    r_colp = Res("colp")
    P.op("sp", lambda e: e.dma_start(out=wnat[0:31, :], in_=W["ev_cf_conv_w"]), writes=[r_wnat], dma_key="e1g")
    for i, nm in enumerate(["ev_cf_conv_b", "ev_cf_ln_g", "ev_cf_ln_b"]):
        P.op("sp", lambda e, i=i, nm=nm: e.dma_start(out=wnat[31 + i:32 + i, :], in_=W[nm].rearrange("(o n) -> o n", o=1)),
             reads=[r_wnat], writes=[r_wnat], dma_key="e1g")
    wall = P.tile([128, 8, 34], F32)
    wcv = wall[:, :, 0:31]
    colp = wall[:, :, 31:34].rearrange("p c j -> p j c")
    r_wcv = Res("wcv")
    for c in range(8):
        P.op("pe", lambda e, c=c: e.transpose(out=C.banks[0][:, c * 40:c * 40 + 34], in_=wnat[:, c * 128:(c + 1) * 128],
                                              identity=C.identf[0:34, 0:34]),
             reads=[r_wnat, C.r_const], writes=[C.rb[0]])
    P.op("act", lambda e: e.copy(out=wall, in_=C.banks[0][:, 0:320].rearrange("p (c j) -> p c j", c=8)[:, :, 0:34]),
         reads=[C.rb[0]], writes=[r_wcv, r_colp])
    diag = P.tile([128, 8, 31, 128], BF16)
    r_diag = {}
    n = 0
    for c in range(8):
        for j in range(0, 31):
            n += 1
            if n % 3 == 0:
                P.op("act", lambda e, c=c, j=j: e.activation(out=diag[:, c, j, :], in_=C.identf, func=AF.Identity,
                                                             scale=wcv[:, c, j:j + 1]),
                     reads=[r_wcv, C.r_const], writes=[r_diag.setdefault((c, j), Res("diag"))])
            else:
                P.op("dve", lambda e, c=c, j=j: e.tensor_scalar(out=diag[:, c, j, :], in0=C.identf, scalar1=wcv[:, c, j:j + 1],
                                                                scalar2=None, op0=ALU.mult),
                     reads=[r_wcv, C.r_const], writes=[r_diag.setdefault((c, j), Res("diag"))])
    xt = [P.tile([128, D], F32) for _ in range(4)]
    r_xt = [Res("xt") for _ in range(4)]
    hT2 = [P.tile([128, 8, NB], BF16) for _ in range(2)]
    r_hT2 = [Res("hT") for _ in range(2)]
    NW = 4
    wsl = [P.tile([128, 8, 512], BF16) for _ in range(NW)]
    r_wsl = [Res("w") for _ in range(NW)]
    gluT = P.tile([128, 8, 30 + NB], BF16)
    r_glu = [Res("glu") for _ in range(8)]
    hcv = P.tile([128, 8, NB], F32)
    r_hcv = [Res("hcv") for _ in range(8)]
    ybo = P.tile([128, 8, NB], BF16)
    r_ybo = Res("ybo")
    sgt = [P.tile([128, NB], F32) for _ in range(2)]
    r_sgt = [Res("sgt") for _ in range(2)]
    NDT = 4
    sq = [P.tile([128, NB], BF16) for _ in range(2)]
    r_sq = [Res("sq") for _ in range(2)]
    hcb = [P.tile([128, NB], BF16) for _ in range(2)]
    r_hcb = [Res("hcb") for _ in range(2)]
    dacc = [P.tile([128, NB], F32) for _ in range(2)]
    r_dacc = [Res("dacc") for _ in range(2)]
    onesb = P.tile([128, 128], BF16)
    P.op("dve", lambda e: e.memset(onesb, 1.0), writes=[r_colp])
    mean = P.tile([128, NB], F32)
    rstd = P.tile([128, NB], F32)
    msq = P.tile([128, NB], F32)
    r_st = Res("stats")
    tmp = [P.tile([128, NB], F32) for _ in range(2)]
    r_tmp = [Res("tmp") for _ in range(2)]
    scr = {"ss": P.tile([128, 8], F32), "junk": P.tile([128, D], BF16),
           "hb": [P.tile([128, D], BF16) for _ in range(2)]}
    r_scr = {"ss": Res("ss"), "junk": Res("junk"), "hb": [Res("hb0"), Res("hb1")]}
    P.op("dve", lambda e: e.memset(gluT[:, :, 0:30], 0.0), writes=r_glu)
    wcnt = 0
    ybT_v = ybT.rearrange("c p t -> p c t")
    load_x_tiles(P, xin, r_xin, xt, r_xt, 0, "e1x%d")
    rmsnorm_tiles(P, C, xt, r_xt, g_bc, r_g, hT2[0], r_hT2[0], (0, 1), scr, r_scr)
    for blk in range(nblk):
        t0 = blk * NB
        if hook is not None:
            hook(blk)
        hT, r_hT = hT2[blk % 2], r_hT2[blk % 2]
        if blk + 1 < nblk:
            load_x_tiles(P, xin, r_xin, xt, r_xt, blk + 1, "e1x%d")
        slots = {}

        def proj(c):
            nonlocal wcnt
            pn, cc = c // 4, c % 4
            if cc == 0:
                for nm, base in (("a", A0), ("g", G0)):
                    ws = wcnt % NW
                    wcnt += 1
                    WS.load(P, "e%s%d" % (nm, pn), wsl[ws], r_wsl[ws], "e1w%d" % ws)
                    slots[nm] = ws
            for (pb, ws) in ((2, slots["a"]), (3, slots["g"])):
                for k in range(8):
                    P.op("pe", lambda e, hT=hT, pb=pb, ws=ws, k=k, cc=cc: e.matmul(
                        C.banks[pb][:], lhsT=wsl[ws][:, k, cc * 128:(cc + 1) * 128], rhs=hT[:, k, :],
                        start=(k == 0), stop=(k == 7)), reads=[r_wsl[ws], r_hT], writes=[C.rb[pb]])
            si = c % 2
            P.op("act", lambda e, si=si: e.activation(out=sgt[si], in_=C.banks[3][:], func=AF.Sigmoid),
                 reads=[C.rb[3]], writes=[r_sgt[si]])
            P.op("dve", lambda e, si=si, c=c: e.tensor_tensor(out=gluT[:, c, 30:30 + NB], in0=sgt[si], in1=C.banks[2][:],
                                                             op=ALU.mult),
                 reads=[C.rb[2], r_sgt[si]], writes=[r_glu[c]])

        def conv(c):
            pb = 4 + (c % 2)
            si = c % 2
            dtaps = [2 * k for k in range(NDT)]
            ptaps = [j for j in range(31) if j not in dtaps]
            for n_, j in enumerate(dtaps):
                if n_ == 0:
                    P.op("dve", lambda e, si=si, c=c, j=j: e.tensor_scalar(out=dacc[si], in0=gluT[:, c, j:j + NB], scalar1=wcv[:, c, j:j + 1],
                                                                          scalar2=None, op0=ALU.mult),
                         reads=[r_glu[c], r_wcv], writes=[r_dacc[si]])
                else:
                    P.op("dve", lambda e, si=si, c=c, j=j: e.scalar_tensor_tensor(
                        out=dacc[si], in0=gluT[:, c, j:j + NB], scalar=wcv[:, c, j:j + 1], in1=dacc[si], op0=ALU.mult, op1=ALU.add),
                        reads=[r_glu[c], r_wcv, r_dacc[si]], writes=[r_dacc[si]])
            for n_, j in enumerate(ptaps):
                P.op("pe", lambda e, pb=pb, c=c, j=j, n_=n_: e.matmul(C.banks[pb][:], lhsT=diag[:, c, j, :],
                                                                     rhs=gluT[:, c, j:j + NB], start=(n_ == 0), stop=(n_ == len(ptaps) - 1)),
                     reads=[r_diag[(c, j)], r_glu[c]], writes=[C.rb[pb]])
            P.op("act", lambda e, pb=pb, c=c: e.activation(out=hcv[:, c, :], in_=C.banks[pb][:], func=AF.Identity,
                                                          bias=colp[:, 0, c:c + 1], scale=1.0),
                 reads=[C.rb[pb], r_colp], writes=[r_hcv[c]])
            if NDT > 0:
                P.op("dve", lambda e, si=si, c=c: e.tensor_tensor(out=hcv[:, c, :], in0=hcv[:, c, :], in1=dacc[si], op=ALU.add),
                     reads=[r_hcv[c], r_dacc[si]], writes=[r_hcv[c]])
            P.op("act", lambda e, si=si, c=c: e.activation(out=sq[si], in_=hcv[:, c, :], func=AF.Square),
                 reads=[r_hcv[c]], writes=[r_sq[si]])
            P.op("act", lambda e, si=si, c=c: e.copy(out=hcb[si], in_=hcv[:, c, :]), reads=[r_hcv[c]], writes=[r_hcb[si]])
            P.op("dve", lambda e, c=c: e.tensor_copy(out=gluT[:, c, 0:30], in_=gluT[:, c, NB:NB + 30]),
                 reads=[r_glu[c]], writes=[r_glu[c]])

        def stats(c):
            si = c % 2
            P.op("pe", lambda e, c=c, si=si: e.matmul(C.banks[6][:], lhsT=onesb, rhs=hcb[si], start=(c == 0), stop=(c == 7)),
                 reads=[r_hcb[si], r_colp], writes=[C.rb[6]])
            P.op("pe", lambda e, c=c, si=si: e.matmul(C.banks[7][:], lhsT=onesb, rhs=sq[si], start=(c == 0), stop=(c == 7)),
                 reads=[r_sq[si], r_colp], writes=[C.rb[7]])

        for step in range(10):
            if step < 8:
                proj(step)
            if 1 <= step < 9:
                conv(step - 1)
            if 2 <= step < 10:
                stats(step - 2)
        if blk + 1 < nblk:
            rmsnorm_tiles(P, C, xt, r_xt, g_bc, r_g, hT2[(blk + 1) % 2], r_hT2[(blk + 1) % 2], (0, 1), scr, r_scr)
        P.op("act", lambda e: e.mul(out=mean, in_=C.banks[6][:], mul=1.0 / D), reads=[C.rb[6]], writes=[r_st])
        P.op("dve", lambda e: e.tensor_tensor(out=msq, in0=mean, in1=mean, op=ALU.mult), reads=[r_st], writes=[r_st])
        P.op("dve", lambda e: e.scalar_tensor_tensor(out=rstd, in0=C.banks[7][:], scalar=1.0 / D, in1=msq,
                                                     op0=ALU.mult, op1=ALU.subtract), reads=[C.rb[7], r_st], writes=[r_st])
        P.op("act", lambda e: e.activation(out=rstd, in_=rstd, func=AF.Sqrt, bias=EPS, scale=1.0), reads=[r_st], writes=[r_st])
        P.op("dve", lambda e: e.reciprocal(out=rstd, in_=rstd), reads=[r_st], writes=[r_st])
        for c in range(8):
            ti = c % 2
            P.op("dve", lambda e, c=c, ti=ti: e.tensor_tensor(out=tmp[ti], in0=hcv[:, c, :], in1=mean, op=ALU.subtract),
                 reads=[r_hcv[c], r_st], writes=[r_tmp[ti]])
            P.op("dve", lambda e, ti=ti: e.tensor_tensor(out=tmp[ti], in0=tmp[ti], in1=rstd, op=ALU.mult),
                 reads=[r_st, r_tmp[ti]], writes=[r_tmp[ti]])
            P.op("act", lambda e, c=c, ti=ti: e.activation(out=ybo[:, c, :], in_=tmp[ti], func=AF.Silu,
                                                          scale=colp[:, 1, c:c + 1], bias=colp[:, 2, c:c + 1]),
                 reads=[r_tmp[ti], r_colp], writes=[r_ybo])
        P.op("sp", lambda e, t0=t0: e.dma_start(out=ybT_v[:, :, t0:t0 + NB], in_=ybo), reads=[r_ybo], writes=[r_yb[blk]],
             dma_key="e1o")


EVEN_NAMES = ["ev_norm", "ev_w_in", "ev_conv_w", "ev_conv_b", "ev_dt_bias", "ev_a_log", "ev_d", "ev_ssd_norm",
              "ev_cf_conv_w", "ev_cf_conv_b", "ev_cf_ln_g", "ev_cf_ln_b", "ev_w_out"]
EVEN_SHAPES = {"ev_norm": [1, D], "ev_w_in": [1, D, 5136], "ev_conv_w": [1, 4, 2048], "ev_conv_b": [1, 2048],
               "ev_dt_bias": [1, 16], "ev_a_log": [1, 16], "ev_d": [1, 16], "ev_ssd_norm": [1, D],
               "ev_cf_conv_w": [1, 31, D], "ev_cf_conv_b": [1, D], "ev_cf_ln_g": [1, D], "ev_cf_ln_b": [1, D],
               "ev_w_out": [1, 2048, D]}


def declare(nc, shapes):
    return {k: nc.dram_tensor(k, v, F32, kind="ExternalInput").ap() for k, v in shapes.items()}


def build_e1_only(nblk=NBLK):
    nc = bass.Bass("TRN2", target_bir_lowering=False)
    xin = nc.dram_tensor("x", [S, D], F32, kind="ExternalInput").ap()
    Wd = declare(nc, EVEN_SHAPES)
    W = {k: v[0] for k, v in Wd.items()}
    ybT = nc.dram_tensor("ybT", [8, 128, S], BF16, kind="ExternalOutput").ap()
    P = Prog(nc, ARENA)
    C = Ctx()
    setup_common(P, C)
    r_in = [Res("xin") for _ in range(32)]
    r_yb = [Res("yb") for _ in range(NBLK)]
    WS = WStore(nc)
    define_panels(WS, We=W)
    WS.emit_group(P, "A")
    phase_e1(P, C, WS, xin, r_in, W, ybT, r_yb, nblk=nblk)
    P.op("sp", None, reads=r_yb)
    P.emit()
    return nc, P


def bc_mid(ap2, n):
    return ap2.unsqueeze(2).broadcast_to([ap2.shape[0], ap2.shape[1], n])


def phase_e2(P, C, WS, xin, r_xin, W, ybT, r_yb, xout, r_xout, nblk=NBLK, dbg=None, hook=None):
    P.reset()
    XC0 = 1024
    DT0 = 3072
    g_bc = P.tile([128, D], F32)
    ng_bc = P.tile([128, D], F32)
    r_g = Res("g")
    load_bcast_vec(P, g_bc, W["ev_norm"], r_g, "e2g")
    load_bcast_vec(P, ng_bc, W["ev_ssd_norm"], r_g, "e2g")
    dtb_bc = P.tile([128, 16], F32)
    ahead_bc = P.tile([128, 16], F32)
    load_bcast_vec(P, dtb_bc, W["ev_dt_bias"], r_g, "e2g")
    load_bcast_vec(P, ahead_bc, W["ev_a_log"], r_g, "e2g")
    P.op("act", lambda e: e.activation(out=ahead_bc, in_=ahead_bc, func=AF.Exp), reads=[r_g], writes=[r_g])
    P.op("dve", lambda e: e.tensor_scalar(out=ahead_bc, in0=ahead_bc, scalar1=-1.0, scalar2=None, op0=ALU.mult),
         reads=[r_g], writes=[r_g])
    Dcol = P.tile([128, 8], F32)
    Dbc = P.tile([128, 16], F32)
    load_bcast_vec(P, Dbc, W["ev_d"], r_g, "e2g")
    for hh in range(2):
        P.op("dve", lambda e, hh=hh: e.tensor_copy(out=Dcol[hh * 64:(hh + 1) * 64, :],
                                                   in_=Dbc[hh * 64:(hh + 1) * 64, :].rearrange("p (c two) -> p two c", two=2)[:, hh, :]),
             reads=[r_g], writes=[r_g])
    diagD = P.tile([128, 8, 128], BF16)
    for c in range(8):
        P.op("dve", lambda e, c=c: e.tensor_scalar(out=diagD[:, c, :], in0=C.identf, scalar1=Dcol[:, c:c + 1], scalar2=None,
                                                   op0=ALU.mult), reads=[r_g, C.r_const], writes=[r_g])
    Tmask = P.tile([128, 128], F32)
    P.op("dve", lambda e: e.memset(Tmask, 1.0), writes=[r_g])
    P.op("pool", lambda e: e.affine_select(out=Tmask, in_=Tmask, pattern=[[1, 128]], compare_op=ALU.is_ge, fill=0.0,
                                           base=0, channel_multiplier=-1), reads=[r_g], writes=[r_g])
    Umask = P.tile([128, 128], F32)
    P.op("dve", lambda e: e.memset(Umask, 1.0), reads=[r_g], writes=[r_g])
    P.op("pool", lambda e: e.affine_select(out=Umask, in_=Umask, pattern=[[-1, 128]], compare_op=ALU.is_gt, fill=0.0,
                                           base=0, channel_multiplier=1), reads=[r_g], writes=[r_g])
    wnat = P.tile([5, 2048], F32)
    P.op("sp", lambda e: e.dma_start(out=wnat[0:4, :], in_=W["ev_conv_w"]), writes=[r_g], dma_key="e2g")
    P.op("sp", lambda e: e.dma_start(out=wnat[4:5, :], in_=W["ev_conv_b"].rearrange("(o n) -> o n", o=1)), writes=[r_g], dma_key="e2g")
    wc5 = P.tile([128, 16, 5], F32)
    wc4 = wc5[:, :, 0:4]
    cb4 = wc5[:, :, 4]
    for c in range(16):
        P.op("pe", lambda e, c=c: e.transpose(out=C.banks[0][:, c * 5:c * 5 + 5], in_=wnat[:, c * 128:(c + 1) * 128],
                                              identity=C.identf[0:5, 0:5]), reads=[r_g, C.r_const], writes=[C.rb[0]])
    P.op("act", lambda e: e.copy(out=wc5, in_=C.banks[0][:, 0:80].rearrange("p (c j) -> p c j", c=16)),
         reads=[C.rb[0]], writes=[r_g])
    wdt = P.tile([128, 8, 16], BF16)
    P.op("pool", lambda e: e.dma_start(out=wdt, in_=W["ev_w_in"][:, DT0:DT0 + 16].rearrange("(k p) n -> p k n", p=128)),
         writes=[r_g], dma_key="e2g")
    halo = P.tile([128, 16, 3], F32)
    r_halo = [Res("halo") for _ in range(16)]
    P.op("dve", lambda e: e.memset(halo, 0.0), writes=r_halo)
    prev = P.tile([128, D], F32)
    prev_bf = P.tile([128, D], BF16)
    r_prev = Res("prev")
    r_prevbf = Res("prevbf")
    P.op("dve", lambda e: e.memset(prev, 0.0), writes=[r_prev])
    P.op("dve", lambda e: e.memset(prev_bf, 0.0), writes=[r_prevbf])

    xt = [P.tile([128, D], F32) for _ in range(4)]
    r_xt = [Res("xt") for _ in range(4)]
    hT2 = [P.tile([128, 8, NB], BF16) for _ in range(2)]
    r_hT2 = [Res("hT") for _ in range(2)]
    NXR = 4
    xres = [P.tile([128, 512], F32) for _ in range(NXR)]
    r_xres = [Res("xres") for _ in range(NXR)]
    xrc = 0
    NW = 4
    wsl = [P.tile([128, 8, 512], BF16) for _ in range(NW)]
    r_wsl = [Res("w") for _ in range(NW)]
    sz = [P.tile([128, D], BF16) for _ in range(4)]
    r_sz = [Res("sz") for _ in range(4)]
    dtt = P.tile([128, 4, 16], F32)
    at = P.tile([128, 4, 16], F32)
    sp_t = [P.tile([128, 4, 16], F32) for _ in range(2)]
    r_dt = Res("dt")
    xsT = P.tile([128, 8, NB], BF16)
    BT = P.tile([128, 4, NB], BF16)
    CT = P.tile([128, 4, NB], BF16)
    r_xbc = [Res("xbc%d" % c) for c in range(16)]
    mixT2 = [P.tile([128, 16, NB], BF16) for _ in range(2)]
    r_mix_a2 = [[Res("mixa%d" % q) for q in range(4)] for _ in range(2)]
    r_mix_b2 = [Res("mixb") for _ in range(2)]
    pending_wout = None
    stg = [P.tile([128, NB + 3], F32) for _ in range(2)]
    r_stg = [Res("stg") for _ in range(2)]
    acc = [P.tile([128, NB], F32) for _ in range(2)]
    r_acc = [Res("acc") for _ in range(2)]
    scr = {"ss": P.tile([128, 8], F32), "junk": P.tile([128, D], BF16),
           "hb": [P.tile([128, D], BF16) for _ in range(2)]}
    r_scr = {"ss": Res("ss"), "junk": Res("junk"), "hb": [Res("hb0"), Res("hb1")]}
    xdt = P.tile([128, D], BF16)
    xdtd = P.tile([128, D], BF16)
    r_xdt, r_xdtd = Res("xdt"), Res("xdtd")
    B_tm = P.tile([128, 512], BF16)
    r_Btm = Res("Btm")
    rhsA = P.tile([128, 16, 128], F32)
    r_rhsA = Res("rhsA")
    expd = P.tile([128, 16, 128], BF16)
    r_expd = Res("expd")
    mCB = P.tile([128, 4, 128], F32)
    r_mCB = Res("mCB")
    scores = P.tile([128, 16, 128], BF16)
    r_scores = Res("scores")
    sm = P.tile([128, 6, 16], F32)
    r_sm = Res("sm")
    ytmp = P.tile([128, D], F32)
    r_ytmp = Res("ytmp")
    yv = P.tile([128, D], F32)
    r_y = Res("y")
    ssg = P.tile([128, 4], F32)
    r_ssg = Res("ssg")
    ya_tm = P.tile([128, D], BF16)
    r_yatm = Res("yatm")
    wcnt = 0

    def next_slot():
        nonlocal wcnt
        ws = wcnt % NW
        wcnt += 1
        return ws

    Win = W["ev_w_in"]
    ybT_v = ybT.rearrange("c p t -> p c t")
    pcnt = 0
    load_x_tiles(P, xin, r_xin, xt, r_xt, 0, "e2x%d")
    rmsnorm_tiles(P, C, xt, r_xt, g_bc, r_g, hT2[0], r_hT2[0], (0, 1), scr, r_scr)
    for blk in range(nblk):
        t0 = blk * NB
        if hook is not None:
            hook(blk)
        hT, r_hT = hT2[blk % 2], r_hT2[blk % 2]
        mixT, r_mix_a, r_mix_b = mixT2[blk % 2], r_mix_a2[blk % 2], r_mix_b2[blk % 2]
        if blk + 1 < nblk:
            load_x_tiles(P, xin, r_xin, xt, r_xt, blk + 1, "e2x%d")
        P.op("sp", lambda e, t0=t0, mixT=mixT: e.dma_start(out=mixT[:, 8:16, :], in_=ybT_v[:, :, t0:t0 + NB]),
             reads=[r_yb[blk]], writes=[r_mix_b], dma_key="e2yb")
        for pn in range(2):
            ws = next_slot()
            WS.load(P, "ez%d" % pn, wsl[ws], r_wsl[ws], "e2w%d" % ws)
            for i in range(4):
                pb = 2 + (pcnt % 2)
                pcnt += 1
                for k in range(8):
                    P.op("pe", lambda e, hT=hT, pb=pb, ws=ws, k=k, i=i: e.matmul(
                        C.banks[pb][:], lhsT=hT[:, k, i * 128:(i + 1) * 128], rhs=wsl[ws][:, k, :],
                        start=(k == 0), stop=(k == 7)), reads=[r_wsl[ws], r_hT], writes=[C.rb[pb]])
                P.op("act", lambda e, pb=pb, i=i, pn=pn: e.activation(out=sz[i][:, pn * 512:(pn + 1) * 512],
                                                                       in_=C.banks[pb][:], func=AF.Silu),
                     reads=[C.rb[pb]], writes=[r_sz[i]])
        pb = 2 + (pcnt % 2)
        pcnt += 1
        for i in range(4):
            for k in range(8):
                P.op("pe", lambda e, hT=hT, pb=pb, k=k, i=i: e.matmul(
                    C.banks[pb][:, i * 16:(i + 1) * 16], lhsT=hT[:, k, i * 128:(i + 1) * 128], rhs=wdt[:, k, :],
                    start=(k == 0), stop=(k == 7)), reads=[r_g, r_hT], writes=[C.rb[pb]])
        dtr, sp1 = sp_t
        P.op("dve", lambda e, pb=pb: e.tensor_tensor(out=dtr, in0=C.banks[pb][:, 0:64].rearrange("p (i h) -> p i h", i=4),
                                                     in1=dtb_bc.unsqueeze(1).broadcast_to([128, 4, 16]), op=ALU.add),
             reads=[C.rb[pb], r_g], writes=[r_dt])
        P.op("dve", lambda e: e.tensor_scalar(out=sp1, in0=dtr, scalar1=-1.0, scalar2=None, op0=ALU.mult), reads=[r_dt], writes=[r_dt])
        P.op("dve", lambda e: e.tensor_tensor(out=sp1, in0=sp1, in1=dtr, op=ALU.min), reads=[r_dt], writes=[r_dt])
        P.op("act", lambda e: e.activation(out=sp1, in_=sp1, func=AF.Exp), reads=[r_dt], writes=[r_dt])
        P.op("act", lambda e: e.activation(out=sp1, in_=sp1, func=AF.Ln, bias=1.0, scale=1.0), reads=[r_dt], writes=[r_dt])
        P.op("dve", lambda e: e.tensor_scalar(out=dtr, in0=dtr, scalar1=0.0, scalar2=None, op0=ALU.max), reads=[r_dt], writes=[r_dt])
        P.op("dve", lambda e: e.tensor_tensor(out=dtt, in0=dtr, in1=sp1, op=ALU.add), reads=[r_dt], writes=[r_dt])
        P.op("dve", lambda e: e.tensor_tensor(out=at, in0=dtt, in1=ahead_bc.unsqueeze(1).broadcast_to([128, 4, 16]), op=ALU.mult),
             reads=[r_dt, r_g], writes=[r_dt])
        for c in range(16):
            if c % 4 == 0:
                ws = next_slot()
                WS.load(P, "ex%d" % (c // 4), wsl[ws], r_wsl[ws], "e2w%d" % ws)
            cc = c % 4
            pb = 2 + (pcnt % 2)
            pcnt += 1
            si = c % 2
            for k in range(8):
                P.op("pe", lambda e, hT=hT, pb=pb, ws=ws, k=k, cc=cc: e.matmul(
                    C.banks[pb][:], lhsT=wsl[ws][:, k, cc * 128:(cc + 1) * 128], rhs=hT[:, k, :],
                    start=(k == 0), stop=(k == 7)), reads=[r_wsl[ws], r_hT], writes=[C.rb[pb]])
            P.op("act", lambda e, pb=pb, si=si: e.copy(out=stg[si][:, 3:3 + NB], in_=C.banks[pb][:]),
                 reads=[C.rb[pb]], writes=[r_stg[si]])
            P.op("dve", lambda e, si=si, c=c: e.tensor_copy(out=stg[si][:, 0:3], in_=halo[:, c, :]),
                 reads=[r_halo[c], r_stg[si]], writes=[r_stg[si]])
            P.op("dve", lambda e, si=si, c=c: e.tensor_scalar(out=acc[si], in0=stg[si][:, 0:NB], scalar1=wc4[:, c, 0:1],
                                                             scalar2=cb4[:, c:c + 1], op0=ALU.mult, op1=ALU.add),
                 reads=[r_stg[si], r_g], writes=[r_acc[si]])
            for j in range(1, 4):
                P.op("dve", lambda e, si=si, c=c, j=j: e.scalar_tensor_tensor(
                    out=acc[si], in0=stg[si][:, j:j + NB], scalar=wc4[:, c, j:j + 1], in1=acc[si], op0=ALU.mult, op1=ALU.add),
                    reads=[r_stg[si], r_g, r_acc[si]], writes=[r_acc[si]])
            P.op("dve", lambda e, si=si, c=c: e.tensor_copy(out=halo[:, c, :], in_=stg[si][:, NB:NB + 3]),
                 reads=[r_stg[si]], writes=[r_halo[c]])
            dst = xsT[:, c, :] if c < 8 else (BT[:, c - 8, :] if c < 12 else CT[:, c - 12, :])
            P.op("act", lambda e, si=si, dst=dst: e.activation(out=dst, in_=acc[si], func=AF.Silu),
                 reads=[r_acc[si]], writes=[r_xbc[c]])
        pend = []
        if pending_wout is not None:
            pending_wout["start"]()
        for q in range(4):
            tq = q * 128
            a_q = at[:, q, :]
            dt_q = dtt[:, q, :]
            tb0 = bank_bf(C, 0)
            tb1 = bank_bf(C, 1)
            for c in range(8):
                P.op("pe", lambda e, c=c, tq=tq, tb0=tb0: e.transpose(out=tb0[:, c * 128:(c + 1) * 128],
                                                                      in_=xsT[:, c, tq:tq + 128], identity=C.ident),
                     reads=[r_xbc[c], C.r_const], writes=[C.rb[0]])
            for g in range(4):
                P.op("pe", lambda e, g=g, tq=tq, tb1=tb1: e.transpose(out=tb1[:, g * 128:(g + 1) * 128],
                                                                      in_=BT[:, g, tq:tq + 128], identity=C.ident),
                     reads=[r_xbc[8 + g], C.r_const], writes=[C.rb[1]])
            P.op("dve", lambda e, tb0=tb0, dt_q=dt_q: e.tensor_tensor(
                out=xdt.rearrange("p (h d) -> p h d", h=16), in0=tb0.rearrange("p (h d) -> p h d", h=16),
                in1=bc_mid(dt_q, 64), op=ALU.mult), reads=[C.rb[0], r_dt], writes=[r_xdt])
            P.op("act", lambda e, tb1=tb1: e.copy(out=B_tm, in_=tb1[:, 0:512]), reads=[C.rb[1]], writes=[r_Btm])
            P.op("pe", lambda e, a_q=a_q: e.matmul(C.banks[2][:, 0:16], lhsT=Tmask, rhs=a_q, start=True, stop=True),
                 reads=[r_dt, r_g], writes=[C.rb[2]])
            P.op("dve", lambda e, a_q=a_q: e.tensor_tensor(out=rhsA, in0=Tmask.unsqueeze(1).broadcast_to([128, 16, 128]),
                                                           in1=bc_mid(a_q, 128), op=ALU.mult),
                 reads=[r_dt, r_g], writes=[r_rhsA])
            for hb_ in range(4):
                P.op("pe", lambda e, hb_=hb_: e.matmul(C.banks[4 + hb_][:], lhsT=Umask,
                                                       rhs=rhsA[:, hb_ * 4:(hb_ + 1) * 4, :].rearrange("p h l -> p (h l)"),
                                                       start=True, stop=True),
                     reads=[r_rhsA, r_g], writes=[C.rb[4 + hb_]])
            P.op("pe", lambda e, a_q=a_q: e.matmul(C.banks[2][:, 16:32], lhsT=C.onesf, rhs=a_q, start=True, stop=True),
                 reads=[r_dt, C.r_const], writes=[C.rb[2]])
            if pending_wout is not None:
                pending_wout["piece"](q // 2, 2 * (q % 2), 1)
            for g in range(4):
                P.op("pe", lambda e, g=g, tq=tq: e.matmul(C.banks[3][:, g * 128:(g + 1) * 128], lhsT=BT[:, g, tq:tq + 128],
                                                          rhs=CT[:, g, tq:tq + 128], start=True, stop=True),
                     reads=[r_xbc[8 + g], r_xbc[12 + g]], writes=[C.rb[3]])
            P.op("act", lambda e: e.copy(out=sm[:, 0, :], in_=C.banks[2][:, 0:16]), reads=[C.rb[2]], writes=[r_sm])
            P.op("act", lambda e: e.activation(out=sm[:, 1, :], in_=sm[:, 0, :], func=AF.Exp), reads=[r_sm], writes=[r_sm])
            P.op("act", lambda e: e.activation(out=sm[:, 3, :], in_=C.banks[2][:, 16:32], func=AF.Exp), reads=[C.rb[2]], writes=[r_sm])
            for hb_ in range(4):
                P.op("act", lambda e, hb_=hb_: e.activation(
                    out=sm[:, 2, hb_ * 4:(hb_ + 1) * 4],
                    in_=C.banks[4 + hb_][:].rearrange("p (h l) -> p h l", h=4)[:, :, 127], func=AF.Exp),
                    reads=[C.rb[4 + hb_]], writes=[r_sm])
                P.op("act", lambda e, hb_=hb_: e.activation(out=expd[:, hb_ * 4:(hb_ + 1) * 4, :].rearrange("p h l -> p (h l)"),
                                                            in_=C.banks[4 + hb_][:], func=AF.Exp),
                     reads=[C.rb[4 + hb_]], writes=[r_expd])
            P.op("dve", lambda e: e.tensor_tensor(out=mCB, in0=C.banks[3][:].rearrange("p (g l) -> p g l", g=4),
                                                  in1=Tmask.unsqueeze(1).broadcast_to([128, 4, 128]), op=ALU.mult),
                 reads=[C.rb[3], r_g], writes=[r_mCB])
            P.op("dve", lambda e: e.tensor_tensor(
                out=scores.rearrange("p (g r) l -> p g r l", g=4), in0=expd.rearrange("p (g r) l -> p g r l", g=4),
                in1=mCB.unsqueeze(2).broadcast_to([128, 4, 4, 128]), op=ALU.mult),
                reads=[r_expd, r_mCB], writes=[r_scores])
            P.op("dve", lambda e: e.tensor_tensor(out=xdtd.rearrange("p (h d) -> p h d", h=16),
                                                  in0=xdt.rearrange("p (h d) -> p h d", h=16), in1=bc_mid(sm[:, 2, :], 64),
                                                  op=ALU.mult), reads=[r_xdt, r_sm], writes=[r_xdtd])
            for h in range(16):
                pb = 4 + h // 8
                col = (h % 8) * 64
                c = h // 2
                first_of_chunk = (h % 2 == 0)
                if first_of_chunk:
                    P.op("pe", lambda e, pb=pb, c=c, tq=tq: e.matmul(
                        C.banks[pb][:, (c % 4) * 128:(c % 4 + 1) * 128], lhsT=xsT[:, c, tq:tq + 128], rhs=diagD[:, c, :],
                        start=True, stop=False), reads=[r_xbc[c], r_g], writes=[C.rb[pb]])
                P.op("pe", lambda e, pb=pb, col=col, h=h: e.matmul(
                    C.banks[pb][:, col:col + 64], lhsT=scores[:, h, :], rhs=xdt[:, h * 64:(h + 1) * 64],
                    start=False, stop=(h % 2 == 1)), reads=[r_scores, r_xdt], writes=[C.rb[pb]])
            for g in range(4):
                pb = 6 + g // 2
                P.op("pe", lambda e, pb=pb, g=g, tq=tq: e.matmul(
                    C.banks[pb][:, (g % 2) * 256:(g % 2 + 1) * 256], lhsT=CT[:, g, tq:tq + 128],
                    rhs=prev_bf[:, g * 256:(g + 1) * 256], start=True, stop=True),
                    reads=[r_xbc[12 + g], r_prevbf], writes=[C.rb[pb]])
            for hf in range(2):
                sl_ = slice(hf * 512, (hf + 1) * 512)
                P.op("dve", lambda e, hf=hf, sl_=sl_: e.tensor_tensor(
                    out=ytmp[:, sl_].rearrange("p (h d) -> p h d", h=8),
                    in0=C.banks[6 + hf][:].rearrange("p (h d) -> p h d", h=8),
                    in1=bc_mid(sm[:, 1, hf * 8:(hf + 1) * 8], 64), op=ALU.mult),
                    reads=[C.rb[6 + hf], r_sm], writes=[r_ytmp])
                P.op("dve", lambda e, hf=hf, sl_=sl_: e.tensor_tensor(out=yv[:, sl_], in0=ytmp[:, sl_], in1=C.banks[4 + hf][:],
                                                                      op=ALU.add),
                     reads=[C.rb[4 + hf], r_ytmp], writes=[r_y])
            for g in range(4):
                pb = 2 + g // 2
                P.op("pe", lambda e, pb=pb, g=g: e.matmul(
                    C.banks[pb][:, (g % 2) * 256:(g % 2 + 1) * 256], lhsT=B_tm[:, g * 128:(g + 1) * 128],
                    rhs=xdtd[:, g * 256:(g + 1) * 256], start=True, stop=True),
                    reads=[r_Btm, r_xdtd], writes=[C.rb[pb]])
            P.op("dve", lambda e: e.tensor_tensor(out=prev.rearrange("p (h d) -> p h d", h=16),
                                                  in0=prev.rearrange("p (h d) -> p h d", h=16),
                                                  in1=bc_mid(sm[:, 3, :], 64), op=ALU.mult),
                 reads=[r_sm, r_prev], writes=[r_prev])
            for hf in range(2):
                sl_ = slice(hf * 512, (hf + 1) * 512)
                P.op("dve", lambda e, hf=hf, sl_=sl_: e.tensor_tensor(out=prev[:, sl_], in0=prev[:, sl_],
                                                                      in1=C.banks[2 + hf][:], op=ALU.add),
                     reads=[C.rb[2 + hf], r_prev], writes=[r_prev])
            P.op("act", lambda e: e.copy(out=prev_bf, in_=prev), reads=[r_prev], writes=[r_prevbf])
            if pending_wout is not None:
                pending_wout["piece"](q // 2, 2 * (q % 2) + 1, 1)
            while pend:
                pend.pop(0)()
            P.op("dve", lambda e, q=q: e.tensor_tensor(out=yv, in0=yv, in1=sz[q], op=ALU.mult),
                 reads=[r_y, r_sz[q]], writes=[r_y])
            for g in range(4):
                P.op("act", lambda e, g=g: e.activation(out=ytmp[:, g * 256:(g + 1) * 256], in_=yv[:, g * 256:(g + 1) * 256],
                                                        func=AF.Square, accum_out=ssg[:, g:g + 1]),
                     reads=[r_y], writes=[r_ytmp, r_ssg])
            P.op("act", lambda e: e.activation(out=ssg, in_=ssg, func=AF.Ln, scale=1.0 / 256, bias=EPS),
                 reads=[r_ssg], writes=[r_ssg])
            P.op("act", lambda e: e.activation(out=ssg, in_=ssg, func=AF.Exp, scale=-0.5), reads=[r_ssg], writes=[r_ssg])
            for g in range(4):
                gs = slice(g * 256, (g + 1) * 256)
                P.op("dve", lambda e, g=g, gs=gs: e.scalar_tensor_tensor(out=ya_tm[:, gs], in0=yv[:, gs], scalar=ssg[:, g:g + 1],
                                                                         in1=ng_bc[:, gs], op0=ALU.mult, op1=ALU.mult),
                     reads=[r_y, r_ssg, r_g], writes=[r_yatm])
            def emit_ya(q=q, tq=tq, tb0=tb0, mixT=mixT, r_mix_a=r_mix_a):
                for c in range(8):
                    P.op("pe", lambda e, c=c, tb0=tb0: e.transpose(out=tb0[:, c * 128:(c + 1) * 128],
                                                                   in_=ya_tm[:, c * 128:(c + 1) * 128], identity=C.ident),
                         reads=[r_yatm, C.r_const], writes=[C.rb[0]])
                P.op("act", lambda e, tb0=tb0, tq=tq, mixT=mixT: e.copy(out=mixT[:, 0:8, tq:tq + 128],
                                                             in_=tb0.rearrange("p (k n) -> p k n", k=8)),
                     reads=[C.rb[0]], writes=[r_mix_a[q]])
            pend.append(emit_ya)
        while pend:
            pend.pop(0)()
        if blk + 1 < nblk:
            rmsnorm_tiles(P, C, xt, r_xt, g_bc, r_g, hT2[(blk + 1) % 2], r_hT2[(blk + 1) % 2], (0, 1), scr, r_scr)
        def make_wout(blk=blk, t0=t0, mixT=mixT, r_mix_a=r_mix_a, r_mix_b=r_mix_b):
            xr_of = {}
            slots = {}

            def issue_xres(half, i):
                nonlocal xrc
                k_ = xrc % NXR
                xrc += 1
                xr_of[(half, i)] = k_
                P.op("sp", lambda e, k_=k_, i=i, half=half: e.dma_start(
                    out=xres[k_], in_=xin[t0 + i * 128:t0 + (i + 1) * 128, half * 512:(half + 1) * 512]),
                    reads=[r_xin[blk * 4 + i]], writes=[r_xres[k_]], dma_key="e2r" + str(k_))

            def start():
                nonlocal wcnt
                for hi in [(half, i) for half in range(2) for i in range(4)][:NXR]:
                    issue_xres(*hi)
                for half in range(2):
                    while wcnt % NW not in (0, 2):
                        wcnt += 1
                    ws = next_slot()
                    ws2 = next_slot()
                    WS.load(P, "eo%d0" % half, wsl[ws], r_wsl[ws], "e2w%d" % ws)
                    WS.load(P, "eo%d1" % half, wsl[ws2], r_wsl[ws2], "e2w%d" % ws2)
                    slots[half] = (ws, ws2)

            def piece(half, i, pb):
                ws, ws2 = slots[half]
                for c in range(16):
                    wsx = ws if c < 8 else ws2
                    P.op("pe", lambda e, pb=pb, c=c, i=i, wsx=wsx: e.matmul(
                        C.banks[pb][:], lhsT=mixT[:, c, i * 128:(i + 1) * 128], rhs=wsl[wsx][:, c % 8, :],
                        start=(c == 0), stop=(c == 15)),
                        reads=[r_mix_a[i], r_mix_b, r_wsl[wsx]], writes=[C.rb[pb]])
                if (half, i) not in xr_of:
                    issue_xres(half, i)
                k_ = xr_of[(half, i)]
                xr_ = xres[k_]
                r_xr = r_xres[k_]
                P.op("dve", lambda e, pb=pb, xr_=xr_: e.tensor_tensor(out=xr_, in0=xr_, in1=C.banks[pb][:], op=ALU.add),
                     reads=[C.rb[pb], r_xr], writes=[r_xr])
                P.op("sp", lambda e, i=i, half=half, xr_=xr_: e.dma_start(
                    out=xout[t0 + i * 128:t0 + (i + 1) * 128, half * 512:(half + 1) * 512], in_=xr_),
                    reads=[r_xr], writes=[r_xout[blk * 4 + i]], dma_key="e2o")

            return {"start": start, "piece": piece}

        pending_wout = make_wout()
    if pending_wout is not None:
        pending_wout["start"]()
        n_ = 0
        for half in range(2):
            for i in range(4):
                pending_wout["piece"](half, i, 2 + (n_ % 2))
                n_ += 1


def build_even_only(nblk=NBLK):
    nc = bass.Bass("TRN2", target_bir_lowering=False)
    xin = nc.dram_tensor("x", [S, D], F32, kind="ExternalInput").ap()
    Wd = declare(nc, EVEN_SHAPES)
    W = {k: v[0] for k, v in Wd.items()}
    ybT = nc.dram_tensor("ybT", [8, 128, S], BF16, kind="Internal").ap()
    out = nc.dram_tensor("out", [S, D], F32, kind="ExternalOutput").ap()
    P = Prog(nc, ARENA)
    C = Ctx()
    setup_common(P, C)
    r_in = [Res("xin") for _ in range(32)]
    r_yb = [Res("yb") for _ in range(NBLK)]
    r_out = [Res("xout") for _ in range(32)]
    WS = WStore(nc)
    define_panels(WS, We=W)
    WS.emit_group(P, "A")
    WS.emit_group(P, "B")
    phase_e1(P, C, WS, xin, r_in, W, ybT, r_yb, nblk=nblk)
    P.barrier()
    phase_e2(P, C, WS, xin, r_in, W, ybT, r_yb, out, r_out, nblk=nblk)
    P.op("sp", None, reads=r_out[:nblk * 4])
    P.emit()
    return nc, P


TWO_PI = 2.0 * math.pi


def sincos(P, X, out_sin, out_cos, t1, t2, rg):
    I32 = mybir.dt.int32
    t1i = t1.bitcast(I32)
    C1 = 6.28125
    C2 = TWO_PI - C1

    def ew(eng, fn):
        P.op(eng, fn, reads=rg, writes=rg)
    ew("dve", lambda e: e.tensor_scalar(out=t1i, in0=X, scalar1=1.0 / TWO_PI, scalar2=None, op0=ALU.mult))
    ew("dve", lambda e: e.tensor_copy(out=t2, in_=t1i))
    ew("dve", lambda e: e.scalar_tensor_tensor(out=t1, in0=t2, scalar=-C1, in1=X, op0=ALU.mult, op1=ALU.add))
    ew("dve", lambda e: e.scalar_tensor_tensor(out=t1, in0=t2, scalar=-C2, in1=t1, op0=ALU.mult, op1=ALU.add))
    ew("act", lambda e: e.activation(out=out_sin, in_=t1, func=AF.Sin))
    ew("dve", lambda e: e.tensor_scalar(out=t2, in0=t1, scalar1=-1.0, scalar2=None, op0=ALU.mult))
    ew("dve", lambda e: e.tensor_tensor(out=t2, in0=t2, in1=t1, op=ALU.max))
    ew("act", lambda e: e.activation(out=out_cos, in_=t2, func=AF.Sin, scale=-1.0, bias=math.pi / 2))


def phase_o1(P, C, WS, xin, r_xin, W, iota, ycT, r_yc, nblk=NBLK, dbg=None, DEC=4, hook=None):
    P.reset()
    KD = NB // DEC
    PB = 512 // KD
    NBT = 16 // PB
    r_g = Res("g")
    key = "o1g"
    rg = [r_g]

    r_ld = []

    def ldres():
        r_ld.append(Res("ld"))
        return r_ld[-1]

    def ew(eng, fn):
        P.op(eng, fn, reads=rg + r_ld, writes=rg)

    g_bc = P.tile([128, D], F32)
    load_bcast_vec(P, g_bc, W["od_norm"], ldres(), key)
    iot = P.tile([128, KD], F32)
    ctD = P.tile([128, 16, KD], F32)
    stD = P.tile([128, 16, KD], F32)
    sel = P.tile([128, 2], F32)
    rmask = P.tile([128, 4], F32)
    BpadD = [P.tile([128, 2, 16, 128], BF16) for _ in range(DEC)]
    Cpad = P.tile([128, 2, 16, 128], BF16)
    CLpad = [P.tile([128, 2, 16, 128], BF16) for _ in range(DEC - 1)]
    Kmat = P.tile([128, 4, max(DEC - 1, 1), 128], BF16)
    Sbuf = P.tile([128, 2, 16, KD + 1], BF16)
    dg8 = P.tile([128, 8], F32)
    dcol = dg8[:, 0:4]
    gbcol = dg8[:, 4:8]
    gw = P.tile([128, 4, 512], BF16)
    rinit = P.tile([128, 2, 16], F32)
    pp = P.tile([128, 16, 16], F32)
    pq = P.tile([128, 24, 16], F32)
    setup_mark = P.off
    targ = P.tile([128, 16 * KD], F32)
    ttmp = P.tile([128, 16 * KD], F32)
    ttmp2 = P.tile([128, 16 * KD], F32)
    pt2 = P.tile([128, 2, 16], F32)
    braw = P.tile([128, 2, 16, 16], F32)
    bbD = [P.tile([128, 2, 16, 16], F32) for _ in range(DEC)]
    btmp = P.tile([128, 16, 16], F32)
    mpad = P.tile([128, 2 * 16 * 32], F32)
    craw = P.tile([128, 2, 16, 16], F32)
    clraw = P.tile([128, 2, 16, 16], F32)
    BpadU = P.tile([128, 2, 16, 128], BF16)
    snat = P.tile([16, 1024], F32)

    P.op("sp", lambda e: e.dma_start(out=iot, in_=iota[:, 0:KD]), writes=[ldres()], dma_key=key)
    lnat = snat[0:16, 0:512].rearrange("p (a n) -> p a n", a=4)
    ld16 = snat[0:16, 512:514]
    for i, nm in ((0, "od_lam_re"), (1, "od_lam_im")):
        lv = W[nm].rearrange("(pr gl) p -> gl pr p", gl=2)
        for gl in range(2):
            P.op("sp", lambda e, i=i, gl=gl, lv=lv: e.dma_start(out=lnat[:, i, gl * 64:(gl + 1) * 64], in_=lv[gl]),
                 writes=[ldres()], dma_key=key)
    P.op("sp", lambda e: e.dma_start(out=ld16, in_=W["od_log_dt"].rearrange("(pr gl) -> pr gl", gl=2)), writes=[ldres()], dma_key=key)
    ew("dve", lambda e: e.tensor_copy(out=lnat[:, 2, :].rearrange("p (gl q) -> p gl q", gl=2), in_=bc_mid(ld16, 64)))
    for i in range(3):
        P.op("pe", lambda e, i=i: e.transpose(out=C.banks[0][:, i * 16:(i + 1) * 16], in_=lnat[:, i, :], identity=C.identf[0:16, 0:16]),
             reads=rg + r_ld + [C.r_const], writes=[C.rb[0]])
    P.op("act", lambda e: e.copy(out=pp[:, 0:3, :], in_=C.banks[0][:, 0:48].rearrange("p (a b) -> p a b", a=3)),
         reads=[C.rb[0]], writes=rg)
    def TT(out, a_, b_, op):
        ew("dve", lambda e: e.tensor_tensor(out=out, in0=a_, in1=b_, op=op))

    ew("act", lambda e: e.activation(out=pp[:, 2, :], in_=pp[:, 2, :], func=AF.Exp))
    TT(pp[:, 3, :], pp[:, 0, :], pp[:, 2, :], ALU.mult)
    TT(pp[:, 4, :], pp[:, 1, :], pp[:, 2, :], ALU.mult)
    ew("act", lambda e: e.activation(out=pp[:, 3, :], in_=pp[:, 3, :], func=AF.Exp))
    sincos(P, pp[:, 4, :], pp[:, 5, :], pp[:, 6, :], pt2[:, 0, :], pt2[:, 1, :], rg)
    ew("dve", lambda e: e.tensor_copy(out=pq[:, 18, :], in_=pp[:, 3, :]))
    for d in range(1, DEC + 1):
        if d > 1:
            TT(pq[:, 18, :], pq[:, 18, :], pp[:, 3, :], ALU.mult)
            ew("dve", lambda e, d=d: e.tensor_scalar(out=pq[:, 19, :], in0=pp[:, 4, :], scalar1=float(d), scalar2=None, op0=ALU.mult))
            sincos(P, pq[:, 19, :], pq[:, 20, :], pq[:, 21, :], pt2[:, 0, :], pt2[:, 1, :], rg)
            TT(pq[:, d - 1, :], pq[:, 18, :], pq[:, 21, :], ALU.mult)
            TT(pq[:, 8 + d - 1, :], pq[:, 18, :], pq[:, 20, :], ALU.mult)
        else:
            TT(pq[:, 0, :], pp[:, 3, :], pp[:, 6, :], ALU.mult)
            TT(pq[:, 8, :], pp[:, 3, :], pp[:, 5, :], ALU.mult)
    ew("dve", lambda e: e.tensor_copy(out=pq[:, 16, :], in_=pq[:, 18, :]))
    ew("dve", lambda e: e.tensor_scalar(out=pq[:, 17, :], in0=pp[:, 4, :], scalar1=float(DEC), scalar2=None, op0=ALU.mult))
    ew("dve", lambda e: e.tensor_scalar(out=pp[:, 7, :], in0=pq[:, 0, :], scalar1=-1.0, scalar2=None, op0=ALU.add))
    ew("dve", lambda e: e.tensor_copy(out=pp[:, 8, :], in_=pq[:, 8, :]))
    TT(pp[:, 9, :], pp[:, 0, :], pp[:, 0, :], ALU.mult)
    TT(pp[:, 14, :], pp[:, 1, :], pp[:, 1, :], ALU.mult)
    TT(pp[:, 9, :], pp[:, 9, :], pp[:, 14, :], ALU.add)
    ew("dve", lambda e: e.reciprocal(out=pp[:, 9, :], in_=pp[:, 9, :]))
    TT(pp[:, 10, :], pp[:, 7, :], pp[:, 0, :], ALU.mult)
    TT(pp[:, 14, :], pp[:, 8, :], pp[:, 1, :], ALU.mult)
    TT(pp[:, 10, :], pp[:, 10, :], pp[:, 14, :], ALU.add)
    TT(pp[:, 10, :], pp[:, 10, :], pp[:, 9, :], ALU.mult)
    TT(pp[:, 11, :], pp[:, 8, :], pp[:, 0, :], ALU.mult)
    TT(pp[:, 14, :], pp[:, 7, :], pp[:, 1, :], ALU.mult)
    TT(pp[:, 11, :], pp[:, 11, :], pp[:, 14, :], ALU.subtract)
    TT(pp[:, 11, :], pp[:, 11, :], pp[:, 9, :], ALU.mult)
    ew("dve", lambda e: e.tensor_scalar(out=pp[:, 15, :], in0=pp[:, 4, :], scalar1=float(NB), scalar2=None, op0=ALU.mult))
    sincos(P, pp[:, 15, :], pp[:, 13, :], pp[:, 12, :], pt2[:, 0, :], pt2[:, 1, :], rg)
    ctf = ctD.rearrange("p a k -> p (a k)")
    stf = stD.rearrange("p a k -> p (a k)")
    ew("dve", lambda e: e.tensor_tensor(out=targ.rearrange("p (a k) -> p a k", a=16), in0=iot.unsqueeze(1).broadcast_to([128, 16, KD]),
                                        in1=bc_mid(pq[:, 17, :], KD), op=ALU.mult))
    sincos(P, targ, stf, ctf, ttmp, ttmp2, rg)
    P.op("sp", lambda e: e.dma_start(out=braw[:, 0], in_=W["od_b_re"].rearrange("(pr gl) p c -> (gl p) pr c", gl=2),
                                     allow_slow_non_contiguous=True), writes=[ldres()], dma_key=key)
    P.op("sp", lambda e: e.dma_start(out=braw[:, 1], in_=W["od_b_im"].rearrange("(pr gl) p c -> (gl p) pr c", gl=2),
                                     allow_slow_non_contiguous=True), writes=[ldres()], dma_key=key)
    cnat = mpad.rearrange("p (i j n) -> p i j n", i=2, j=2)[:, :, :, 0:128]
    nq = 0
    for i, nm in ((0, "od_c_re"), (1, "od_c_im")):
        cv = W[nm].rearrange("(pr gl) co p -> gl pr co p", gl=2)
        for j in range(2):
            for gl in range(2):
                nq += 1
                for pr8 in range(8):
                    nq += 1
                    P.op("sp", lambda e, i=i, j=j, gl=gl, cv=cv, pr8=pr8: e.dma_start(
                        out=cnat[pr8 * 16:(pr8 + 1) * 16, i, j, gl * 64:(gl + 1) * 64], in_=cv[gl, 8 * j + pr8]),
                        writes=[ldres()], dma_key=key)
    dnat = snat[0:8, 640:768]
    P.op("sp", lambda e: e.dma_start(out=dnat[0:4, :], in_=W["od_s5_d"].rearrange("(c p) -> c p", p=128)), writes=[ldres()], dma_key=key)
    P.op("sp", lambda e: e.dma_start(out=dnat[4:8, :], in_=W["od_glu_b"].rearrange("(c p) -> c p", p=128)), writes=[ldres()], dma_key=key)
    P.op("pe", lambda e: e.transpose(out=C.banks[1][:, 0:8], in_=dnat, identity=C.identf[0:8, 0:8]),
         reads=rg + r_ld + [C.r_const], writes=[C.rb[1]])
    P.op("act", lambda e: e.copy(out=dg8, in_=C.banks[1][:, 0:8]), reads=[C.rb[1]], writes=rg)
    P.op("pool", lambda e: e.dma_start(out=gw, in_=W["od_glu_w"].rearrange("(k p) n -> p k n", p=128)), writes=[ldres()], dma_key=key)

    ew("dve", lambda e: e.reduce_sum(out=sel, in_=C.identf.rearrange("p (a b) -> p a b", a=2), axis=AX.X))
    ew("dve", lambda e: e.reduce_sum(out=rmask, in_=C.identf.rearrange("p (a b) -> p a b", a=4), axis=AX.X))
    fre_b = bc_mid(pp[:, 10, :], 16)
    fim_b = bc_mid(pp[:, 11, :], 16)
    TT(bbD[0][:, 0], braw[:, 0], fre_b, ALU.mult)
    TT(btmp, braw[:, 1], fim_b, ALU.mult)
    TT(bbD[0][:, 0], bbD[0][:, 0], btmp, ALU.subtract)
    TT(bbD[0][:, 1], braw[:, 1], fre_b, ALU.mult)
    TT(btmp, braw[:, 0], fim_b, ALU.mult)
    TT(bbD[0][:, 1], bbD[0][:, 1], btmp, ALU.add)
    for d in range(1, DEC):
        lre_b = bc_mid(pq[:, d - 1, :], 16)
        lim_b = bc_mid(pq[:, 8 + d - 1, :], 16)
        TT(bbD[d][:, 0], bbD[0][:, 0], lre_b, ALU.mult)
        TT(btmp, bbD[0][:, 1], lim_b, ALU.mult)
        TT(bbD[d][:, 0], bbD[d][:, 0], btmp, ALU.subtract)
        TT(bbD[d][:, 1], bbD[0][:, 1], lre_b, ALU.mult)
        TT(btmp, bbD[0][:, 0], lim_b, ALU.mult)
        TT(bbD[d][:, 1], bbD[d][:, 1], btmp, ALU.add)
    for i in range(2):
        for j in range(2):
            P.op("pe", lambda e, i=i, j=j: e.transpose(out=C.banks[2 + j][:, 0:128], in_=cnat[:, i, j, :], identity=C.identf),
                 reads=rg + r_ld + [C.r_const], writes=[C.rb[2 + j]])
            P.op("act", lambda e, i=i, j=j: e.copy(out=craw[:, i, 8 * j:8 * j + 8, :],
                                                   in_=C.banks[2 + j][:, 0:128].rearrange("p (a b) -> p a b", a=8)),
                 reads=[C.rb[2 + j]], writes=rg)

    r_zero = Res("zero")
    for zt in [Cpad, BpadU] + CLpad:
        P.op("pool", lambda e, zt=zt: e.memset(zt, 0.0), writes=[r_zero])

    def build_pad(dst, src, neg_im):
        for i in range(2):
            for gl in range(2):
                d0 = dst[:, i]
                dv_ = bass.AP(d0.tensor, d0.offset + gl * 16, [[d0.ap[0][0], 128], [512, 4], [160, 4], [1, 16]])
                P.op("dve", lambda e, i=i, gl=gl, dv_=dv_: e.tensor_scalar(
                    out=dv_, in0=src[:, i].rearrange("p (a q) c -> p a q c", a=4), scalar1=sel[:, gl:gl + 1],
                    scalar2=(-1.0 if (i == 1 and neg_im) else 1.0), op0=ALU.mult, op1=ALU.mult),
                    reads=rg + r_ld + [r_zero], writes=rg)

    build_pad(Cpad, craw, True)
    for j in range(DEC - 1):
        lre_b = bc_mid(pq[:, j, :], 16)
        lim_b = bc_mid(pq[:, 8 + j, :], 16)
        TT(clraw[:, 0], craw[:, 0], lre_b, ALU.mult)
        TT(btmp, craw[:, 1], lim_b, ALU.mult)
        TT(clraw[:, 0], clraw[:, 0], btmp, ALU.subtract)
        TT(clraw[:, 1], craw[:, 0], lim_b, ALU.mult)
        TT(btmp, craw[:, 1], lre_b, ALU.mult)
        TT(clraw[:, 1], clraw[:, 1], btmp, ALU.add)
        build_pad(CLpad[j], clraw, True)
    mp5 = mpad.rearrange("p (i pr gl c) -> p i pr gl c", i=2, pr=16, gl=2)
    mp3 = mpad.rearrange("p (i j n) -> p i j n", i=2, j=4)
    for d in range(DEC):
        jj = DEC - 1 - d
        for i in range(2):
            for gl in range(2):
                ew("dve", lambda e, i=i, gl=gl, d=d: e.tensor_scalar(out=mp5[:, i, :, gl, :], in0=bbD[d][:, i], scalar1=sel[:, gl:gl + 1],
                                                                   scalar2=None, op0=ALU.mult))
        for i in range(2):
            for j in range(4):
                P.op("pe", lambda e, i=i, j=j: e.transpose(out=C.banks[2 + (j % 2)][:, 0:128], in_=mp3[:, i, j, :], identity=C.identf),
                     reads=rg + [C.r_const], writes=[C.rb[2 + (j % 2)]])
                P.op("dve", lambda e, i=i, j=j, jj=jj: e.tensor_tensor(
                    out=BpadD[jj][:, i, 4 * j:4 * j + 4, :], in0=C.banks[2 + (j % 2)][:, 0:128].unsqueeze(1).broadcast_to([128, 4, 128]),
                    in1=bc_mid(rmask, 128), op=ALU.mult), reads=rg + [C.rb[2 + (j % 2)]], writes=rg)
        if d < DEC - 1:
            build_pad(BpadU, bbD[d], False)
            for ch in range(4):
                pb = 4 + (ch % 2)
                n = 0
                for prl in range(4):
                    for i in range(2):
                        P.op("pe", lambda e, pb=pb, ch=ch, prl=prl, i=i, n=n: e.matmul(
                            C.banks[pb][:, 0:128], lhsT=BpadU[:, i, 4 * ch + prl, :], rhs=Cpad[:, i, 4 * ch + prl, :],
                            start=(n == 0), stop=(n == 7)), reads=rg, writes=[C.rb[pb]])
                        n += 1
                P.op("act", lambda e, pb=pb, ch=ch, d=d: e.copy(out=Kmat[:, ch, d, :], in_=C.banks[pb][:, 0:128]),
                     reads=[C.rb[pb]], writes=rg)
    r_rinit = Res("rinit")
    r_S = [Res("S%d" % bt) for bt in range(NBT)]
    P.op("dve", lambda e: e.memset(rinit, 0.0), reads=rg + r_ld, writes=[r_rinit] + rg)
    P.op("dve", lambda e: e.memset(Sbuf, 0.0), reads=rg, writes=r_S)
    P.off = setup_mark
    P.barrier()

    xt = [P.tile([128, D], F32) for _ in range(4)]
    r_xt = [Res("xt") for _ in range(4)]
    hT2 = [P.tile([128, 8, NB], BF16) for _ in range(2)]
    r_hT2 = [Res("hT") for _ in range(2)]
    wsl = [P.tile([128, 8, 512], BF16) for _ in range(1)]
    r_wsl = [Res("w") for _ in range(1)]
    uT = P.tile([128, 4, NB], F32)
    uTb = P.tile([128, 4, NB], BF16)
    r_u = [Res("u%d" % c) for c in range(4)]
    m4 = [P.tile([128, 512], F32) for _ in range(4)]
    r_m = Res("m4")
    rr = [P.tile([128, 512], F32) for _ in range(2)]
    r_r = Res("r")
    yT = P.tile([128, 4, NB], F32)
    yTb = P.tile([128, 4, NB], BF16)
    r_yT = [Res("yT%d" % c) for c in range(4)]
    gt = [P.tile([128, NB], F32) for _ in range(2)]
    r_gt = Res("gt")
    yco = P.tile([128, 4, NB], BF16)
    r_yco = Res("yco")
    scr = {"ss": P.tile([128, 8], F32), "junk": P.tile([128, D], BF16),
           "hb": [P.tile([128, D], BF16) for _ in range(2)]}
    r_scr = {"ss": Res("ss"), "junk": Res("junk"), "hb": [Res("hb0"), Res("hb1")]}
    if dbg is not None:
        dbg.update(pp=pp.rearrange('p a b -> p (a b)'), pq=pq.rearrange('p a b -> p (a b)'), Sre=Sbuf[:, 0].rearrange('p a b -> p (a b)'), Sim=Sbuf[:, 1].rearrange('p a b -> p (a b)'), uTb0=uTb[:, 0, :], yT0=yT[:, 0, :], rr0=rr[0], rr1=rr[1], B0=BpadD[0][:, 0, 0, :], Bl=BpadD[DEC - 1][:, 0, 0, :], Cp=Cpad[:, 0, 0, :], K0=Kmat[:, 0, 0, :])
    ycT_v = ycT.rearrange("c p t -> p c t")
    GC = 2.0 * math.sqrt(2.0 / math.pi)

    def dview(ap2):
        return ap2.rearrange("p (k j) -> p j k", j=DEC)

    load_x_tiles(P, xin, r_xin, xt, r_xt, 0, "o1x%d")
    rmsnorm_tiles(P, C, xt, r_xt, g_bc, r_g, hT2[0], r_hT2[0], (0, 1), scr, r_scr)
    for blk in range(nblk):
        t0 = blk * NB
        if hook is not None:
            hook(blk)
        hT, r_hT = hT2[blk % 2], r_hT2[blk % 2]
        if blk + 1 < nblk:
            load_x_tiles(P, xin, r_xin, xt, r_xt, blk + 1, "o1x%d")
        ws = 0
        WS.load(P, "ou", wsl[ws], r_wsl[ws], "o1w%d" % ws)
        for c in range(4):
            pb = 2 + (c % 2)
            for k in range(8):
                P.op("pe", lambda e, hT=hT, pb=pb, ws=ws, k=k, c=c: e.matmul(
                    C.banks[pb][:], lhsT=wsl[ws][:, k, c * 128:(c + 1) * 128], rhs=hT[:, k, :],
                    start=(k == 0), stop=(k == 7)), reads=[r_wsl[ws], r_hT], writes=[C.rb[pb]])
            P.op("act", lambda e, pb=pb, c=c: e.copy(out=uT[:, c, :], in_=C.banks[pb][:]), reads=[C.rb[pb]], writes=[r_u[c]])
            P.op("dve", lambda e, c=c: e.tensor_copy(out=uTb[:, c, :], in_=uT[:, c, :]), reads=[r_u[c]], writes=[r_u[c]])

        def issue_w(bt):
            pa, pbk = (4, 5) if bt % 2 == 0 else (6, 7)
            for q in range(PB):
                pr = bt * PB + q
                ch = pr // 4
                for (pbank, i) in ((pa, 0), (pbk, 1)):
                    for j in range(DEC):
                        P.op("pe", lambda e, pbank=pbank, i=i, pr=pr, ch=ch, j=j, q=q: e.matmul(
                            C.banks[pbank][:, q * KD:(q + 1) * KD], lhsT=BpadD[j][:, i, pr, :], rhs=dview(uTb[:, ch, :])[:, j, :],
                            start=(j == 0), stop=(j == DEC - 1)), reads=rg + [r_u[ch]], writes=[C.rb[pbank]])

        issue_w(0)
        for bt in range(NBT):
            pa, pbk = (4, 5) if bt % 2 == 0 else (6, 7)
            if bt + 1 < NBT:
                issue_w(bt + 1)
            p0 = bt * PB
            cb_ = ctD[:, p0:p0 + PB, :].rearrange("p a k -> p (a k)")
            sb_ = stD[:, p0:p0 + PB, :].rearrange("p a k -> p (a k)")
            bre, bim = C.banks[pa][:], C.banks[pbk][:]
            P.op("dve", lambda e, cb_=cb_, bre=bre: e.tensor_tensor(out=m4[0], in0=bre, in1=cb_, op=ALU.mult), reads=rg + [C.rb[pa]], writes=[r_m])
            P.op("dve", lambda e, sb_=sb_, bim=bim: e.tensor_tensor(out=m4[1], in0=bim, in1=sb_, op=ALU.mult), reads=rg + [C.rb[pbk]], writes=[r_m])
            P.op("dve", lambda e, cb_=cb_, bim=bim: e.tensor_tensor(out=m4[2], in0=bim, in1=cb_, op=ALU.mult), reads=rg + [C.rb[pbk]], writes=[r_m])
            P.op("dve", lambda e, sb_=sb_, bre=bre: e.tensor_tensor(out=m4[3], in0=bre, in1=sb_, op=ALU.mult), reads=rg + [C.rb[pa]], writes=[r_m])
            P.op("dve", lambda e: e.tensor_tensor(out=m4[0], in0=m4[0], in1=m4[1], op=ALU.add), reads=[r_m], writes=[r_m])
            P.op("dve", lambda e: e.tensor_tensor(out=m4[2], in0=m4[2], in1=m4[3], op=ALU.subtract), reads=[r_m], writes=[r_m])
            for q in range(PB):
                pr = p0 + q
                rho_b = pq[:, 16, pr:pr + 1].broadcast_to([128, KD])
                for i in range(2):
                    P.op("dve", lambda e, i=i, pr=pr, q=q, rho_b=rho_b: e.tensor_tensor_scan(
                        out=rr[i][:, q * KD:(q + 1) * KD], data0=rho_b, data1=m4[2 * i][:, q * KD:(q + 1) * KD],
                        initial=rinit[:, i, pr:pr + 1], op0=ALU.mult, op1=ALU.add),
                        reads=rg + [r_m, r_rinit], writes=[r_r])
            c5 = pp[:, 12, p0:p0 + PB]
            s5 = pp[:, 13, p0:p0 + PB]
            l_re = rr[0].rearrange("p (a k) -> p a k", a=PB)[:, :, KD - 1]
            l_im = rr[1].rearrange("p (a k) -> p a k", a=PB)[:, :, KD - 1]
            tA = pp[:, 14, p0:p0 + PB]
            tB = pp[:, 15, p0:p0 + PB]
            P.op("dve", lambda e, s5=s5, l_im=l_im, tA=tA: e.tensor_tensor(out=tA, in0=l_im, in1=s5, op=ALU.mult), reads=rg + [r_r], writes=rg)
            P.op("dve", lambda e, s5=s5, l_re=l_re, tB=tB: e.tensor_tensor(out=tB, in0=l_re, in1=s5, op=ALU.mult), reads=rg + [r_r], writes=rg)
            P.op("dve", lambda e, c5=c5, l_re=l_re, p0=p0: e.tensor_tensor(out=rinit[:, 0, p0:p0 + PB], in0=l_re, in1=c5, op=ALU.mult),
                 reads=rg + [r_r], writes=[r_rinit])
            P.op("dve", lambda e, c5=c5, l_im=l_im, p0=p0: e.tensor_tensor(out=rinit[:, 1, p0:p0 + PB], in0=l_im, in1=c5, op=ALU.mult),
                 reads=rg + [r_r], writes=[r_rinit])
            P.op("dve", lambda e, tA=tA, p0=p0: e.tensor_tensor(out=rinit[:, 0, p0:p0 + PB], in0=rinit[:, 0, p0:p0 + PB], in1=tA, op=ALU.subtract),
                 reads=rg + [r_rinit], writes=[r_rinit])
            P.op("dve", lambda e, tB=tB, p0=p0: e.tensor_tensor(out=rinit[:, 1, p0:p0 + PB], in0=rinit[:, 1, p0:p0 + PB], in1=tB, op=ALU.add),
                 reads=rg + [r_rinit], writes=[r_rinit])
            if blk > 0:
                for i in range(2):
                    P.op("act", lambda e, i=i, p0=p0: e.copy(out=Sbuf[:, i, p0:p0 + PB, 0], in_=Sbuf[:, i, p0:p0 + PB, KD]),
                         reads=[r_S[bt]], writes=[r_S[bt]])
            P.op("dve", lambda e, cb_=cb_: e.tensor_tensor(out=m4[0], in0=rr[0], in1=cb_, op=ALU.mult), reads=rg + [r_r, r_m], writes=[r_m])
            P.op("dve", lambda e, sb_=sb_: e.tensor_tensor(out=m4[1], in0=rr[1], in1=sb_, op=ALU.mult), reads=rg + [r_r], writes=[r_m])
            P.op("dve", lambda e, sb_=sb_: e.tensor_tensor(out=m4[2], in0=rr[0], in1=sb_, op=ALU.mult), reads=rg + [r_r], writes=[r_m])
            P.op("dve", lambda e, cb_=cb_: e.tensor_tensor(out=m4[3], in0=rr[1], in1=cb_, op=ALU.mult), reads=rg + [r_r], writes=[r_m])
            P.op("dve", lambda e, p0=p0: e.tensor_tensor(out=Sbuf[:, 0, p0:p0 + PB, 1:KD + 1], in0=m4[0].rearrange("p (a k) -> p a k", a=PB),
                                                         in1=m4[1].rearrange("p (a k) -> p a k", a=PB), op=ALU.subtract),
                 reads=[r_m], writes=[r_S[bt]])
            P.op("dve", lambda e, p0=p0: e.tensor_tensor(out=Sbuf[:, 1, p0:p0 + PB, 1:KD + 1], in0=m4[2].rearrange("p (a k) -> p a k", a=PB),
                                                         in1=m4[3].rearrange("p (a k) -> p a k", a=PB), op=ALU.add),
                 reads=[r_m], writes=[r_S[bt]])
            if (p0 + PB) % 4 == 0:
                ch = (p0 + PB) // 4 - 1
                yb_ = 2 + (ch % 2)
                bts = sorted(set((4 * ch + x) // PB for x in range(4)))
                for j in range(DEC):
                    n = 0
                    ntot = (j + 1 if j < DEC - 1 else 0) + 8
                    if j < DEC - 1:
                        for i in range(j + 1):
                            P.op("pe", lambda e, yb_=yb_, ch=ch, j=j, i=i, n=n, ntot=ntot: e.matmul(
                                C.banks[yb_][:, j * KD:(j + 1) * KD], lhsT=Kmat[:, ch, j - i, :], rhs=dview(uTb[:, ch, :])[:, i, :],
                                start=(n == 0), stop=(n == ntot - 1)), reads=rg + [r_u[ch]], writes=[C.rb[yb_]])
                            n += 1
                    for pr in range(4 * ch, 4 * ch + 4):
                        for i in range(2):
                            if j < DEC - 1:
                                lhs = CLpad[j][:, i, pr, :]
                                rhs = Sbuf[:, i, pr, 0:KD]
                            else:
                                lhs = Cpad[:, i, pr, :]
                                rhs = Sbuf[:, i, pr, 1:KD + 1]
                            P.op("pe", lambda e, yb_=yb_, j=j, lhs=lhs, rhs=rhs, n=n, ntot=ntot: e.matmul(
                                C.banks[yb_][:, j * KD:(j + 1) * KD], lhsT=lhs, rhs=rhs, start=(n == 0), stop=(n == ntot - 1)),
                                reads=rg + [r_S[x] for x in bts], writes=[C.rb[yb_]])
                            n += 1
                last_pair = True
                if last_pair:
                    P.op("dve", lambda e, yb_=yb_, ch=ch: e.scalar_tensor_tensor(
                        out=dview(yT[:, ch, :]), in0=dview(uT[:, ch, :]), scalar=dcol[:, ch:ch + 1],
                        in1=C.banks[yb_][:].rearrange("p (j k) -> p j k", j=DEC), op0=ALU.mult, op1=ALU.add),
                        reads=rg + [C.rb[yb_], r_u[ch]], writes=[r_yT[ch]])
                    P.op("act", lambda e, ch=ch: e.activation(out=gt[0], in_=yT[:, ch, :], func=AF.Square), reads=rg + [r_yT[ch]], writes=[r_gt])
                    P.op("dve", lambda e: e.tensor_scalar(out=gt[0], in0=gt[0], scalar1=0.044715, scalar2=1.0, op0=ALU.mult, op1=ALU.add),
                         reads=[r_gt], writes=[r_gt])
                    P.op("dve", lambda e, ch=ch: e.tensor_tensor(out=gt[0], in0=gt[0], in1=yT[:, ch, :], op=ALU.mult),
                         reads=[r_gt, r_yT[ch]], writes=[r_gt])
                    P.op("act", lambda e: e.activation(out=gt[1], in_=gt[0], func=AF.Sigmoid, scale=GC), reads=[r_gt], writes=[r_gt])
                    P.op("dve", lambda e, ch=ch: e.tensor_tensor(out=yT[:, ch, :], in0=yT[:, ch, :], in1=gt[1], op=ALU.mult),
                         reads=[r_gt, r_yT[ch]], writes=[r_yT[ch]])
                    P.op("act", lambda e, ch=ch: e.copy(out=yTb[:, ch, :], in_=yT[:, ch, :]), reads=[r_yT[ch]], writes=[r_yT[ch]])
        if blk + 1 < nblk:
            rmsnorm_tiles(P, C, xt, r_xt, g_bc, r_g, hT2[(blk + 1) % 2], r_hT2[(blk + 1) % 2], (0, 1), scr, r_scr)
        for co in range(4):
            pb = co % 2
            for ci in range(4):
                P.op("pe", lambda e, pb=pb, ci=ci, co=co: e.matmul(
                    C.banks[pb][:], lhsT=gw[:, ci, co * 128:(co + 1) * 128], rhs=yTb[:, ci, :], start=(ci == 0), stop=(ci == 3)),
                    reads=rg + [r_yT[ci]], writes=[C.rb[pb]])
            P.op("act", lambda e, pb=pb, co=co: e.activation(out=gt[1], in_=C.banks[pb][:], func=AF.Sigmoid,
                                                            bias=gbcol[:, co:co + 1], scale=1.0),
                 reads=rg + [C.rb[pb], r_gt], writes=[r_gt])
            P.op("dve", lambda e, co=co: e.tensor_tensor(out=yco[:, co, :], in0=yT[:, co, :], in1=gt[1], op=ALU.mult),
                 reads=[r_gt, r_yT[co]], writes=[r_yco])
        P.op("sp", lambda e, t0=t0: e.dma_start(out=ycT_v[:, :, t0:t0 + NB], in_=yco), reads=[r_yco], writes=[r_yc[blk]],
             dma_key="o1o")


ODD_SHAPES = {"od_norm": [1, D], "od_w_in": [1, D, 4608], "od_lam_re": [1, 32, 64], "od_lam_im": [1, 32, 64],
              "od_b_re": [1, 32, 64, 16], "od_b_im": [1, 32, 64, 16], "od_c_re": [1, 32, 16, 64], "od_c_im": [1, 32, 16, 64],
              "od_log_dt": [1, 32], "od_s5_d": [1, 512], "od_glu_w": [1, 512, 512], "od_glu_b": [1, 512],
              "od_gn_g": [1, D], "od_gn_b": [1, D], "od_w_out": [1, 1536, D]}
ODD_NAMES = list(ODD_SHAPES.keys())


def build_o1_only(nblk=NBLK, DEC=4):
    nc = bass.Bass("TRN2", target_bir_lowering=False)
    xin = nc.dram_tensor("x", [S, D], F32, kind="ExternalInput").ap()
    Wd = declare(nc, ODD_SHAPES)
    W = {k: v[0] for k, v in Wd.items()}
    iota = nc.dram_tensor("c_iota", [128, NB], F32, kind="ExternalInput").ap()
    ycT = nc.dram_tensor("ycT", [4, 128, S], BF16, kind="ExternalOutput").ap()
    P = Prog(nc, ARENA)
    C = Ctx()
    setup_common(P, C)
    r_in = [Res("xin") for _ in range(32)]
    r_yc = [Res("yc") for _ in range(NBLK)]
    dbg = {}
    WS = WStore(nc)
    define_panels(WS, Wo=W)
    WS.emit_group(P, "D")
    phase_o1(P, C, WS, xin, r_in, W, iota, ycT, r_yc, nblk=nblk, dbg=dbg, DEC=DEC)
    P.barrier()
    r_d = Res("dbg")
    for k, ap in dbg.items():
        o = nc.dram_tensor("dbg_" + k, [ap.shape[0], ap.shape[1]], F32, kind="ExternalOutput").ap()
        P.op("pool", lambda e, o=o, ap=ap: e.dma_start(out=o, in_=ap), writes=[r_d], dma_key="dbg")
    P.op("sp", None, reads=r_yc[:nblk] + [r_d])
    P.emit()
    return nc, P


RET_GAMMA = (1.0 - np.exp(np.linspace(math.log(1.0 / 32), math.log(1.0 / 512), 4, dtype=np.float32))).astype(np.float32)


def host_consts():
    c = {}
    c["c_iota"] = np.broadcast_to(np.arange(NB, dtype=np.float32)[None, :], (128, NB)).copy()
    inv_freq = (np.float32(10000.0) ** (-np.arange(0, 256, 2, dtype=np.float32) / np.float32(256))).astype(np.float32)
    ang = (np.arange(S, dtype=np.float32)[None, :] * inv_freq[:, None]).astype(np.float32)
    c["c_cos"] = np.cos(ang).astype(np.float32)
    c["c_sin"] = np.sin(ang).astype(np.float32)
    lg = np.log(RET_GAMMA.astype(np.float64))
    idx = np.arange(128, dtype=np.float64)
    diff = idx[None, :] - idx[:, None]
    dm = np.where(diff >= 0, np.exp(lg[:, None, None] * np.maximum(diff, 0.0)[None]), 0.0) / 16.0
    c["c_dmat"] = np.ascontiguousarray(dm.transpose(1, 0, 2)).astype(np.float32).reshape(128, 512)
    kd = np.exp(lg[None, :] * (127.0 - idx)[:, None]) / 16.0
    qd = np.exp(lg[None, :] * (idx + 1.0)[:, None])
    c["c_kq"] = np.concatenate([kd, qd], axis=1).astype(np.float32)
    return c


CONST_SHAPES = {"c_iota": [128, NB], "c_cos": [128, S], "c_sin": [128, S], "c_dmat": [128, 512], "c_kq": [128, 8]}


def phase_o2(P, C, WS, xin, r_xin, W, K_, ycT, r_yc, xout, r_xout, nblk=NBLK, hook=None):
    P.reset()
    QC0, KC0, VC0, GC0 = 512, 1536, 2560, 3584
    r_g = Res("g")
    rg = [r_g]
    key = "o2g"
    g_bc = P.tile([128, D], F32)
    gng = P.tile([128, D], F32)
    gnb = P.tile([128, D], F32)
    load_bcast_vec(P, g_bc, W["od_norm"], r_g, key)
    load_bcast_vec(P, gng, W["od_gn_g"], r_g, key)
    load_bcast_vec(P, gnb, W["od_gn_b"], r_g, key)
    dmat = P.tile([128, 4, 128], F32)
    kq = P.tile([128, 8], F32)
    P.op("sp", lambda e: e.dma_start(out=dmat, in_=K_["c_dmat"].rearrange("p (h l) -> p h l", h=4)), writes=rg, dma_key=key)
    P.op("sp", lambda e: e.dma_start(out=kq, in_=K_["c_kq"]), writes=rg, dma_key=key)
    cg = [float(np.float64(RET_GAMMA[h]) ** 128) for h in range(4)]
    prev = P.tile([128, 4, 512], F32)
    prev_bf = P.tile([128, 4, 512], BF16)
    r_prev = [Res("prev%d" % h) for h in range(4)]
    r_prevbf = [Res("prevbf%d" % h) for h in range(4)]
    P.op("dve", lambda e: e.memset(prev, 0.0), writes=r_prev)
    P.op("dve", lambda e: e.memset(prev_bf, 0.0), writes=r_prevbf)
    xt = [P.tile([128, D], F32) for _ in range(4)]
    r_xt = [Res("xt") for _ in range(4)]
    hT2 = [P.tile([128, 8, NB], BF16) for _ in range(2)]
    r_hT2 = [Res("hT") for _ in range(2)]
    NXR = 8
    xres = [P.tile([128, 512], F32) for _ in range(NXR)]
    r_xres = [Res("xres") for _ in range(NXR)]
    xrc = 0
    NW = 4
    wsl = [P.tile([128, 8, 512], BF16) for _ in range(NW)]
    r_wsl = [Res("w") for _ in range(NW)]
    cs = P.tile([128, NB], F32)
    sn = P.tile([128, NB], F32)
    r_cs = Res("cs")
    qT = P.tile([128, 8, NB], BF16)
    kT = P.tile([128, 8, NB], BF16)
    r_qT = [Res("qT%d" % h) for h in range(4)]
    r_kT = [Res("kT%d" % h) for h in range(4)]
    v_tm = P.tile([128, 4, D], BF16)
    r_v = [Res("v%d" % i) for i in range(4)]
    sgate = [P.tile([128, D], F32) for _ in range(4)]
    r_sg = [Res("sg%d" % i) for i in range(4)]
    mixT2 = [P.tile([128, 12, NB], BF16) for _ in range(2)]
    r_mix_c2 = [Res("mixc") for _ in range(2)]
    r_mix_d2 = [[Res("mixd%d" % q) for q in range(4)] for _ in range(2)]
    pending_wout = None
    mm_ = [P.tile([128, NB], F32) for _ in range(4)]
    r_mm = [Res("mm%d" % i) for i in range(2)]
    sc_bf = P.tile([128, 4, 128], BF16)
    r_sc = Res("sc")
    ktd = P.tile([128, D], BF16)
    r_ktd = Res("ktd")
    tmp = P.tile([128, D], F32)
    r_tmp = Res("tmp")
    ov = P.tile([128, D], F32)
    r_o = Res("o")
    st8 = P.tile([128, 4, 4], F32)
    r_st = Res("st8")
    yd_tm = P.tile([128, D], BF16)
    r_yd = Res("yd")
    scr = {"ss": P.tile([128, 8], F32), "junk": P.tile([128, D], BF16),
           "hb": [P.tile([128, D], BF16) for _ in range(2)]}
    r_scr = {"ss": Res("ss"), "junk": Res("junk"), "hb": [Res("hb0"), Res("hb1")]}
    wcnt = 0
    rcnt = 0

    def next_slot():
        nonlocal wcnt
        ws = wcnt % NW
        wcnt += 1
        return ws

    Win = W["od_w_in"]
    ycT_v = ycT.rearrange("c p t -> p c t")
    pcnt = 0
    load_x_tiles(P, xin, r_xin, xt, r_xt, 0, "o2x%d")
    rmsnorm_tiles(P, C, xt, r_xt, g_bc, r_g, hT2[0], r_hT2[0], (0, 1), scr, r_scr)
    for blk in range(nblk):
        t0 = blk * NB
        if hook is not None:
            hook(blk)
        hT, r_hT = hT2[blk % 2], r_hT2[blk % 2]
        mixT, r_mix_c, r_mix_d = mixT2[blk % 2], r_mix_c2[blk % 2], r_mix_d2[blk % 2]
        if blk + 1 < nblk:
            load_x_tiles(P, xin, r_xin, xt, r_xt, blk + 1, "o2x%d")
        P.op("sp", lambda e, t0=t0: e.dma_start(out=cs, in_=K_["c_cos"][:, t0:t0 + NB]), writes=[r_cs], dma_key="o2t")
        P.op("sp", lambda e, t0=t0: e.dma_start(out=sn, in_=K_["c_sin"][:, t0:t0 + NB]), writes=[r_cs], dma_key="o2t")
        P.op("sp", lambda e, t0=t0, mixT=mixT: e.dma_start(out=mixT[:, 0:4, :], in_=ycT_v[:, :, t0:t0 + NB]),
             reads=[r_yc[blk]], writes=[r_mix_c], dma_key="o2yc")
        for (dstT, r_dst, c0, wnm) in ((qT, r_qT, QC0, "oq"), (kT, r_kT, KC0, "ok")):
            for pn in range(2):
                ws = next_slot()
                WS.load(P, "%s%d" % (wnm, pn), wsl[ws], r_wsl[ws], "o2w%d" % ws)
                for hh in range(2):
                    h = 2 * pn + hh
                    pbase = 2 if (rcnt % 2 == 0) else 4
                    rcnt += 1
                    for half in range(2):
                        cc = hh * 2 + half
                        pb = pbase + half
                        for k in range(8):
                            P.op("pe", lambda e, hT=hT, pb=pb, ws=ws, k=k, cc=cc: e.matmul(
                                C.banks[pb][:], lhsT=wsl[ws][:, k, cc * 128:(cc + 1) * 128], rhs=hT[:, k, :],
                                start=(k == 0), stop=(k == 7)), reads=[r_wsl[ws], r_hT], writes=[C.rb[pb]])
                    A_, B_ = C.banks[pbase][:], C.banks[pbase + 1][:]
                    P.op("dve", lambda e, A_=A_: e.tensor_tensor(out=mm_[0], in0=A_, in1=cs, op=ALU.mult),
                         reads=[C.rb[pbase], r_cs], writes=[r_mm[0]])
                    P.op("dve", lambda e, B_=B_: e.tensor_tensor(out=mm_[1], in0=B_, in1=sn, op=ALU.mult),
                         reads=[C.rb[pbase + 1], r_cs], writes=[r_mm[0]])
                    P.op("dve", lambda e, A_=A_: e.tensor_tensor(out=mm_[2], in0=A_, in1=sn, op=ALU.mult),
                         reads=[C.rb[pbase], r_cs], writes=[r_mm[1]])
                    P.op("dve", lambda e, B_=B_: e.tensor_tensor(out=mm_[3], in0=B_, in1=cs, op=ALU.mult),
                         reads=[C.rb[pbase + 1], r_cs], writes=[r_mm[1]])
                    P.op("dve", lambda e, h=h, dstT=dstT: e.tensor_tensor(out=dstT[:, 2 * h, :], in0=mm_[0], in1=mm_[1], op=ALU.subtract),
                         reads=[r_mm[0]], writes=[r_dst[h]])
                    P.op("dve", lambda e, h=h, dstT=dstT: e.tensor_tensor(out=dstT[:, 2 * h + 1, :], in0=mm_[2], in1=mm_[3], op=ALU.add),
                         reads=[r_mm[1]], writes=[r_dst[h]])
        for (c0, isg) in ((VC0, False), (GC0, True)):
            for pn in range(2):
                ws = next_slot()
                WS.load(P, "%s%d" % ("og" if isg else "ov", pn), wsl[ws], r_wsl[ws], "o2w%d" % ws)
                for i in range(4):
                    pb = 2 + (pcnt % 2)
                    pcnt += 1
                    for k in range(8):
                        P.op("pe", lambda e, hT=hT, pb=pb, ws=ws, k=k, i=i: e.matmul(
                            C.banks[pb][:], lhsT=hT[:, k, i * 128:(i + 1) * 128], rhs=wsl[ws][:, k, :],
                            start=(k == 0), stop=(k == 7)), reads=[r_wsl[ws], r_hT], writes=[C.rb[pb]])
                    if isg:
                        P.op("act", lambda e, pb=pb, i=i, pn=pn: e.activation(out=sgate[i][:, pn * 512:(pn + 1) * 512],
                                                                               in_=C.banks[pb][:], func=AF.Silu),
                             reads=[C.rb[pb]], writes=[r_sg[i]])
                    else:
                        P.op("act", lambda e, pb=pb, i=i, pn=pn: e.copy(out=v_tm[:, i, pn * 512:(pn + 1) * 512], in_=C.banks[pb][:]),
                             reads=[C.rb[pb]], writes=[r_v[i]])
        pend = []
        if pending_wout is not None:
            pending_wout["start"]()
        for q in range(4):
            tq = q * 128
            for h in range(4):
                for dc in range(2):
                    P.op("pe", lambda e, h=h, dc=dc, tq=tq: e.matmul(
                        C.banks[4][:, h * 128:(h + 1) * 128], lhsT=kT[:, 2 * h + dc, tq:tq + 128], rhs=qT[:, 2 * h + dc, tq:tq + 128],
                        start=(dc == 0), stop=(dc == 1)), reads=[r_kT[h], r_qT[h]], writes=[C.rb[4]])
            P.op("dve", lambda e: e.tensor_tensor(out=sc_bf, in0=C.banks[4][:].rearrange("p (h l) -> p h l", h=4), in1=dmat, op=ALU.mult),
                 reads=rg + [C.rb[4]], writes=[r_sc])
            tb0 = bank_bf(C, 0)
            for c in range(8):
                P.op("pe", lambda e, c=c, tq=tq, tb0=tb0: e.transpose(out=tb0[:, c * 128:(c + 1) * 128], in_=kT[:, c, tq:tq + 128],
                                                                      identity=C.ident),
                     reads=[r_kT[c // 2], C.r_const], writes=[C.rb[0]])
            if pending_wout is not None:
                pending_wout["piece"](q // 2, 2 * (q % 2), 1)
            P.op("dve", lambda e, tb0=tb0: e.tensor_tensor(out=ktd.rearrange("p (h d) -> p h d", h=4),
                                                           in0=tb0.rearrange("p (h d) -> p h d", h=4),
                                                           in1=bc_mid(kq[:, 0:4], 256), op=ALU.mult),
                 reads=rg + [C.rb[0]], writes=[r_ktd])
            for h in range(4):
                pb = 6 + h // 2
                P.op("pe", lambda e, pb=pb, h=h, q=q: e.matmul(
                    C.banks[pb][:, (h % 2) * 256:(h % 2 + 1) * 256], lhsT=sc_bf[:, h, :], rhs=v_tm[:, q, h * 256:(h + 1) * 256],
                    start=True, stop=True), reads=[r_sc, r_v[q]], writes=[C.rb[pb]])
            for h in range(4):
                pb = 2 + h // 2
                for dc in range(2):
                    P.op("pe", lambda e, pb=pb, h=h, dc=dc, tq=tq: e.matmul(
                        C.banks[pb][:, (h % 2) * 256:(h % 2 + 1) * 256], lhsT=qT[:, 2 * h + dc, tq:tq + 128],
                        rhs=prev_bf[:, h, dc * 256:(dc + 1) * 256], start=(dc == 0), stop=(dc == 1)),
                        reads=[r_qT[h], r_prevbf[h]], writes=[C.rb[pb]])
            for h in range(4):
                pb = 2 + h // 2
                P.op("act", lambda e, pb=pb, h=h: e.activation(
                    out=tmp[:, h * 256:(h + 1) * 256], in_=C.banks[pb][:, (h % 2) * 256:(h % 2 + 1) * 256], func=AF.Identity,
                    scale=kq[:, 4 + h:5 + h]), reads=rg + [C.rb[pb]], writes=[r_tmp])
            for hf in range(2):
                P.op("dve", lambda e, hf=hf: e.tensor_tensor(out=ov[:, hf * 512:(hf + 1) * 512], in0=tmp[:, hf * 512:(hf + 1) * 512],
                                                             in1=C.banks[6 + hf][:], op=ALU.add),
                     reads=[r_tmp, C.rb[6 + hf]], writes=[r_o])
            for h in range(4):
                for dc in range(2):
                    P.op("pe", lambda e, h=h, dc=dc, q=q: e.matmul(
                        C.banks[5][:, dc * 256:(dc + 1) * 256], lhsT=ktd[:, h * 256 + dc * 128:h * 256 + (dc + 1) * 128],
                        rhs=v_tm[:, q, h * 256:(h + 1) * 256], start=True, stop=True),
                        reads=[r_ktd, r_v[q]], writes=[C.rb[5]])
                P.op("dve", lambda e, h=h: e.scalar_tensor_tensor(out=prev[:, h, :], in0=prev[:, h, :], scalar=cg[h], in1=C.banks[5][:],
                                                                  op0=ALU.mult, op1=ALU.add),
                     reads=[C.rb[5], r_prev[h]], writes=[r_prev[h]])
                P.op("act", lambda e, h=h: e.copy(out=prev_bf[:, h, :], in_=prev[:, h, :]), reads=[r_prev[h]], writes=[r_prevbf[h]])
            if pending_wout is not None:
                pending_wout["piece"](q // 2, 2 * (q % 2) + 1, 1)
            while pend:
                pend.pop(0)()
            for h in range(4):
                hs = slice(h * 256, (h + 1) * 256)
                P.op("act", lambda e, h=h, hs=hs: e.activation(out=tmp[:, hs], in_=ov[:, hs], func=AF.Identity, accum_out=st8[:, 0, h:h + 1]),
                     reads=[r_o], writes=[r_tmp, r_st])
                P.op("act", lambda e, h=h, hs=hs: e.activation(out=tmp[:, hs], in_=ov[:, hs], func=AF.Square, accum_out=st8[:, 1, h:h + 1]),
                     reads=[r_o], writes=[r_tmp, r_st])
            P.op("dve", lambda e: e.tensor_scalar(out=st8[:, 2, :], in0=st8[:, 0, :], scalar1=1.0 / 256, scalar2=None, op0=ALU.mult),
                 reads=[r_st], writes=[r_st])
            P.op("dve", lambda e: e.tensor_tensor(out=st8[:, 0, :], in0=st8[:, 2, :], in1=st8[:, 2, :], op=ALU.mult), reads=[r_st], writes=[r_st])
            P.op("dve", lambda e: e.scalar_tensor_tensor(out=st8[:, 3, :], in0=st8[:, 1, :], scalar=1.0 / 256, in1=st8[:, 0, :],
                                                         op0=ALU.mult, op1=ALU.subtract), reads=[r_st], writes=[r_st])
            P.op("act", lambda e: e.activation(out=st8[:, 3, :], in_=st8[:, 3, :], func=AF.Ln, bias=EPS, scale=1.0), reads=[r_st], writes=[r_st])
            P.op("act", lambda e: e.activation(out=st8[:, 3, :], in_=st8[:, 3, :], func=AF.Exp, scale=-0.5), reads=[r_st], writes=[r_st])
            for h in range(4):
                hs = slice(h * 256, (h + 1) * 256)
                P.op("dve", lambda e, h=h, hs=hs: e.tensor_scalar(out=ov[:, hs], in0=ov[:, hs], scalar1=st8[:, 2, h:h + 1],
                                                                   scalar2=st8[:, 3, h:h + 1], op0=ALU.subtract, op1=ALU.mult),
                     reads=[r_st, r_o], writes=[r_o])
            P.op("dve", lambda e: e.tensor_tensor(out=ov, in0=ov, in1=gng, op=ALU.mult), reads=rg + [r_o], writes=[r_o])
            P.op("dve", lambda e: e.tensor_tensor(out=ov, in0=ov, in1=gnb, op=ALU.add), reads=rg + [r_o], writes=[r_o])
            P.op("dve", lambda e, q=q: e.tensor_tensor(out=yd_tm, in0=ov, in1=sgate[q], op=ALU.mult), reads=[r_o, r_sg[q]], writes=[r_yd])
            tb1 = bank_bf(C, 1)

            def emit_yd(q=q, tq=tq, tb1=tb1, mixT=mixT, r_mix_d=r_mix_d):
                for c in range(8):
                    P.op("pe", lambda e, c=c, tb1=tb1: e.transpose(out=tb1[:, c * 128:(c + 1) * 128], in_=yd_tm[:, c * 128:(c + 1) * 128],
                                                                   identity=C.ident), reads=[r_yd, C.r_const], writes=[C.rb[1]])
                P.op("act", lambda e, tb1=tb1, tq=tq, mixT=mixT: e.copy(out=mixT[:, 4:12, tq:tq + 128], in_=tb1.rearrange("p (k n) -> p k n", k=8)),
                     reads=[C.rb[1]], writes=[r_mix_d[q]])
            pend.append(emit_yd)
        while pend:
            pend.pop(0)()
        if blk + 1 < nblk:
            rmsnorm_tiles(P, C, xt, r_xt, g_bc, r_g, hT2[(blk + 1) % 2], r_hT2[(blk + 1) % 2], (0, 1), scr, r_scr)
        def make_wout(blk=blk, t0=t0, mixT=mixT, r_mix_c=r_mix_c, r_mix_d=r_mix_d):
            xr_of = {}
            slots = {}

            def issue_xres(half, i):
                nonlocal xrc
                k_ = xrc % NXR
                xrc += 1
                xr_of[(half, i)] = k_
                P.op("sp", lambda e, k_=k_, i=i, half=half: e.dma_start(
                    out=xres[k_], in_=xin[t0 + i * 128:t0 + (i + 1) * 128, half * 512:(half + 1) * 512]),
                    reads=[r_xin[blk * 4 + i]], writes=[r_xres[k_]], dma_key="o2r" + str(k_))

            def start():
                nonlocal wcnt
                for hi in [(half, i) for half in range(2) for i in range(4)][:NXR]:
                    issue_xres(*hi)
                for half in range(2):
                    while wcnt % NW not in (0, 2):
                        wcnt += 1
                    ws = next_slot()
                    ws2 = next_slot()
                    WS.load(P, "oo%d0" % half, wsl[ws], r_wsl[ws], "o2w%d" % ws)
                    WS.load(P, "oo%d1" % half, wsl[ws2], r_wsl[ws2], "o2w%d" % ws2)
                    slots[half] = (ws, ws2)

            def piece(half, i, pb):
                ws, ws2 = slots[half]
                for c in range(12):
                    wsx = ws if c < 8 else ws2
                    P.op("pe", lambda e, pb=pb, c=c, i=i, wsx=wsx: e.matmul(
                        C.banks[pb][:], lhsT=mixT[:, c, i * 128:(i + 1) * 128], rhs=wsl[wsx][:, c % 8, :],
                        start=(c == 0), stop=(c == 11)),
                        reads=[r_mix_d[i], r_mix_c, r_wsl[wsx]], writes=[C.rb[pb]])
                if (half, i) not in xr_of:
                    issue_xres(half, i)
                k_ = xr_of[(half, i)]
                xr_ = xres[k_]
                r_xr = r_xres[k_]
                P.op("dve", lambda e, pb=pb, xr_=xr_: e.tensor_tensor(out=xr_, in0=xr_, in1=C.banks[pb][:], op=ALU.add),
                     reads=[C.rb[pb], r_xr], writes=[r_xr])
                P.op("sp", lambda e, i=i, half=half, xr_=xr_: e.dma_start(
                    out=xout[t0 + i * 128:t0 + (i + 1) * 128, half * 512:(half + 1) * 512], in_=xr_),
                    reads=[r_xr], writes=[r_xout[blk * 4 + i]], dma_key="o2o")

            return {"start": start, "piece": piece}

        pending_wout = make_wout()
    if pending_wout is not None:
        pending_wout["start"]()
        n_ = 0
        for half in range(2):
            for i in range(4):
                pending_wout["piece"](half, i, 2 + (n_ % 2))
                n_ += 1


def build_odd_only(nblk=NBLK):
    nc = bass.Bass("TRN2", target_bir_lowering=False)
    xin = nc.dram_tensor("x", [S, D], F32, kind="ExternalInput").ap()
    Wd = declare(nc, ODD_SHAPES)
    W = {k: v[0] for k, v in Wd.items()}
    K_ = declare(nc, CONST_SHAPES)
    ycT = nc.dram_tensor("ycT", [4, 128, S], BF16, kind="Internal").ap()
    out = nc.dram_tensor("out", [S, D], F32, kind="ExternalOutput").ap()
    P = Prog(nc, ARENA)
    C = Ctx()
    setup_common(P, C)
    r_in = [Res("xin") for _ in range(32)]
    r_yc = [Res("yc") for _ in range(NBLK)]
    r_out = [Res("xout") for _ in range(32)]
    WS = WStore(nc)
    define_panels(WS, Wo=W)
    WS.emit_group(P, "D")
    phase_o1(P, C, WS, xin, r_in, W, K_["c_iota"], ycT, r_yc, nblk=nblk)
    P.barrier()
    phase_o2(P, C, WS, xin, r_in, W, K_, ycT, r_yc, out, r_out, nblk=nblk)
    P.op("sp", None, reads=r_out[:nblk * 4])
    P.emit()
    return nc, P


FFN_SHAPES = {"ffn_norm": [2, D], "ffn_w_gate": [2, D, FF], "ffn_w_up": [2, D, FF], "ffn_w_down": [2, FF, D],
              "final_norm": [D]}


def build_full(nblk=NBLK):
    nc = bass.Bass("TRN2", target_bir_lowering=False)
    xin = nc.dram_tensor("x", [S, D], F32, kind="ExternalInput").ap()
    We = {k: v[0] for k, v in declare(nc, EVEN_SHAPES).items()}
    Wo = {k: v[0] for k, v in declare(nc, ODD_SHAPES).items()}
    Wf = declare(nc, FFN_SHAPES)
    K_ = declare(nc, CONST_SHAPES)
    ybT = nc.dram_tensor("ybT", [8, 128, S], BF16, kind="Internal").ap()
    ycT = nc.dram_tensor("ycT", [4, 128, S], BF16, kind="Internal").ap()
    xa = nc.dram_tensor("xa", [S, D], F32, kind="Internal").ap()
    xb = nc.dram_tensor("xb", [S, D], F32, kind="Internal").ap()
    out = nc.dram_tensor("out", [S, D], F32, kind="ExternalOutput").ap()
    P = Prog(nc, ARENA)
    C = Ctx()
    setup_common(P, C)

    def rl(n, name):
        return [Res(name) for _ in range(n)]
    r_x, r_xa, r_xb, r_out = rl(32, "x"), rl(32, "xa"), rl(32, "xb"), rl(32, "out")
    r_yb, r_yc = rl(NBLK, "yb"), rl(NBLK, "yc")
    WS = WStore(nc)
    define_panels(WS, We=We, Wo=Wo, Wf=Wf)
    WS.emit_group(P, "A")
    WS.emit_group(P, "B")
    phase_e1(P, C, WS, xin, r_x, We, ybT, r_yb, nblk=nblk)
    P.barrier()
    WS.emit_group(P, "C")
    phase_e2(P, C, WS, xin, r_x, We, ybT, r_yb, xa, r_xa, nblk=nblk)
    P.barrier()
    WS.emit_group(P, "D")
    phase_ffn(P, C, WS, 0, xa, r_xa, xb, r_xb, Wf["ffn_norm"][0], nblk=nblk, tag="f")
    P.barrier()
    WS.emit_group(P, "E")
    phase_o1(P, C, WS, xb, r_xb, Wo, K_["c_iota"], ycT, r_yc, nblk=nblk)
    P.barrier()
    phase_o2(P, C, WS, xb, r_xb, Wo, K_, ycT, r_yc, xa, r_xa, nblk=nblk)
    P.barrier()
    phase_ffn(P, C, WS, 1, xa, r_xa, out, r_out, Wf["ffn_norm"][1], g_final=Wf["final_norm"], nblk=nblk, tag="f")
    P.op("sp", None, reads=r_out[:nblk * 4])
    P.emit()
    return nc, P


_CACHE = {}


def kernel(**inputs):
    n = 8
    if "nc" not in _CACHE:
        _CACHE["nc"] = build_full()[0]
        _CACHE["consts"] = host_consts()
    nc = _CACHE["nc"]
    consts = _CACHE["consts"]
    x = np.ascontiguousarray(np.asarray(inputs["x"], dtype=np.float32))
    shared = {}
    for k in list(EVEN_SHAPES) + list(ODD_SHAPES) + list(FFN_SHAPES):
        shared[k] = np.ascontiguousarray(np.asarray(inputs[k], dtype=np.float32))
    shared.update(consts)
    in_maps = [dict(x=x[c], **shared) for c in range(n)]
    res = run_bass_kernel_spmd(nc, in_maps, core_ids=list(range(n)))
    return np.stack([np.asarray(r["out"], dtype=np.float32) for r in res.results], axis=0)
```

```python
import math
import os
import numpy as np
from contextlib import ExitStack
import concourse.bass as bass
import concourse.mybir as mybir
from concourse.bass_utils import run_bass_kernel_spmd

F32 = mybir.dt.float32
BF16 = mybir.dt.bfloat16
ALU = mybir.AluOpType
AF = mybir.ActivationFunctionType
AX = mybir.AxisListType

ENGS = ("pe", "act", "dve", "pool", "sp")
SAME_ENG_SYNC = True
SAME_ENG_WINDOW = 1
FFN_WQ = "pool"

S = 4096
D = 1024
FF = 2816
EPS = 1e-5
NB = 512
NBLK = S // NB
ARENA = 51 * 1024


class Res:
    __slots__ = ("name", "w", "rs")

    def __init__(self, name=""):
        self.name = name
        self.w = None
        self.rs = []


class Op:
    __slots__ = ("eng", "fn", "deps", "dma_key", "signal", "ev", "waits", "is_dma", "dma_cnt", "idx", "phase")


class Prog:
    def __init__(self, nc, arena_words):
        self.nc = nc
        self.stack = ExitStack()
        self.ops = {e: [] for e in ENGS}
        self.dma_counts = {}
        self.dma_last = {}
        self.arena = self.stack.enter_context(nc.sbuf_tensor("arena", [128, arena_words], F32))
        self.arena_words = arena_words
        self.off = 0
        self.keep = 0
        self.phase = 0

    def alloc(self, nwords):
        o = self.off
        self.off += nwords
        assert self.off <= self.arena_words, ("arena overflow", self.off, self.arena_words)
        return o

    def tile(self, shape, dt, name=None):
        n = 1
        for s_ in shape[1:]:
            n *= s_
        if dt == BF16:
            words = (n + 1) // 2
        else:
            words = n
        o = self.alloc(words)
        ap = self.arena[0:shape[0], o:o + words]
        if dt == BF16:
            ap = ap.bitcast(BF16)
            if n % 2:
                ap = ap[:, 0:n]
        if len(shape) == 3:
            ap = ap.rearrange("p (a b) -> p a b", a=shape[1])
        elif len(shape) == 4:
            ap = ap.rearrange("p (a b c) -> p a b c", a=shape[1], b=shape[2])
        return ap

    def reset(self):
        self.off = self.keep

    def ps(self, name, shape, dt):
        return self.stack.enter_context(self.nc.psum_tensor(name, list(shape), dt))

    def op(self, eng, fn, reads=(), writes=(), dma_key=None):
        o = Op()
        o.eng = eng
        o.fn = fn
        o.dma_key = dma_key
        o.is_dma = dma_key is not None
        o.signal = o.is_dma
        o.ev = None
        o.idx = len(self.ops[eng])
        o.phase = self.phase
        raw = {}
        oth = {}
        for r in reads:
            if r.w is not None:
                raw[id(r.w)] = r.w
        for w in writes:
            if w.w is not None:
                oth[id(w.w)] = w.w
            last = {}
            for rd in w.rs:
                k = rd.dma_key if rd.is_dma else rd.eng
                last[k] = rd
            for rd in last.values():
                oth[id(rd)] = rd
        dl = []
        seen = set()
        for is_raw, dd in ((True, raw), (False, oth)):
            for d in dd.values():
                if d is o or id(d) in seen:
                    continue
                if (not d.is_dma) and d.eng == eng and not o.is_dma:
                    if eng == "pe":
                        continue
                    if (not is_raw) or (o.idx - d.idx > SAME_ENG_WINDOW):
                        continue
                seen.add(id(d))
                if d.is_dma:
                    dl.append((d, self.dma_counts[d.dma_key]))
                else:
                    d.signal = True
                    dl.append((d, None))
        o.deps = dl
        if o.is_dma:
            self.dma_counts[dma_key] = self.dma_counts.get(dma_key, 0) + 1
            o.dma_cnt = self.dma_counts[dma_key]
            self.dma_last[dma_key] = o
        for r in reads:
            r.rs.append(o)
        for w in writes:
            w.w = o
            w.rs = []
        self.ops[eng].append(o)
        return o

    def barrier(self):
        lasts = []
        for e in ENGS:
            for o in reversed(self.ops[e]):
                if (not o.is_dma) and o.fn is not None:
                    o.signal = True
                    lasts.append(o)
                    break
        for e in ENGS:
            w = Op()
            w.eng = e
            w.fn = None
            w.is_dma = False
            w.signal = False
            w.dma_key = None
            w.ev = None
            w.idx = len(self.ops[e])
            w.phase = self.phase
            w.deps = [(o, None) for o in lasts if (o.eng != e or SAME_ENG_SYNC)]
            w.deps += [(o, self.dma_counts[k]) for k, o in self.dma_last.items()]
            self.ops[e].append(w)
        self.phase += 1

    def emit(self):
        nc = self.nc
        st = self.stack
        sems = {}
        dsems = {k: st.enter_context(nc.semaphore("d_" + k)) for k in self.dma_counts}
        self.sem_max = 0
        for e in ENGS:
            c = {}
            for o in self.ops[e]:
                if o.is_dma:
                    o.ev = (dsems[o.dma_key], 16 * o.dma_cnt)
                    self.sem_max = max(self.sem_max, 16 * o.dma_cnt)
                elif o.signal:
                    k = (e, o.phase)
                    if k not in sems:
                        sems[k] = st.enter_context(nc.semaphore("s_%s%d" % k))
                    c[k] = c.get(k, 0) + 1
                    o.ev = (sems[k], c[k])
                    self.sem_max = max(self.sem_max, c[k])
        assert self.sem_max < 4000, self.sem_max
        nwait = 0
        for e in ENGS:
            seen = {}
            for o in self.ops[e]:
                w = {}
                for d, cnt in o.deps:
                    if d.is_dma:
                        s, v = dsems[d.dma_key], 16 * cnt
                    else:
                        s, v = d.ev
                    k = id(s)
                    if seen.get(k, 0) >= v:
                        continue
                    if k not in w or w[k][1] < v:
                        w[k] = (s, v)
                for k, (s, v) in w.items():
                    seen[k] = v
                o.waits = list(w.values())
                nwait += len(o.waits)
        self.nwait = nwait
        block = st.enter_context(nc.Block())

        def run(eng_name):
            def body(e):
                for o in self.ops[eng_name]:
                    for s, v in o.waits:
                        e.wait_ge(s, v)
                    if o.fn is None:
                        continue
                    ins = o.fn(e)
                    if o.signal:
                        ins.then_inc(o.ev[0], 16 if o.is_dma else 1)
            return body

        block.tensor(run("pe"))
        block.scalar(run("act"))
        block.vector(run("dve"))
        block.gpsimd(run("pool"))
        block.sync(run("sp"))


class Ctx:
    pass


def setup_common(P, C):
    nc = P.nc
    C.identf = P.tile([128, 128], F32)
    C.ident = P.tile([128, 128], BF16)
    C.onesf = P.tile([128, 128], F32)
    C.r_const = Res("const")
    P.op("dve", lambda e: e.memset(C.identf, 0.0), writes=[C.r_const])
    P.op("pool", lambda e: e.affine_select(out=C.identf, in_=C.identf, pattern=[[-1, 128]],
                                           compare_op=ALU.not_equal, fill=1.0, base=0,
                                           channel_multiplier=1),
         reads=[C.r_const], writes=[C.r_const])
    P.op("dve", lambda e: e.tensor_copy(out=C.ident, in_=C.identf), reads=[C.r_const], writes=[C.r_const])
    P.op("dve", lambda e: e.memset(C.onesf, 1.0), reads=[C.r_const], writes=[C.r_const])
    C.banks = [P.ps("pb%d" % i, [128, 512], F32) for i in range(8)]
    C.rb = [Res("pb%d" % i) for i in range(8)]
    P.keep = P.off


def bank_bf(C, i):
    return C.banks[i][:].bitcast(BF16)


def load_bcast_vec(P, dst, src_vec, res, key):
    n = dst.shape[1]
    P.op("sp", lambda e: e.dma_start(out=dst, in_=src_vec.rearrange("(o n) -> o n", o=1).broadcast_to([128, n])),
         writes=[res], dma_key=key)


def rmsnorm_tiles(P, C, xts, r_xts, g_bc, r_g, hT, r_hT, tp_banks, scr, r_scr):
    nt = len(xts)
    ss = scr["ss"]
    for i in range(nt):
        P.op("act", lambda e, i=i: e.activation(out=scr["junk"], in_=xts[i], func=AF.Square,
                                                accum_out=ss[:, i:i + 1]),
             reads=[r_xts[i]], writes=[r_scr["junk"], r_scr["ss"]])
    P.op("act", lambda e: e.activation(out=ss[:, 0:nt], in_=ss[:, 0:nt], func=AF.Sqrt, scale=1.0 / D, bias=EPS),
         reads=[r_scr["ss"]], writes=[r_scr["ss"]])
    P.op("dve", lambda e: e.reciprocal(out=ss[:, 0:nt], in_=ss[:, 0:nt]), reads=[r_scr["ss"]], writes=[r_scr["ss"]])
    for i in range(nt):
        hb = scr["hb"][i % 2]
        r_hb = r_scr["hb"][i % 2]
        P.op("dve", lambda e, i=i, hb=hb: e.scalar_tensor_tensor(out=hb, in0=xts[i], scalar=ss[:, i:i + 1], in1=g_bc,
                                                               op0=ALU.mult, op1=ALU.mult),
             reads=[r_xts[i], r_scr["ss"], r_g], writes=[r_hb])
        b = tp_banks[i % 2]
        tb = bank_bf(C, b)
        for k in range(8):
            P.op("pe", lambda e, k=k, hb=hb, tb=tb: e.transpose(out=tb[:, k * 128:(k + 1) * 128],
                                                               in_=hb[:, k * 128:(k + 1) * 128], identity=C.ident),
                 reads=[r_hb, C.r_const], writes=[C.rb[b]])
        P.op("act", lambda e, hT=hT, i=i, tb=tb: e.copy(out=hT[:, :, i * 128:(i + 1) * 128],
                                                in_=tb.rearrange("p (k n) -> p k n", k=8)),
             reads=[C.rb[b]], writes=[r_hT])


def rmsnorm_a(P, C, xts, r_xts, g_bc, r_g, scr, r_scr, hb4, r_hb4):
    ss = scr["ss"]
    for i in range(4):
        P.op("act", lambda e, i=i: e.activation(out=scr["junk"], in_=xts[i], func=AF.Square, accum_out=ss[:, i:i + 1]),
             reads=[r_xts[i]], writes=[r_scr["junk"], r_scr["ss"]])
    P.op("act", lambda e: e.activation(out=ss[:, 0:4], in_=ss[:, 0:4], func=AF.Sqrt, scale=1.0 / D, bias=EPS),
         reads=[r_scr["ss"]], writes=[r_scr["ss"]])
    P.op("dve", lambda e: e.reciprocal(out=ss[:, 0:4], in_=ss[:, 0:4]), reads=[r_scr["ss"]], writes=[r_scr["ss"]])
    for i in range(4):
        P.op("dve", lambda e, i=i: e.scalar_tensor_tensor(out=hb4[i], in0=xts[i], scalar=ss[:, i:i + 1], in1=g_bc,
                                                          op0=ALU.mult, op1=ALU.mult),
             reads=[r_xts[i], r_scr["ss"], r_g], writes=[r_hb4[i]])


def rmsnorm_b(P, C, hb4, r_hb4, hT, r_hT, tp_banks):
    for i in range(4):
        b = tp_banks[i % 2]
        tb = bank_bf(C, b)
        for k in range(8):
            P.op("pe", lambda e, k=k, i=i, tb=tb: e.transpose(out=tb[:, k * 128:(k + 1) * 128],
                                                             in_=hb4[i][:, k * 128:(k + 1) * 128], identity=C.ident),
                 reads=[r_hb4[i], C.r_const], writes=[C.rb[b]])
        P.op("act", lambda e, hT=hT, i=i, tb=tb: e.copy(out=hT[:, :, i * 128:(i + 1) * 128],
                                                        in_=tb.rearrange("p (k n) -> p k n", k=8)),
             reads=[C.rb[b]], writes=[r_hT])


def load_x_tiles(P, xin, r_xin, xt, r_xt, blk, keyfmt):
    t0 = blk * NB
    for i in range(4):
        P.op("sp", lambda e, i=i, t0=t0: e.dma_start(out=xt[i], in_=xin[t0 + i * 128:t0 + (i + 1) * 128, :]),
             reads=[r_xin[blk * 4 + i]], writes=[r_xt[i]], dma_key=keyfmt % i)


def phase_ffn(P, C, WS, layer, xin, r_xin, xout, r_xout, g_norm, g_final=None, nblk=NBLK, tag="f", hook=None):
    P.reset()
    g_bc = P.tile([128, D], F32)
    r_g = Res("g")
    load_bcast_vec(P, g_bc, g_norm, r_g, tag + "g")
    if g_final is not None:
        gf_bc = P.tile([128, D], F32)
        load_bcast_vec(P, gf_bc, g_final, r_g, tag + "g")
    xt = [[P.tile([128, D], F32) for _ in range(4)] for _ in range(2)]
    r_xt = [[Res("xt") for _ in range(4)] for _ in range(2)]
    hT = [P.tile([128, 8, NB], BF16) for _ in range(2)]
    r_hT = [Res("hT") for _ in range(2)]
    aT = P.tile([128, 22, NB], BF16)
    r_aT = [Res("aT%d" % f) for f in range(22)]
    NW = 4
    wgu = [P.tile([128, 8, 512], BF16) for _ in range(NW)]
    r_wgu = [Res("wgu") for _ in range(NW)]
    wd = [P.tile([128, 22, 512], BF16) for _ in range(2)]
    r_wd = [Res("wd") for _ in range(2)]
    sg = [P.tile([128, 512], F32) for _ in range(2)]
    r_sg = [Res("sg") for _ in range(2)]
    scr = {"ss": P.tile([128, 8], F32), "junk": P.tile([128, D], BF16),
           "hb": [P.tile([128, D], BF16) for _ in range(2)]}
    r_scr = {"ss": Res("ss"), "junk": Res("junk"), "hb": [Res("hb0"), Res("hb1")]}
    ss2 = P.tile([128, 8], F32)
    r_ss2 = Res("ss2")

    panels = [(c0, min(512, FF - c0)) for c0 in range(0, FF, 512)]
    wcnt = 0
    gcnt = 0
    dcnt = 0
    wdcnt = 0
    load_x_tiles(P, xin, r_xin, xt[0], r_xt[0], 0, tag + "x0%d")
    rmsnorm_tiles(P, C, xt[0], r_xt[0], g_bc, r_g, hT[0], r_hT[0], (0, 1), scr, r_scr)
    for blk in range(nblk):
        sl = blk % 2
        t0 = blk * NB
        if hook is not None:
            hook(blk)
        for pi, (c0, cw) in enumerate(panels):
            slots = []
            for nm in ("fg", "fu"):
                ws = wcnt % NW
                wcnt += 1
                WS.load(P, "%s%d_%d" % (nm, layer, pi), wgu[ws], r_wgu[ws], "%sw%d" % (tag, ws), eng=FFN_WQ)
                slots.append(ws)
            for cc in range(cw // 128):
                f = c0 // 128 + cc
                pg = 2 + 2 * (gcnt % 2)
                pu = pg + 1
                sgi = gcnt % 2
                gcnt += 1
                for (pb, ws) in ((pg, slots[0]), (pu, slots[1])):
                    for k in range(8):
                        P.op("pe", lambda e, hT=hT, pb=pb, ws=ws, k=k, cc=cc, sl=sl: e.matmul(
                            C.banks[pb][:], lhsT=wgu[ws][:, k, cc * 128:(cc + 1) * 128], rhs=hT[sl][:, k, :],
                            start=(k == 0), stop=(k == 7)),
                            reads=[r_wgu[ws], r_hT[sl]], writes=[C.rb[pb]])
                P.op("act", lambda e, pg=pg, sgi=sgi: e.activation(out=sg[sgi], in_=C.banks[pg][:], func=AF.Silu),
                     reads=[C.rb[pg]], writes=[r_sg[sgi]])
                P.op("dve", lambda e, hT=hT, pu=pu, sgi=sgi, f=f: e.tensor_tensor(out=aT[:, f, :], in0=sg[sgi], in1=C.banks[pu][:],
                                                                         op=ALU.mult),
                     reads=[C.rb[pu], r_sg[sgi]], writes=[r_aT[f]])
        if blk + 1 < nblk:
            nsl = (blk + 1) % 2
            load_x_tiles(P, xin, r_xin, xt[nsl], r_xt[nsl], blk + 1, tag + "x" + str(nsl) + "%d")
            rmsnorm_tiles(P, C, xt[nsl], r_xt[nsl], g_bc, r_g, hT[nsl], r_hT[nsl], (0, 1), scr, r_scr)
        for half in range(2):
            ws = wdcnt % 2
            wdcnt += 1
            WS.load(P, "fd%d_%d" % (layer, half), wd[ws], r_wd[ws], "%sd%d" % (tag, ws), eng=FFN_WQ)
            for i in range(4):
                pb = 6 + (dcnt % 2)
                dcnt += 1
                for f in range(22):
                    P.op("pe", lambda e, pb=pb, f=f, i=i, ws=ws: e.matmul(
                        C.banks[pb][:], lhsT=aT[:, f, i * 128:(i + 1) * 128], rhs=wd[ws][:, f, :],
                        start=(f == 0), stop=(f == 21)),
                        reads=[r_aT[f], r_wd[ws]], writes=[C.rb[pb]])
                xs_ = xt[sl][i][:, half * 512:(half + 1) * 512]
                P.op("dve", lambda e, pb=pb, xs_=xs_: e.tensor_tensor(out=xs_, in0=xs_, in1=C.banks[pb][:], op=ALU.add),
                     reads=[C.rb[pb], r_xt[sl][i]], writes=[r_xt[sl][i]])
                if half == 1:
                    if g_final is not None:
                        P.op("act", lambda e, i=i, sl=sl: e.activation(out=scr["junk"], in_=xt[sl][i], func=AF.Square,
                                                                       accum_out=ss2[:, i:i + 1]),
                             reads=[r_xt[sl][i]], writes=[r_scr["junk"], r_ss2])
                        P.op("act", lambda e, i=i: e.activation(out=ss2[:, i:i + 1], in_=ss2[:, i:i + 1], func=AF.Sqrt,
                                                                scale=1.0 / D, bias=EPS),
                             reads=[r_ss2], writes=[r_ss2])
                        P.op("dve", lambda e, i=i: e.reciprocal(out=ss2[:, i:i + 1], in_=ss2[:, i:i + 1]),
                             reads=[r_ss2], writes=[r_ss2])
                        P.op("dve", lambda e, i=i, sl=sl: e.scalar_tensor_tensor(
                            out=xt[sl][i], in0=xt[sl][i], scalar=ss2[:, i:i + 1], in1=gf_bc, op0=ALU.mult, op1=ALU.mult),
                            reads=[r_ss2, r_g, r_xt[sl][i]], writes=[r_xt[sl][i]])
                    P.op("sp", lambda e, i=i, sl=sl, t0=t0: e.dma_start(out=xout[t0 + i * 128:t0 + (i + 1) * 128, :],
                                                                        in_=xt[sl][i]),
                         reads=[r_xt[sl][i]], writes=[r_xout[blk * 4 + i]], dma_key=tag + "o")


WNAMES_FFN = ["ffn_norm", "ffn_w_gate", "ffn_w_up", "ffn_w_down", "final_norm"]


def build_ffn_only(layer=0, final=False, nblk=NBLK):
    nc = bass.Bass("TRN2", target_bir_lowering=False)
    xin = nc.dram_tensor("x", [S, D], F32, kind="ExternalInput").ap()
    g = nc.dram_tensor("ffn_norm", [2, D], F32, kind="ExternalInput").ap()
    wg = nc.dram_tensor("ffn_w_gate", [2, D, FF], F32, kind="ExternalInput").ap()
    wu = nc.dram_tensor("ffn_w_up", [2, D, FF], F32, kind="ExternalInput").ap()
    wdn = nc.dram_tensor("ffn_w_down", [2, FF, D], F32, kind="ExternalInput").ap()
    gf = nc.dram_tensor("final_norm", [D], F32, kind="ExternalInput").ap()
    out = nc.dram_tensor("out", [S, D], F32, kind="ExternalOutput").ap()
    P = Prog(nc, ARENA)
    C = Ctx()
    setup_common(P, C)
    r_in = [Res("xin") for _ in range(32)]
    r_out = [Res("xout") for _ in range(32)]
    WS = WStore(nc)
    define_panels(WS, Wf={"ffn_w_gate": wg, "ffn_w_up": wu, "ffn_w_down": wdn})
    WS.emit_group(P, "C" if layer == 0 else "E")
    phase_ffn(P, C, WS, layer, xin, r_in, out, r_out, g[layer], g_final=(gf if final else None), nblk=nblk)
    P.op("sp", None, reads=r_out)
    P.emit()
    return nc, P


def load_cols(P, dst, vec, res, key, nchunk):
    P.op("sp", lambda e: e.dma_start(out=dst, in_=vec.rearrange("(c p) -> p c", p=128), allow_slow_non_contiguous=True),
         writes=[res], dma_key=key)


def stream_w(P, slot_ap, r_slot, key, W, c0, cw, kc=8):
    P.op("pool", lambda e: e.dma_start(out=slot_ap[:, 0:kc, 0:cw], in_=W[:, c0:c0 + cw].rearrange("(k p) n -> p k n", p=128)),
         writes=r_slot if isinstance(r_slot, list) else [r_slot], dma_key=key)


class WStore:
    def __init__(self, nc):
        self.nc = nc
        self.specs = {}
        self.groups = {}

    def add(self, group, key, W, r0, nrows, c0, cw):
        kc = nrows // 128
        t = self.nc.dram_tensor("wb_" + key, [128, kc * cw], BF16, kind="Internal").ap()
        self.specs[key] = (t, Res(key), kc, cw, W, r0, nrows, c0, group)
        self.groups.setdefault(group, []).append(key)

    def emit_group(self, P, group):
        for key in self.groups.get(group, []):
            t, res, kc, cw, W, r0, nrows, c0, _ = self.specs[key]
            P.op("pool", lambda e, t=t, kc=kc, cw=cw, W=W, r0=r0, nrows=nrows, c0=c0: e.dma_start(
                out=t.rearrange("p (k n) -> p k n", k=kc), in_=W[r0:r0 + nrows, c0:c0 + cw].rearrange("(k p) n -> p k n", p=128)),
                writes=[res], dma_key="cv" + group)

    def load(self, P, key, slot_ap, r_slot, dma_key, eng="pool"):
        t, res, kc, cw, _, _, _, _, _ = self.specs[key]
        P.op(eng, lambda e, t=t, kc=kc, cw=cw: e.dma_start(out=slot_ap[:, 0:kc, 0:cw], in_=t.rearrange("p (k n) -> p k n", k=kc)),
             reads=[res], writes=r_slot if isinstance(r_slot, list) else [r_slot], dma_key=dma_key)


def define_panels(WS, We=None, Wo=None, Wf=None):
    if We is not None:
        for i in range(2):
            WS.add("A", "ea%d" % i, We["ev_w_in"], 0, 1024, 3088 + i * 512, 512)
            WS.add("A", "eg%d" % i, We["ev_w_in"], 0, 1024, 4112 + i * 512, 512)
        for i in range(2):
            WS.add("B", "ez%d" % i, We["ev_w_in"], 0, 1024, i * 512, 512)
        for i in range(4):
            WS.add("B", "ex%d" % i, We["ev_w_in"], 0, 1024, 1024 + i * 512, 512)
        for half in range(2):
            for part in range(2):
                WS.add("B", "eo%d%d" % (half, part), We["ev_w_out"], part * 1024, 1024, half * 512, 512)
    if Wf is not None:
        for l, grp in ((0, "C"), (1, "E")):
            for i, c0 in enumerate(range(0, FF, 512)):
                cw = min(512, FF - c0)
                WS.add(grp, "fg%d_%d" % (l, i), Wf["ffn_w_gate"][l], 0, 1024, c0, cw)
                WS.add(grp, "fu%d_%d" % (l, i), Wf["ffn_w_up"][l], 0, 1024, c0, cw)
            for half in range(2):
                WS.add(grp, "fd%d_%d" % (l, half), Wf["ffn_w_down"][l], 0, FF, half * 512, 512)
    if Wo is not None:
        WS.add("D", "ou", Wo["od_w_in"], 0, 1024, 0, 512)
        for nm, c0 in (("oq", 512), ("ok", 1536), ("ov", 2560), ("og", 3584)):
            for i in range(2):
                WS.add("D", "%s%d" % (nm, i), Wo["od_w_in"], 0, 1024, c0 + i * 512, 512)
        for half in range(2):
            WS.add("D", "oo%d0" % half, Wo["od_w_out"], 0, 1024, half * 512, 512)
            WS.add("D", "oo%d1" % half, Wo["od_w_out"], 1024, 512, half * 512, 512)


def phase_e1(P, C, WS, xin, r_xin, W, ybT, r_yb, nblk=NBLK, hook=None):
    P.reset()
    A0 = 3088
    G0 = 4112
    g_bc = P.tile([128, D], F32)
    r_g = Res("g")
    load_bcast_vec(P, g_bc, W["ev_norm"], r_g, "e1g")
    wnat = P.tile([34, D], F32)
    r_wnat = Res("wnat")
    r_colp = Res("colp")
    P.op("sp", lambda e: e.dma_start(out=wnat[0:31, :], in_=W["ev_cf_conv_w"]), writes=[r_wnat], dma_key="e1g")
    for i, nm in enumerate(["ev_cf_conv_b", "ev_cf_ln_g", "ev_cf_ln_b"]):
        P.op("sp", lambda e, i=i, nm=nm: e.dma_start(out=wnat[31 + i:32 + i, :], in_=W[nm].rearrange("(o n) -> o n", o=1)),
             reads=[r_wnat], writes=[r_wnat], dma_key="e1g")
    wall = P.tile([128, 8, 34], F32)
    wcv = wall[:, :, 0:31]
    colp = wall[:, :, 31:34].rearrange("p c j -> p j c")
    r_wcv = Res("wcv")
    for c in range(8):
        P.op("pe", lambda e, c=c: e.transpose(out=C.banks[0][:, c * 40:c * 40 + 34], in_=wnat[:, c * 128:(c + 1) * 128],
                                              identity=C.identf[0:34, 0:34]),
             reads=[r_wnat, C.r_const], writes=[C.rb[0]])
    P.op("act", lambda e: e.copy(out=wall, in_=C.banks[0][:, 0:320].rearrange("p (c j) -> p c j", c=8)[:, :, 0:34]),
         reads=[C.rb[0]], writes=[r_wcv, r_colp])
    diag = P.tile([128, 8, 31, 128], BF16)
    r_diag = {}
    n = 0
    for c in range(8):
        for j in range(0, 31):
            n += 1
            if n % 3 == 0:
                P.op("act", lambda e, c=c, j=j: e.activation(out=diag[:, c, j, :], in_=C.identf, func=AF.Identity,
                                                             scale=wcv[:, c, j:j + 1]),
                     reads=[r_wcv, C.r_const], writes=[r_diag.setdefault((c, j), Res("diag"))])
            else:
                P.op("dve", lambda e, c=c, j=j: e.tensor_scalar(out=diag[:, c, j, :], in0=C.identf, scalar1=wcv[:, c, j:j + 1],
                                                                scalar2=None, op0=ALU.mult),
                     reads=[r_wcv, C.r_const], writes=[r_diag.setdefault((c, j), Res("diag"))])
    xt = [P.tile([128, D], F32) for _ in range(4)]
    r_xt = [Res("xt") for _ in range(4)]
    hT2 = [P.tile([128, 8, NB], BF16) for _ in range(2)]
    r_hT2 = [Res("hT") for _ in range(2)]
    NW = 4
    wsl = [P.tile([128, 8, 512], BF16) for _ in range(NW)]
    r_wsl = [Res("w") for _ in range(NW)]
    gluT = P.tile([128, 8, 30 + NB], BF16)
    r_glu = [Res("glu") for _ in range(8)]
    hcv = P.tile([128, 8, NB], F32)
    r_hcv = [Res("hcv") for _ in range(8)]
    ybo = P.tile([128, 8, NB], BF16)
    r_ybo = Res("ybo")
    sgt = [P.tile([128, NB], F32) for _ in range(2)]
    r_sgt = [Res("sgt") for _ in range(2)]
    NDT = 4
    sq = [P.tile([128, NB], BF16) for _ in range(2)]
    r_sq = [Res("sq") for _ in range(2)]
    hcb = [P.tile([128, NB], BF16) for _ in range(2)]
    r_hcb = [Res("hcb") for _ in range(2)]
    dacc = [P.tile([128, NB], F32) for _ in range(2)]
    r_dacc = [Res("dacc") for _ in range(2)]
    onesb = P.tile([128, 128], BF16)
    P.op("dve", lambda e: e.memset(onesb, 1.0), writes=[r_colp])
    mean = P.tile([128, NB], F32)
    rstd = P.tile([128, NB], F32)
    msq = P.tile([128, NB], F32)
    r_st = Res("stats")
    tmp = [P.tile([128, NB], F32) for _ in range(2)]
    r_tmp = [Res("tmp") for _ in range(2)]
    scr = {"ss": P.tile([128, 8], F32), "junk": P.tile([128, D], BF16),
           "hb": [P.tile([128, D], BF16) for _ in range(2)]}
    r_scr = {"ss": Res("ss"), "junk": Res("junk"), "hb": [Res("hb0"), Res("hb1")]}
    P.op("dve", lambda e: e.memset(gluT[:, :, 0:30], 0.0), writes=r_glu)
    wcnt = 0
    ybT_v = ybT.rearrange("c p t -> p c t")
    load_x_tiles(P, xin, r_xin, xt, r_xt, 0, "e1x%d")
    rmsnorm_tiles(P, C, xt, r_xt, g_bc, r_g, hT2[0], r_hT2[0], (0, 1), scr, r_scr)
    for blk in range(nblk):
        t0 = blk * NB
        if hook is not None:
            hook(blk)
        hT, r_hT = hT2[blk % 2], r_hT2[blk % 2]
        if blk + 1 < nblk:
            load_x_tiles(P, xin, r_xin, xt, r_xt, blk + 1, "e1x%d")
        slots = {}

        def proj(c):
            nonlocal wcnt
            pn, cc = c // 4, c % 4
            if cc == 0:
                for nm, base in (("a", A0), ("g", G0)):
                    ws = wcnt % NW
                    wcnt += 1
                    WS.load(P, "e%s%d" % (nm, pn), wsl[ws], r_wsl[ws], "e1w%d" % ws)
                    slots[nm] = ws
            for (pb, ws) in ((2, slots["a"]), (3, slots["g"])):
                for k in range(8):
                    P.op("pe", lambda e, hT=hT, pb=pb, ws=ws, k=k, cc=cc: e.matmul(
                        C.banks[pb][:], lhsT=wsl[ws][:, k, cc * 128:(cc + 1) * 128], rhs=hT[:, k, :],
                        start=(k == 0), stop=(k == 7)), reads=[r_wsl[ws], r_hT], writes=[C.rb[pb]])
            si = c % 2
            P.op("act", lambda e, si=si: e.activation(out=sgt[si], in_=C.banks[3][:], func=AF.Sigmoid),
                 reads=[C.rb[3]], writes=[r_sgt[si]])
            P.op("dve", lambda e, si=si, c=c: e.tensor_tensor(out=gluT[:, c, 30:30 + NB], in0=sgt[si], in1=C.banks[2][:],
                                                             op=ALU.mult),
                 reads=[C.rb[2], r_sgt[si]], writes=[r_glu[c]])

        def conv(c):
            pb = 4 + (c % 2)
            si = c % 2
            dtaps = [2 * k for k in range(NDT)]
            ptaps = [j for j in range(31) if j not in dtaps]
            for n_, j in enumerate(dtaps):
                if n_ == 0:
                    P.op("dve", lambda e, si=si, c=c, j=j: e.tensor_scalar(out=dacc[si], in0=gluT[:, c, j:j + NB], scalar1=wcv[:, c, j:j + 1],
                                                                          scalar2=None, op0=ALU.mult),
                         reads=[r_glu[c], r_wcv], writes=[r_dacc[si]])
                else:
                    P.op("dve", lambda e, si=si, c=c, j=j: e.scalar_tensor_tensor(
                        out=dacc[si], in0=gluT[:, c, j:j + NB], scalar=wcv[:, c, j:j + 1], in1=dacc[si], op0=ALU.mult, op1=ALU.add),
                        reads=[r_glu[c], r_wcv, r_dacc[si]], writes=[r_dacc[si]])
            for n_, j in enumerate(ptaps):
                P.op("pe", lambda e, pb=pb, c=c, j=j, n_=n_: e.matmul(C.banks[pb][:], lhsT=diag[:, c, j, :],
                                                                     rhs=gluT[:, c, j:j + NB], start=(n_ == 0), stop=(n_ == len(ptaps) - 1)),
                     reads=[r_diag[(c, j)], r_glu[c]], writes=[C.rb[pb]])
            P.op("act", lambda e, pb=pb, c=c: e.activation(out=hcv[:, c, :], in_=C.banks[pb][:], func=AF.Identity,
                                                          bias=colp[:, 0, c:c + 1], scale=1.0),
                 reads=[C.rb[pb], r_colp], writes=[r_hcv[c]])
            if NDT > 0:
                P.op("dve", lambda e, si=si, c=c: e.tensor_tensor(out=hcv[:, c, :], in0=hcv[:, c, :], in1=dacc[si], op=ALU.add),
                     reads=[r_hcv[c], r_dacc[si]], writes=[r_hcv[c]])
            P.op("act", lambda e, si=si, c=c: e.activation(out=sq[si], in_=hcv[:, c, :], func=AF.Square),
                 reads=[r_hcv[c]], writes=[r_sq[si]])
            P.op("act", lambda e, si=si, c=c: e.copy(out=hcb[si], in_=hcv[:, c, :]), reads=[r_hcv[c]], writes=[r_hcb[si]])
            P.op("dve", lambda e, c=c: e.tensor_copy(out=gluT[:, c, 0:30], in_=gluT[:, c, NB:NB + 30]),
                 reads=[r_glu[c]], writes=[r_glu[c]])

        def stats(c):
            si = c % 2
            P.op("pe", lambda e, c=c, si=si: e.matmul(C.banks[6][:], lhsT=onesb, rhs=hcb[si], start=(c == 0), stop=(c == 7)),
                 reads=[r_hcb[si], r_colp], writes=[C.rb[6]])
            P.op("pe", lambda e, c=c, si=si: e.matmul(C.banks[7][:], lhsT=onesb, rhs=sq[si], start=(c == 0), stop=(c == 7)),
                 reads=[r_sq[si], r_colp], writes=[C.rb[7]])

        for step in range(10):
            if step < 8:
                proj(step)
            if 1 <= step < 9:
                conv(step - 1)
            if 2 <= step < 10:
                stats(step - 2)
        if blk + 1 < nblk:
            rmsnorm_tiles(P, C, xt, r_xt, g_bc, r_g, hT2[(blk + 1) % 2], r_hT2[(blk + 1) % 2], (0, 1), scr, r_scr)
        P.op("act", lambda e: e.mul(out=mean, in_=C.banks[6][:], mul=1.0 / D), reads=[C.rb[6]], writes=[r_st])
        P.op("dve", lambda e: e.tensor_tensor(out=msq, in0=mean, in1=mean, op=ALU.mult), reads=[r_st], writes=[r_st])
        P.op("dve", lambda e: e.scalar_tensor_tensor(out=rstd, in0=C.banks[7][:], scalar=1.0 / D, in1=msq,
                                                     op0=ALU.mult, op1=ALU.subtract), reads=[C.rb[7], r_st], writes=[r_st])
        P.op("act", lambda e: e.activation(out=rstd, in_=rstd, func=AF.Sqrt, bias=EPS, scale=1.0), reads=[r_st], writes=[r_st])
        P.op("dve", lambda e: e.reciprocal(out=rstd, in_=rstd), reads=[r_st], writes=[r_st])
        for c in range(8):
            ti = c % 2
            P.op("dve", lambda e, c=c, ti=ti: e.tensor_tensor(out=tmp[ti], in0=hcv[:, c, :], in1=mean, op=ALU.subtract),
                 reads=[r_hcv[c], r_st], writes=[r_tmp[ti]])
            P.op("dve", lambda e, ti=ti: e.tensor_tensor(out=tmp[ti], in0=tmp[ti], in1=rstd, op=ALU.mult),
                 reads=[r_st, r_tmp[ti]], writes=[r_tmp[ti]])
            P.op("act", lambda e, c=c, ti=ti: e.activation(out=ybo[:, c, :], in_=tmp[ti], func=AF.Silu,
                                                          scale=colp[:, 1, c:c + 1], bias=colp[:, 2, c:c + 1]),
                 reads=[r_tmp[ti], r_colp], writes=[r_ybo])
        P.op("sp", lambda e, t0=t0: e.dma_start(out=ybT_v[:, :, t0:t0 + NB], in_=ybo), reads=[r_ybo], writes=[r_yb[blk]],
             dma_key="e1o")


EVEN_NAMES = ["ev_norm", "ev_w_in", "ev_conv_w", "ev_conv_b", "ev_dt_bias", "ev_a_log", "ev_d", "ev_ssd_norm",
              "ev_cf_conv_w", "ev_cf_conv_b", "ev_cf_ln_g", "ev_cf_ln_b", "ev_w_out"]
EVEN_SHAPES = {"ev_norm": [1, D], "ev_w_in": [1, D, 5136], "ev_conv_w": [1, 4, 2048], "ev_conv_b": [1, 2048],
               "ev_dt_bias": [1, 16], "ev_a_log": [1, 16], "ev_d": [1, 16], "ev_ssd_norm": [1, D],
               "ev_cf_conv_w": [1, 31, D], "ev_cf_conv_b": [1, D], "ev_cf_ln_g": [1, D], "ev_cf_ln_b": [1, D],
               "ev_w_out": [1, 2048, D]}


def declare(nc, shapes):
    return {k: nc.dram_tensor(k, v, F32, kind="ExternalInput").ap() for k, v in shapes.items()}


def build_e1_only(nblk=NBLK):
    nc = bass.Bass("TRN2", target_bir_lowering=False)
    xin = nc.dram_tensor("x", [S, D], F32, kind="ExternalInput").ap()
    Wd = declare(nc, EVEN_SHAPES)
    W = {k: v[0] for k, v in Wd.items()}
    ybT = nc.dram_tensor("ybT", [8, 128, S], BF16, kind="ExternalOutput").ap()
    P = Prog(nc, ARENA)
    C = Ctx()
    setup_common(P, C)
    r_in = [Res("xin") for _ in range(32)]
    r_yb = [Res("yb") for _ in range(NBLK)]
    WS = WStore(nc)
    define_panels(WS, We=W)
    WS.emit_group(P, "A")
    phase_e1(P, C, WS, xin, r_in, W, ybT, r_yb, nblk=nblk)
    P.op("sp", None, reads=r_yb)
    P.emit()
    return nc, P


def bc_mid(ap2, n):
    return ap2.unsqueeze(2).broadcast_to([ap2.shape[0], ap2.shape[1], n])


def phase_e2(P, C, WS, xin, r_xin, W, ybT, r_yb, xout, r_xout, nblk=NBLK, dbg=None, hook=None, after_first_loads=None):
    P.reset()
    XC0 = 1024
    DT0 = 3072
    g_bc = P.tile([128, D], F32)
    ng_bc = P.tile([128, D], F32)
    r_g = Res("g")
    load_bcast_vec(P, g_bc, W["ev_norm"], r_g, "e2g")
    load_bcast_vec(P, ng_bc, W["ev_ssd_norm"], r_g, "e2g")
    dtb_bc = P.tile([128, 16], F32)
    ahead_bc = P.tile([128, 16], F32)
    load_bcast_vec(P, dtb_bc, W["ev_dt_bias"], r_g, "e2g")
    load_bcast_vec(P, ahead_bc, W["ev_a_log"], r_g, "e2g")
    P.op("act", lambda e: e.activation(out=ahead_bc, in_=ahead_bc, func=AF.Exp), reads=[r_g], writes=[r_g])
    P.op("dve", lambda e: e.tensor_scalar(out=ahead_bc, in0=ahead_bc, scalar1=-1.0, scalar2=None, op0=ALU.mult),
         reads=[r_g], writes=[r_g])
    Dcol = P.tile([128, 8], F32)
    Dbc = P.tile([128, 16], F32)
    load_bcast_vec(P, Dbc, W["ev_d"], r_g, "e2g")
    for hh in range(2):
        P.op("dve", lambda e, hh=hh: e.tensor_copy(out=Dcol[hh * 64:(hh + 1) * 64, :],
                                                   in_=Dbc[hh * 64:(hh + 1) * 64, :].rearrange("p (c two) -> p two c", two=2)[:, hh, :]),
             reads=[r_g], writes=[r_g])
    diagD = P.tile([128, 8, 128], BF16)
    for c in range(8):
        P.op("dve", lambda e, c=c: e.tensor_scalar(out=diagD[:, c, :], in0=C.identf, scalar1=Dcol[:, c:c + 1], scalar2=None,
                                                   op0=ALU.mult), reads=[r_g, C.r_const], writes=[r_g])
    Tmask = P.tile([128, 128], F32)
    P.op("dve", lambda e: e.memset(Tmask, 1.0), writes=[r_g])
    P.op("pool", lambda e: e.affine_select(out=Tmask, in_=Tmask, pattern=[[1, 128]], compare_op=ALU.is_ge, fill=0.0,
                                           base=0, channel_multiplier=-1), reads=[r_g], writes=[r_g])
    Umask = P.tile([128, 128], F32)
    P.op("dve", lambda e: e.memset(Umask, 1.0), reads=[r_g], writes=[r_g])
    P.op("pool", lambda e: e.affine_select(out=Umask, in_=Umask, pattern=[[-1, 128]], compare_op=ALU.is_gt, fill=0.0,
                                           base=0, channel_multiplier=1), reads=[r_g], writes=[r_g])
    wnat = P.tile([5, 2048], F32)
    P.op("sp", lambda e: e.dma_start(out=wnat[0:4, :], in_=W["ev_conv_w"]), writes=[r_g], dma_key="e2g")
    P.op("sp", lambda e: e.dma_start(out=wnat[4:5, :], in_=W["ev_conv_b"].rearrange("(o n) -> o n", o=1)), writes=[r_g], dma_key="e2g")
    wc5 = P.tile([128, 16, 5], F32)
    wc4 = wc5[:, :, 0:4]
    cb4 = wc5[:, :, 4]
    for c in range(16):
        P.op("pe", lambda e, c=c: e.transpose(out=C.banks[0][:, c * 5:c * 5 + 5], in_=wnat[:, c * 128:(c + 1) * 128],
                                              identity=C.identf[0:5, 0:5]), reads=[r_g, C.r_const], writes=[C.rb[0]])
    P.op("act", lambda e: e.copy(out=wc5, in_=C.banks[0][:, 0:80].rearrange("p (c j) -> p c j", c=16)),
         reads=[C.rb[0]], writes=[r_g])
    wdt = P.tile([128, 8, 16], BF16)
    P.op("pool", lambda e: e.dma_start(out=wdt, in_=W["ev_w_in"][:, DT0:DT0 + 16].rearrange("(k p) n -> p k n", p=128)),
         writes=[r_g], dma_key="e2g")
    halo = P.tile([128, 16, 3], F32)
    r_halo = [Res("halo") for _ in range(16)]
    P.op("dve", lambda e: e.memset(halo, 0.0), writes=r_halo)
    prev = P.tile([128, D], F32)
    prev_bf = P.tile([128, D], BF16)
    r_prev = Res("prev")
    r_prevbf = Res("prevbf")
    P.op("dve", lambda e: e.memset(prev, 0.0), writes=[r_prev])
    P.op("dve", lambda e: e.memset(prev_bf, 0.0), writes=[r_prevbf])

    xt = [P.tile([128, D], F32) for _ in range(4)]
    r_xt = [Res("xt") for _ in range(4)]
    hT2 = [P.tile([128, 8, NB], BF16) for _ in range(2)]
    r_hT2 = [Res("hT") for _ in range(2)]
    NXR = 4
    xres = [P.tile([128, 512], F32) for _ in range(NXR)]
    r_xres = [Res("xres") for _ in range(NXR)]
    xrc = 0
    NW = 4
    wsl = [P.tile([128, 8, 512], BF16) for _ in range(NW)]
    r_wsl = [Res("w") for _ in range(NW)]
    sz = [P.tile([128, D], BF16) for _ in range(4)]
    r_sz = [Res("sz") for _ in range(4)]
    dtt = P.tile([128, 4, 16], F32)
    at = P.tile([128, 4, 16], F32)
    sp_t = [P.tile([128, 4, 16], F32) for _ in range(2)]
    r_dt = Res("dt")
    xsT = P.tile([128, 8, NB], BF16)
    BT = P.tile([128, 4, NB], BF16)
    CT = P.tile([128, 4, NB], BF16)
    r_xbc = [Res("xbc%d" % c) for c in range(16)]
    mixT2 = [P.tile([128, 16, NB], BF16) for _ in range(2)]
    r_mix_a2 = [[Res("mixa%d" % q) for q in range(4)] for _ in range(2)]
    r_mix_b2 = [Res("mixb") for _ in range(2)]
    pending_wout = None
    stg = [P.tile([128, NB + 3], F32) for _ in range(2)]
    r_stg = [Res("stg") for _ in range(2)]
    acc = [P.tile([128, NB], F32) for _ in range(2)]
    r_acc = [Res("acc") for _ in range(2)]
    scr = {"ss": P.tile([128, 8], F32), "junk": P.tile([128, D], BF16),
           "hb": [P.tile([128, D], BF16) for _ in range(2)]}
    r_scr = {"ss": Res("ss"), "junk": Res("junk"), "hb": [Res("hb0"), Res("hb1")]}
    xdt = P.tile([128, D], BF16)
    xdtd = P.tile([128, D], BF16)
    r_xdt, r_xdtd = Res("xdt"), Res("xdtd")
    B_tm = P.tile([128, 512], BF16)
    r_Btm = Res("Btm")
    rhsA = P.tile([128, 16, 128], F32)
    r_rhsA = Res("rhsA")
    expd = P.tile([128, 16, 128], BF16)
    r_expd = Res("expd")
    mCB = P.tile([128, 4, 128], F32)
    r_mCB = Res("mCB")
    scores = P.tile([128, 16, 128], BF16)
    r_scores = Res("scores")
    sm = P.tile([128, 6, 16], F32)
    r_sm = Res("sm")
    ytmp = P.tile([128, D], F32)
    r_ytmp = Res("ytmp")
    yv = P.tile([128, D], F32)
    r_y = Res("y")
    ssg = P.tile([128, 4], F32)
    r_ssg = Res("ssg")
    ya_tm = P.tile([128, D], BF16)
    r_yatm = Res("yatm")
    wcnt = 0

    def next_slot():
        nonlocal wcnt
        ws = wcnt % NW
        wcnt += 1
        return ws

    Win = W["ev_w_in"]
    ybT_v = ybT.rearrange("c p t -> p c t")
    pcnt = 0
    load_x_tiles(P, xin, r_xin, xt, r_xt, 0, "e2x%d")
    rmsnorm_tiles(P, C, xt, r_xt, g_bc, r_g, hT2[0], r_hT2[0], (0, 1), scr, r_scr)
    for blk in range(nblk):
        t0 = blk * NB
        if hook is not None:
            hook(blk)
        hT, r_hT = hT2[blk % 2], r_hT2[blk % 2]
        mixT, r_mix_a, r_mix_b = mixT2[blk % 2], r_mix_a2[blk % 2], r_mix_b2[blk % 2]
        if blk + 1 < nblk:
            load_x_tiles(P, xin, r_xin, xt, r_xt, blk + 1, "e2x%d")
        P.op("sp", lambda e, t0=t0, mixT=mixT: e.dma_start(out=mixT[:, 8:16, :], in_=ybT_v[:, :, t0:t0 + NB]),
             reads=[r_yb[blk]], writes=[r_mix_b], dma_key="e2yb")
        for pn in range(2):
            ws = next_slot()
            WS.load(P, "ez%d" % pn, wsl[ws], r_wsl[ws], "e2w%d" % ws)
            for i in range(4):
                pb = 2 + (pcnt % 2)
                pcnt += 1
                for k in range(8):
                    P.op("pe", lambda e, hT=hT, pb=pb, ws=ws, k=k, i=i: e.matmul(
                        C.banks[pb][:], lhsT=hT[:, k, i * 128:(i + 1) * 128], rhs=wsl[ws][:, k, :],
                        start=(k == 0), stop=(k == 7)), reads=[r_wsl[ws], r_hT], writes=[C.rb[pb]])
                P.op("act", lambda e, pb=pb, i=i, pn=pn: e.activation(out=sz[i][:, pn * 512:(pn + 1) * 512],
                                                                       in_=C.banks[pb][:], func=AF.Silu),
                     reads=[C.rb[pb]], writes=[r_sz[i]])
        pb = 2 + (pcnt % 2)
        pcnt += 1
        for i in range(4):
            for k in range(8):
                P.op("pe", lambda e, hT=hT, pb=pb, k=k, i=i: e.matmul(
                    C.banks[pb][:, i * 16:(i + 1) * 16], lhsT=hT[:, k, i * 128:(i + 1) * 128], rhs=wdt[:, k, :],
                    start=(k == 0), stop=(k == 7)), reads=[r_g, r_hT], writes=[C.rb[pb]])
        dtr, sp1 = sp_t
        P.op("dve", lambda e, pb=pb: e.tensor_tensor(out=dtr, in0=C.banks[pb][:, 0:64].rearrange("p (i h) -> p i h", i=4),
                                                     in1=dtb_bc.unsqueeze(1).broadcast_to([128, 4, 16]), op=ALU.add),
             reads=[C.rb[pb], r_g], writes=[r_dt])
        P.op("dve", lambda e: e.tensor_scalar(out=sp1, in0=dtr, scalar1=-1.0, scalar2=None, op0=ALU.mult), reads=[r_dt], writes=[r_dt])
        P.op("dve", lambda e: e.tensor_tensor(out=sp1, in0=sp1, in1=dtr, op=ALU.min), reads=[r_dt], writes=[r_dt])
        P.op("act", lambda e: e.activation(out=sp1, in_=sp1, func=AF.Exp), reads=[r_dt], writes=[r_dt])
        P.op("act", lambda e: e.activation(out=sp1, in_=sp1, func=AF.Ln, bias=1.0, scale=1.0), reads=[r_dt], writes=[r_dt])
        P.op("dve", lambda e: e.tensor_scalar(out=dtr, in0=dtr, scalar1=0.0, scalar2=None, op0=ALU.max), reads=[r_dt], writes=[r_dt])
        P.op("dve", lambda e: e.tensor_tensor(out=dtt, in0=dtr, in1=sp1, op=ALU.add), reads=[r_dt], writes=[r_dt])
        P.op("dve", lambda e: e.tensor_tensor(out=at, in0=dtt, in1=ahead_bc.unsqueeze(1).broadcast_to([128, 4, 16]), op=ALU.mult),
             reads=[r_dt, r_g], writes=[r_dt])
        for c in range(16):
            if c % 4 == 0:
                ws = next_slot()
                WS.load(P, "ex%d" % (c // 4), wsl[ws], r_wsl[ws], "e2w%d" % ws)
                if blk == 0 and c == 12 and after_first_loads is not None:
                    after_first_loads()
            cc = c % 4
            pb = 2 + (pcnt % 2)
            pcnt += 1
            si = c % 2
            for k in range(8):
                P.op("pe", lambda e, hT=hT, pb=pb, ws=ws, k=k, cc=cc: e.matmul(
                    C.banks[pb][:], lhsT=wsl[ws][:, k, cc * 128:(cc + 1) * 128], rhs=hT[:, k, :],
                    start=(k == 0), stop=(k == 7)), reads=[r_wsl[ws], r_hT], writes=[C.rb[pb]])
            P.op("act", lambda e, pb=pb, si=si: e.copy(out=stg[si][:, 3:3 + NB], in_=C.banks[pb][:]),
                 reads=[C.rb[pb]], writes=[r_stg[si]])
            P.op("dve", lambda e, si=si, c=c: e.tensor_copy(out=stg[si][:, 0:3], in_=halo[:, c, :]),
                 reads=[r_halo[c], r_stg[si]], writes=[r_stg[si]])
            P.op("dve", lambda e, si=si, c=c: e.tensor_scalar(out=acc[si], in0=stg[si][:, 0:NB], scalar1=wc4[:, c, 0:1],
                                                             scalar2=cb4[:, c:c + 1], op0=ALU.mult, op1=ALU.add),
                 reads=[r_stg[si], r_g], writes=[r_acc[si]])
            for j in range(1, 4):
                P.op("dve", lambda e, si=si, c=c, j=j: e.scalar_tensor_tensor(
                    out=acc[si], in0=stg[si][:, j:j + NB], scalar=wc4[:, c, j:j + 1], in1=acc[si], op0=ALU.mult, op1=ALU.add),
                    reads=[r_stg[si], r_g, r_acc[si]], writes=[r_acc[si]])
            P.op("dve", lambda e, si=si, c=c: e.tensor_copy(out=halo[:, c, :], in_=stg[si][:, NB:NB + 3]),
                 reads=[r_stg[si]], writes=[r_halo[c]])
            dst = xsT[:, c, :] if c < 8 else (BT[:, c - 8, :] if c < 12 else CT[:, c - 12, :])
            P.op("act", lambda e, si=si, dst=dst: e.activation(out=dst, in_=acc[si], func=AF.Silu),
                 reads=[r_acc[si]], writes=[r_xbc[c]])
        pend = []
        if pending_wout is not None:
            pending_wout["start"]()
        for q in range(4):
            tq = q * 128
            a_q = at[:, q, :]
            dt_q = dtt[:, q, :]
            tb0 = bank_bf(C, 0)
            tb1 = bank_bf(C, 1)
            for c in range(8):
                P.op("pe", lambda e, c=c, tq=tq, tb0=tb0: e.transpose(out=tb0[:, c * 128:(c + 1) * 128],
                                                                      in_=xsT[:, c, tq:tq + 128], identity=C.ident),
                     reads=[r_xbc[c], C.r_const], writes=[C.rb[0]])
            for g in range(4):
                P.op("pe", lambda e, g=g, tq=tq, tb1=tb1: e.transpose(out=tb1[:, g * 128:(g + 1) * 128],
                                                                      in_=BT[:, g, tq:tq + 128], identity=C.ident),
                     reads=[r_xbc[8 + g], C.r_const], writes=[C.rb[1]])
            P.op("dve", lambda e, tb0=tb0, dt_q=dt_q: e.tensor_tensor(
                out=xdt.rearrange("p (h d) -> p h d", h=16), in0=tb0.rearrange("p (h d) -> p h d", h=16),
                in1=bc_mid(dt_q, 64), op=ALU.mult), reads=[C.rb[0], r_dt], writes=[r_xdt])
            P.op("act", lambda e, tb1=tb1: e.copy(out=B_tm, in_=tb1[:, 0:512]), reads=[C.rb[1]], writes=[r_Btm])
            P.op("pe", lambda e, a_q=a_q: e.matmul(C.banks[2][:, 0:16], lhsT=Tmask, rhs=a_q, start=True, stop=True),
                 reads=[r_dt, r_g], writes=[C.rb[2]])
            P.op("dve", lambda e, a_q=a_q: e.tensor_tensor(out=rhsA, in0=Tmask.unsqueeze(1).broadcast_to([128, 16, 128]),
                                                           in1=bc_mid(a_q, 128), op=ALU.mult),
                 reads=[r_dt, r_g], writes=[r_rhsA])
            for hb_ in range(4):
                P.op("pe", lambda e, hb_=hb_: e.matmul(C.banks[4 + hb_][:], lhsT=Umask,
                                                       rhs=rhsA[:, hb_ * 4:(hb_ + 1) * 4, :].rearrange("p h l -> p (h l)"),
                                                       start=True, stop=True),
                     reads=[r_rhsA, r_g], writes=[C.rb[4 + hb_]])
            P.op("pe", lambda e, a_q=a_q: e.matmul(C.banks[2][:, 16:32], lhsT=C.onesf, rhs=a_q, start=True, stop=True),
                 reads=[r_dt, C.r_const], writes=[C.rb[2]])
            if pending_wout is not None:
                pending_wout["piece"](q // 2, 2 * (q % 2), 1)
            for g in range(4):
                P.op("pe", lambda e, g=g, tq=tq: e.matmul(C.banks[3][:, g * 128:(g + 1) * 128], lhsT=BT[:, g, tq:tq + 128],
                                                          rhs=CT[:, g, tq:tq + 128], start=True, stop=True),
                     reads=[r_xbc[8 + g], r_xbc[12 + g]], writes=[C.rb[3]])
            P.op("act", lambda e: e.copy(out=sm[:, 0, :], in_=C.banks[2][:, 0:16]), reads=[C.rb[2]], writes=[r_sm])
            P.op("act", lambda e: e.activation(out=sm[:, 1, :], in_=sm[:, 0, :], func=AF.Exp), reads=[r_sm], writes=[r_sm])
            P.op("act", lambda e: e.activation(out=sm[:, 3, :], in_=C.banks[2][:, 16:32], func=AF.Exp), reads=[C.rb[2]], writes=[r_sm])
            for hb_ in range(4):
                P.op("act", lambda e, hb_=hb_: e.activation(
                    out=sm[:, 2, hb_ * 4:(hb_ + 1) * 4],
                    in_=C.banks[4 + hb_][:].rearrange("p (h l) -> p h l", h=4)[:, :, 127], func=AF.Exp),
                    reads=[C.rb[4 + hb_]], writes=[r_sm])
                P.op("act", lambda e, hb_=hb_: e.activation(out=expd[:, hb_ * 4:(hb_ + 1) * 4, :].rearrange("p h l -> p (h l)"),
                                                            in_=C.banks[4 + hb_][:], func=AF.Exp),
                     reads=[C.rb[4 + hb_]], writes=[r_expd])
            P.op("dve", lambda e: e.tensor_tensor(out=mCB, in0=C.banks[3][:].rearrange("p (g l) -> p g l", g=4),
                                                  in1=Tmask.unsqueeze(1).broadcast_to([128, 4, 128]), op=ALU.mult),
                 reads=[C.rb[3], r_g], writes=[r_mCB])
            P.op("dve", lambda e: e.tensor_tensor(
                out=scores.rearrange("p (g r) l -> p g r l", g=4), in0=expd.rearrange("p (g r) l -> p g r l", g=4),
                in1=mCB.unsqueeze(2).broadcast_to([128, 4, 4, 128]), op=ALU.mult),
                reads=[r_expd, r_mCB], writes=[r_scores])
            P.op("dve", lambda e: e.tensor_tensor(out=xdtd.rearrange("p (h d) -> p h d", h=16),
                                                  in0=xdt.rearrange("p (h d) -> p h d", h=16), in1=bc_mid(sm[:, 2, :], 64),
                                                  op=ALU.mult), reads=[r_xdt, r_sm], writes=[r_xdtd])
            for h in range(16):
                pb = 4 + h // 8
                col = (h % 8) * 64
                c = h // 2
                first_of_chunk = (h % 2 == 0)
                if first_of_chunk:
                    P.op("pe", lambda e, pb=pb, c=c, tq=tq: e.matmul(
                        C.banks[pb][:, (c % 4) * 128:(c % 4 + 1) * 128], lhsT=xsT[:, c, tq:tq + 128], rhs=diagD[:, c, :],
                        start=True, stop=False), reads=[r_xbc[c], r_g], writes=[C.rb[pb]])
                P.op("pe", lambda e, pb=pb, col=col, h=h: e.matmul(
                    C.banks[pb][:, col:col + 64], lhsT=scores[:, h, :], rhs=xdt[:, h * 64:(h + 1) * 64],
                    start=False, stop=(h % 2 == 1)), reads=[r_scores, r_xdt], writes=[C.rb[pb]])
            for g in range(4):
                pb = 6 + g // 2
                P.op("pe", lambda e, pb=pb, g=g, tq=tq: e.matmul(
                    C.banks[pb][:, (g % 2) * 256:(g % 2 + 1) * 256], lhsT=CT[:, g, tq:tq + 128],
                    rhs=prev_bf[:, g * 256:(g + 1) * 256], start=True, stop=True),
                    reads=[r_xbc[12 + g], r_prevbf], writes=[C.rb[pb]])
            for hf in range(2):
                sl_ = slice(hf * 512, (hf + 1) * 512)
                P.op("dve", lambda e, hf=hf, sl_=sl_: e.tensor_tensor(
                    out=ytmp[:, sl_].rearrange("p (h d) -> p h d", h=8),
                    in0=C.banks[6 + hf][:].rearrange("p (h d) -> p h d", h=8),
                    in1=bc_mid(sm[:, 1, hf * 8:(hf + 1) * 8], 64), op=ALU.mult),
                    reads=[C.rb[6 + hf], r_sm], writes=[r_ytmp])
                P.op("dve", lambda e, hf=hf, sl_=sl_: e.tensor_tensor(out=yv[:, sl_], in0=ytmp[:, sl_], in1=C.banks[4 + hf][:],
                                                                      op=ALU.add),
                     reads=[C.rb[4 + hf], r_ytmp], writes=[r_y])
            for g in range(4):
                pb = 2 + g // 2
                P.op("pe", lambda e, pb=pb, g=g: e.matmul(
                    C.banks[pb][:, (g % 2) * 256:(g % 2 + 1) * 256], lhsT=B_tm[:, g * 128:(g + 1) * 128],
                    rhs=xdtd[:, g * 256:(g + 1) * 256], start=True, stop=True),
                    reads=[r_Btm, r_xdtd], writes=[C.rb[pb]])
            P.op("dve", lambda e: e.tensor_tensor(out=prev.rearrange("p (h d) -> p h d", h=16),
                                                  in0=prev.rearrange("p (h d) -> p h d", h=16),
                                                  in1=bc_mid(sm[:, 3, :], 64), op=ALU.mult),
                 reads=[r_sm, r_prev], writes=[r_prev])
            for hf in range(2):
                sl_ = slice(hf * 512, (hf + 1) * 512)
                P.op("dve", lambda e, hf=hf, sl_=sl_: e.tensor_tensor(out=prev[:, sl_], in0=prev[:, sl_],
                                                                      in1=C.banks[2 + hf][:], op=ALU.add),
                     reads=[C.rb[2 + hf], r_prev], writes=[r_prev])
            P.op("act", lambda e: e.copy(out=prev_bf, in_=prev), reads=[r_prev], writes=[r_prevbf])
            if pending_wout is not None:
                pending_wout["piece"](q // 2, 2 * (q % 2) + 1, 1)
            while pend:
                pend.pop(0)()
            P.op("dve", lambda e, q=q: e.tensor_tensor(out=yv, in0=yv, in1=sz[q], op=ALU.mult),
                 reads=[r_y, r_sz[q]], writes=[r_y])
            for g in range(4):
                P.op("act", lambda e, g=g: e.activation(out=ytmp[:, g * 256:(g + 1) * 256], in_=yv[:, g * 256:(g + 1) * 256],
                                                        func=AF.Square, accum_out=ssg[:, g:g + 1]),
                     reads=[r_y], writes=[r_ytmp, r_ssg])
            P.op("act", lambda e: e.activation(out=ssg, in_=ssg, func=AF.Ln, scale=1.0 / 256, bias=EPS),
                 reads=[r_ssg], writes=[r_ssg])
            P.op("act", lambda e: e.activation(out=ssg, in_=ssg, func=AF.Exp, scale=-0.5), reads=[r_ssg], writes=[r_ssg])
            for g in range(4):
                gs = slice(g * 256, (g + 1) * 256)
                P.op("dve", lambda e, g=g, gs=gs: e.scalar_tensor_tensor(out=ya_tm[:, gs], in0=yv[:, gs], scalar=ssg[:, g:g + 1],
                                                                         in1=ng_bc[:, gs], op0=ALU.mult, op1=ALU.mult),
                     reads=[r_y, r_ssg, r_g], writes=[r_yatm])
            def emit_ya(q=q, tq=tq, tb0=tb0, mixT=mixT, r_mix_a=r_mix_a):
                for c in range(8):
                    P.op("pe", lambda e, c=c, tb0=tb0: e.transpose(out=tb0[:, c * 128:(c + 1) * 128],
                                                                   in_=ya_tm[:, c * 128:(c + 1) * 128], identity=C.ident),
                         reads=[r_yatm, C.r_const], writes=[C.rb[0]])
                P.op("act", lambda e, tb0=tb0, tq=tq, mixT=mixT: e.copy(out=mixT[:, 0:8, tq:tq + 128],
                                                             in_=tb0.rearrange("p (k n) -> p k n", k=8)),
                     reads=[C.rb[0]], writes=[r_mix_a[q]])
            pend.append(emit_ya)
        while pend:
            pend.pop(0)()
        if blk + 1 < nblk:
            rmsnorm_tiles(P, C, xt, r_xt, g_bc, r_g, hT2[(blk + 1) % 2], r_hT2[(blk + 1) % 2], (0, 1), scr, r_scr)
        def make_wout(blk=blk, t0=t0, mixT=mixT, r_mix_a=r_mix_a, r_mix_b=r_mix_b):
            xr_of = {}
            slots = {}

            def issue_xres(half, i):
                nonlocal xrc
                k_ = xrc % NXR
                xrc += 1
                xr_of[(half, i)] = k_
                P.op("sp", lambda e, k_=k_, i=i, half=half: e.dma_start(
                    out=xres[k_], in_=xin[t0 + i * 128:t0 + (i + 1) * 128, half * 512:(half + 1) * 512]),
                    reads=[r_xin[blk * 4 + i]], writes=[r_xres[k_]], dma_key="e2r" + str(k_))

            def start():
                nonlocal wcnt
                for hi in [(half, i) for half in range(2) for i in range(4)][:NXR]:
                    issue_xres(*hi)
                for half in range(2):
                    while wcnt % NW not in (0, 2):
                        wcnt += 1
                    ws = next_slot()
                    ws2 = next_slot()
                    WS.load(P, "eo%d0" % half, wsl[ws], r_wsl[ws], "e2w%d" % ws)
                    WS.load(P, "eo%d1" % half, wsl[ws2], r_wsl[ws2], "e2w%d" % ws2)
                    slots[half] = (ws, ws2)

            def piece(half, i, pb):
                ws, ws2 = slots[half]
                for c in range(16):
                    wsx = ws if c < 8 else ws2
                    P.op("pe", lambda e, pb=pb, c=c, i=i, wsx=wsx: e.matmul(
                        C.banks[pb][:], lhsT=mixT[:, c, i * 128:(i + 1) * 128], rhs=wsl[wsx][:, c % 8, :],
                        start=(c == 0), stop=(c == 15)),
                        reads=[r_mix_a[i], r_mix_b, r_wsl[wsx]], writes=[C.rb[pb]])
                if (half, i) not in xr_of:
                    issue_xres(half, i)
                k_ = xr_of[(half, i)]
                xr_ = xres[k_]
                r_xr = r_xres[k_]
                P.op("dve", lambda e, pb=pb, xr_=xr_: e.tensor_tensor(out=xr_, in0=xr_, in1=C.banks[pb][:], op=ALU.add),
                     reads=[C.rb[pb], r_xr], writes=[r_xr])
                P.op("sp", lambda e, i=i, half=half, xr_=xr_: e.dma_start(
                    out=xout[t0 + i * 128:t0 + (i + 1) * 128, half * 512:(half + 1) * 512], in_=xr_),
                    reads=[r_xr], writes=[r_xout[blk * 4 + i]], dma_key="e2o")

            return {"start": start, "piece": piece}

        pending_wout = make_wout()
    if pending_wout is not None:
        pending_wout["start"]()
        n_ = 0
        for half in range(2):
            for i in range(4):
                pending_wout["piece"](half, i, 2 + (n_ % 2))
                n_ += 1


def build_even_only(nblk=NBLK):
    nc = bass.Bass("TRN2", target_bir_lowering=False)
    xin = nc.dram_tensor("x", [S, D], F32, kind="ExternalInput").ap()
    Wd = declare(nc, EVEN_SHAPES)
    W = {k: v[0] for k, v in Wd.items()}
    ybT = nc.dram_tensor("ybT", [8, 128, S], BF16, kind="Internal").ap()
    out = nc.dram_tensor("out", [S, D], F32, kind="ExternalOutput").ap()
    P = Prog(nc, ARENA)
    C = Ctx()
    setup_common(P, C)
    r_in = [Res("xin") for _ in range(32)]
    r_yb = [Res("yb") for _ in range(NBLK)]
    r_out = [Res("xout") for _ in range(32)]
    WS = WStore(nc)
    define_panels(WS, We=W)
    WS.emit_group(P, "A")
    WS.emit_group(P, "B")
    phase_e1(P, C, WS, xin, r_in, W, ybT, r_yb, nblk=nblk)
    P.barrier()
    phase_e2(P, C, WS, xin, r_in, W, ybT, r_yb, out, r_out, nblk=nblk)
    P.op("sp", None, reads=r_out[:nblk * 4])
    P.emit()
    return nc, P


TWO_PI = 2.0 * math.pi


def sincos(P, X, out_sin, out_cos, t1, t2, rg):
    I32 = mybir.dt.int32
    t1i = t1.bitcast(I32)
    C1 = 6.28125
    C2 = TWO_PI - C1

    def ew(eng, fn):
        P.op(eng, fn, reads=rg, writes=rg)
    ew("dve", lambda e: e.tensor_scalar(out=t1i, in0=X, scalar1=1.0 / TWO_PI, scalar2=None, op0=ALU.mult))
    ew("dve", lambda e: e.tensor_copy(out=t2, in_=t1i))
    ew("dve", lambda e: e.scalar_tensor_tensor(out=t1, in0=t2, scalar=-C1, in1=X, op0=ALU.mult, op1=ALU.add))
    ew("dve", lambda e: e.scalar_tensor_tensor(out=t1, in0=t2, scalar=-C2, in1=t1, op0=ALU.mult, op1=ALU.add))
    ew("act", lambda e: e.activation(out=out_sin, in_=t1, func=AF.Sin))
    ew("dve", lambda e: e.tensor_scalar(out=t2, in0=t1, scalar1=-1.0, scalar2=None, op0=ALU.mult))
    ew("dve", lambda e: e.tensor_tensor(out=t2, in0=t2, in1=t1, op=ALU.max))
    ew("act", lambda e: e.activation(out=out_cos, in_=t2, func=AF.Sin, scale=-1.0, bias=math.pi / 2))


def phase_o1(P, C, WS, xin, r_xin, W, iota, ycT, r_yc, nblk=NBLK, dbg=None, DEC=4, hook=None):
    P.reset()
    KD = NB // DEC
    PB = 512 // KD
    NBT = 16 // PB
    r_g = Res("g")
    key = "o1g"
    rg = [r_g]

    r_ld = []

    def ldres():
        r_ld.append(Res("ld"))
        return r_ld[-1]

    def ew(eng, fn):
        P.op(eng, fn, reads=rg + r_ld, writes=rg)

    g_bc = P.tile([128, D], F32)
    load_bcast_vec(P, g_bc, W["od_norm"], ldres(), key)
    iot = P.tile([128, KD], F32)
    ctD = P.tile([128, 16, KD], F32)
    stD = P.tile([128, 16, KD], F32)
    sel = P.tile([128, 2], F32)
    rmask = P.tile([128, 4], F32)
    BpadD = [P.tile([128, 2, 16, 128], BF16) for _ in range(DEC)]
    Cpad = P.tile([128, 2, 16, 128], BF16)
    CLpad = [P.tile([128, 2, 16, 128], BF16) for _ in range(DEC - 1)]
    Kmat = P.tile([128, 4, max(DEC - 1, 1), 128], BF16)
    Sbuf = P.tile([128, 2, 16, KD + 1], BF16)
    dg8 = P.tile([128, 8], F32)
    dcol = dg8[:, 0:4]
    gbcol = dg8[:, 4:8]
    gw = P.tile([128, 4, 512], BF16)
    rinit = P.tile([128, 2, 16], F32)
    pp = P.tile([128, 16, 16], F32)
    pq = P.tile([128, 24, 16], F32)
    setup_mark = P.off
    targ = P.tile([128, 16 * KD], F32)
    ttmp = P.tile([128, 16 * KD], F32)
    ttmp2 = P.tile([128, 16 * KD], F32)
    pt2 = P.tile([128, 2, 16], F32)
    braw = P.tile([128, 2, 16, 16], F32)
    bbD = [P.tile([128, 2, 16, 16], F32) for _ in range(DEC)]
    btmp = P.tile([128, 16, 16], F32)
    mpad = P.tile([128, 2 * 16 * 32], F32)
    craw = P.tile([128, 2, 16, 16], F32)
    clraw = P.tile([128, 2, 16, 16], F32)
    BpadU = P.tile([128, 2, 16, 128], BF16)
    snat = P.tile([16, 1024], F32)

    P.op("sp", lambda e: e.dma_start(out=iot, in_=iota[:, 0:KD]), writes=[ldres()], dma_key=key)
    lnat = snat[0:16, 0:512].rearrange("p (a n) -> p a n", a=4)
    ld16 = snat[0:16, 512:514]
    for i, nm in ((0, "od_lam_re"), (1, "od_lam_im")):
        lv = W[nm].rearrange("(pr gl) p -> gl pr p", gl=2)
        for gl in range(2):
            P.op("sp", lambda e, i=i, gl=gl, lv=lv: e.dma_start(out=lnat[:, i, gl * 64:(gl + 1) * 64], in_=lv[gl]),
                 writes=[ldres()], dma_key=key)
    P.op("sp", lambda e: e.dma_start(out=ld16, in_=W["od_log_dt"].rearrange("(pr gl) -> pr gl", gl=2)), writes=[ldres()], dma_key=key)
    ew("dve", lambda e: e.tensor_copy(out=lnat[:, 2, :].rearrange("p (gl q) -> p gl q", gl=2), in_=bc_mid(ld16, 64)))
    for i in range(3):
        P.op("pe", lambda e, i=i: e.transpose(out=C.banks[0][:, i * 16:(i + 1) * 16], in_=lnat[:, i, :], identity=C.identf[0:16, 0:16]),
             reads=rg + r_ld + [C.r_const], writes=[C.rb[0]])
    P.op("act", lambda e: e.copy(out=pp[:, 0:3, :], in_=C.banks[0][:, 0:48].rearrange("p (a b) -> p a b", a=3)),
         reads=[C.rb[0]], writes=rg)
    def TT(out, a_, b_, op):
        ew("dve", lambda e: e.tensor_tensor(out=out, in0=a_, in1=b_, op=op))

    ew("act", lambda e: e.activation(out=pp[:, 2, :], in_=pp[:, 2, :], func=AF.Exp))
    TT(pp[:, 3, :], pp[:, 0, :], pp[:, 2, :], ALU.mult)
    TT(pp[:, 4, :], pp[:, 1, :], pp[:, 2, :], ALU.mult)
    ew("act", lambda e: e.activation(out=pp[:, 3, :], in_=pp[:, 3, :], func=AF.Exp))
    sincos(P, pp[:, 4, :], pp[:, 5, :], pp[:, 6, :], pt2[:, 0, :], pt2[:, 1, :], rg)
    ew("dve", lambda e: e.tensor_copy(out=pq[:, 18, :], in_=pp[:, 3, :]))
    for d in range(1, DEC + 1):
        if d > 1:
            TT(pq[:, 18, :], pq[:, 18, :], pp[:, 3, :], ALU.mult)
            ew("dve", lambda e, d=d: e.tensor_scalar(out=pq[:, 19, :], in0=pp[:, 4, :], scalar1=float(d), scalar2=None, op0=ALU.mult))
            sincos(P, pq[:, 19, :], pq[:, 20, :], pq[:, 21, :], pt2[:, 0, :], pt2[:, 1, :], rg)
            TT(pq[:, d - 1, :], pq[:, 18, :], pq[:, 21, :], ALU.mult)
            TT(pq[:, 8 + d - 1, :], pq[:, 18, :], pq[:, 20, :], ALU.mult)
        else:
            TT(pq[:, 0, :], pp[:, 3, :], pp[:, 6, :], ALU.mult)
            TT(pq[:, 8, :], pp[:, 3, :], pp[:, 5, :], ALU.mult)
    ew("dve", lambda e: e.tensor_copy(out=pq[:, 16, :], in_=pq[:, 18, :]))
    ew("dve", lambda e: e.tensor_scalar(out=pq[:, 17, :], in0=pp[:, 4, :], scalar1=float(DEC), scalar2=None, op0=ALU.mult))
    ew("dve", lambda e: e.tensor_scalar(out=pp[:, 7, :], in0=pq[:, 0, :], scalar1=-1.0, scalar2=None, op0=ALU.add))
    ew("dve", lambda e: e.tensor_copy(out=pp[:, 8, :], in_=pq[:, 8, :]))
    TT(pp[:, 9, :], pp[:, 0, :], pp[:, 0, :], ALU.mult)
    TT(pp[:, 14, :], pp[:, 1, :], pp[:, 1, :], ALU.mult)
    TT(pp[:, 9, :], pp[:, 9, :], pp[:, 14, :], ALU.add)
    ew("dve", lambda e: e.reciprocal(out=pp[:, 9, :], in_=pp[:, 9, :]))
    TT(pp[:, 10, :], pp[:, 7, :], pp[:, 0, :], ALU.mult)
    TT(pp[:, 14, :], pp[:, 8, :], pp[:, 1, :], ALU.mult)
    TT(pp[:, 10, :], pp[:, 10, :], pp[:, 14, :], ALU.add)
    TT(pp[:, 10, :], pp[:, 10, :], pp[:, 9, :], ALU.mult)
    TT(pp[:, 11, :], pp[:, 8, :], pp[:, 0, :], ALU.mult)
    TT(pp[:, 14, :], pp[:, 7, :], pp[:, 1, :], ALU.mult)
    TT(pp[:, 11, :], pp[:, 11, :], pp[:, 14, :], ALU.subtract)
    TT(pp[:, 11, :], pp[:, 11, :], pp[:, 9, :], ALU.mult)
    ew("dve", lambda e: e.tensor_scalar(out=pp[:, 15, :], in0=pp[:, 4, :], scalar1=float(NB), scalar2=None, op0=ALU.mult))
    sincos(P, pp[:, 15, :], pp[:, 13, :], pp[:, 12, :], pt2[:, 0, :], pt2[:, 1, :], rg)
    ctf = ctD.rearrange("p a k -> p (a k)")
    stf = stD.rearrange("p a k -> p (a k)")
    ew("dve", lambda e: e.tensor_tensor(out=targ.rearrange("p (a k) -> p a k", a=16), in0=iot.unsqueeze(1).broadcast_to([128, 16, KD]),
                                        in1=bc_mid(pq[:, 17, :], KD), op=ALU.mult))
    sincos(P, targ, stf, ctf, ttmp, ttmp2, rg)
    P.op("sp", lambda e: e.dma_start(out=braw[:, 0], in_=W["od_b_re"].rearrange("(pr gl) p c -> (gl p) pr c", gl=2),
                                     allow_slow_non_contiguous=True), writes=[ldres()], dma_key=key)
    P.op("sp", lambda e: e.dma_start(out=braw[:, 1], in_=W["od_b_im"].rearrange("(pr gl) p c -> (gl p) pr c", gl=2),
                                     allow_slow_non_contiguous=True), writes=[ldres()], dma_key=key)
    cnat = mpad.rearrange("p (i j n) -> p i j n", i=2, j=2)[:, :, :, 0:128]
    nq = 0
    for i, nm in ((0, "od_c_re"), (1, "od_c_im")):
        cv = W[nm].rearrange("(pr gl) co p -> gl pr co p", gl=2)
        for j in range(2):
            for gl in range(2):
                nq += 1
                for pr8 in range(8):
                    nq += 1
                    P.op("sp", lambda e, i=i, j=j, gl=gl, cv=cv, pr8=pr8: e.dma_start(
                        out=cnat[pr8 * 16:(pr8 + 1) * 16, i, j, gl * 64:(gl + 1) * 64], in_=cv[gl, 8 * j + pr8]),
                        writes=[ldres()], dma_key=key)
    dnat = snat[0:8, 640:768]
    P.op("sp", lambda e: e.dma_start(out=dnat[0:4, :], in_=W["od_s5_d"].rearrange("(c p) -> c p", p=128)), writes=[ldres()], dma_key=key)
    P.op("sp", lambda e: e.dma_start(out=dnat[4:8, :], in_=W["od_glu_b"].rearrange("(c p) -> c p", p=128)), writes=[ldres()], dma_key=key)
    P.op("pe", lambda e: e.transpose(out=C.banks[1][:, 0:8], in_=dnat, identity=C.identf[0:8, 0:8]),
         reads=rg + r_ld + [C.r_const], writes=[C.rb[1]])
    P.op("act", lambda e: e.copy(out=dg8, in_=C.banks[1][:, 0:8]), reads=[C.rb[1]], writes=rg)
    P.op("pool", lambda e: e.dma_start(out=gw, in_=W["od_glu_w"].rearrange("(k p) n -> p k n", p=128)), writes=[ldres()], dma_key=key)

    ew("dve", lambda e: e.reduce_sum(out=sel, in_=C.identf.rearrange("p (a b) -> p a b", a=2), axis=AX.X))
    ew("dve", lambda e: e.reduce_sum(out=rmask, in_=C.identf.rearrange("p (a b) -> p a b", a=4), axis=AX.X))
    fre_b = bc_mid(pp[:, 10, :], 16)
    fim_b = bc_mid(pp[:, 11, :], 16)
    TT(bbD[0][:, 0], braw[:, 0], fre_b, ALU.mult)
    TT(btmp, braw[:, 1], fim_b, ALU.mult)
    TT(bbD[0][:, 0], bbD[0][:, 0], btmp, ALU.subtract)
    TT(bbD[0][:, 1], braw[:, 1], fre_b, ALU.mult)
    TT(btmp, braw[:, 0], fim_b, ALU.mult)
    TT(bbD[0][:, 1], bbD[0][:, 1], btmp, ALU.add)
    for d in range(1, DEC):
        lre_b = bc_mid(pq[:, d - 1, :], 16)
        lim_b = bc_mid(pq[:, 8 + d - 1, :], 16)
        TT(bbD[d][:, 0], bbD[0][:, 0], lre_b, ALU.mult)
        TT(btmp, bbD[0][:, 1], lim_b, ALU.mult)
        TT(bbD[d][:, 0], bbD[d][:, 0], btmp, ALU.subtract)
        TT(bbD[d][:, 1], bbD[0][:, 1], lre_b, ALU.mult)
        TT(btmp, bbD[0][:, 0], lim_b, ALU.mult)
        TT(bbD[d][:, 1], bbD[d][:, 1], btmp, ALU.add)
    for i in range(2):
        for j in range(2):
            P.op("pe", lambda e, i=i, j=j: e.transpose(out=C.banks[2 + j][:, 0:128], in_=cnat[:, i, j, :], identity=C.identf),
                 reads=rg + r_ld + [C.r_const], writes=[C.rb[2 + j]])
            P.op("act", lambda e, i=i, j=j: e.copy(out=craw[:, i, 8 * j:8 * j + 8, :],
                                                   in_=C.banks[2 + j][:, 0:128].rearrange("p (a b) -> p a b", a=8)),
                 reads=[C.rb[2 + j]], writes=rg)

    r_zero = Res("zero")
    for zt in [Cpad, BpadU] + CLpad:
        P.op("pool", lambda e, zt=zt: e.memset(zt, 0.0), writes=[r_zero])

    def build_pad(dst, src, neg_im):
        for i in range(2):
            for gl in range(2):
                d0 = dst[:, i]
                dv_ = bass.AP(d0.tensor, d0.offset + gl * 16, [[d0.ap[0][0], 128], [512, 4], [160, 4], [1, 16]])
                P.op("dve", lambda e, i=i, gl=gl, dv_=dv_: e.tensor_scalar(
                    out=dv_, in0=src[:, i].rearrange("p (a q) c -> p a q c", a=4), scalar1=sel[:, gl:gl + 1],
                    scalar2=(-1.0 if (i == 1 and neg_im) else 1.0), op0=ALU.mult, op1=ALU.mult),
                    reads=rg + r_ld + [r_zero], writes=rg)

    build_pad(Cpad, craw, True)
    for j in range(DEC - 1):
        lre_b = bc_mid(pq[:, j, :], 16)
        lim_b = bc_mid(pq[:, 8 + j, :], 16)
        TT(clraw[:, 0], craw[:, 0], lre_b, ALU.mult)
        TT(btmp, craw[:, 1], lim_b, ALU.mult)
        TT(clraw[:, 0], clraw[:, 0], btmp, ALU.subtract)
        TT(clraw[:, 1], craw[:, 0], lim_b, ALU.mult)
        TT(btmp, craw[:, 1], lre_b, ALU.mult)
        TT(clraw[:, 1], clraw[:, 1], btmp, ALU.add)
        build_pad(CLpad[j], clraw, True)
    mp5 = mpad.rearrange("p (i pr gl c) -> p i pr gl c", i=2, pr=16, gl=2)
    mp3 = mpad.rearrange("p (i j n) -> p i j n", i=2, j=4)
    for d in range(DEC):
        jj = DEC - 1 - d
        for i in range(2):
            for gl in range(2):
                ew("dve", lambda e, i=i, gl=gl, d=d: e.tensor_scalar(out=mp5[:, i, :, gl, :], in0=bbD[d][:, i], scalar1=sel[:, gl:gl + 1],
                                                                   scalar2=None, op0=ALU.mult))
        for i in range(2):
            for j in range(4):
                P.op("pe", lambda e, i=i, j=j: e.transpose(out=C.banks[2 + (j % 2)][:, 0:128], in_=mp3[:, i, j, :], identity=C.identf),
                     reads=rg + [C.r_const], writes=[C.rb[2 + (j % 2)]])
                P.op("dve", lambda e, i=i, j=j, jj=jj: e.tensor_tensor(
                    out=BpadD[jj][:, i, 4 * j:4 * j + 4, :], in0=C.banks[2 + (j % 2)][:, 0:128].unsqueeze(1).broadcast_to([128, 4, 128]),
                    in1=bc_mid(rmask, 128), op=ALU.mult), reads=rg + [C.rb[2 + (j % 2)]], writes=rg)
        if d < DEC - 1:
            build_pad(BpadU, bbD[d], False)
            for ch in range(4):
                pb = 4 + (ch % 2)
                n = 0
                for prl in range(4):
                    for i in range(2):
                        P.op("pe", lambda e, pb=pb, ch=ch, prl=prl, i=i, n=n: e.matmul(
                            C.banks[pb][:, 0:128], lhsT=BpadU[:, i, 4 * ch + prl, :], rhs=Cpad[:, i, 4 * ch + prl, :],
                            start=(n == 0), stop=(n == 7)), reads=rg, writes=[C.rb[pb]])
                        n += 1
                P.op("act", lambda e, pb=pb, ch=ch, d=d: e.copy(out=Kmat[:, ch, d, :], in_=C.banks[pb][:, 0:128]),
                     reads=[C.rb[pb]], writes=rg)
    r_rinit = Res("rinit")
    r_S = [Res("S%d" % bt) for bt in range(NBT)]
    P.op("dve", lambda e: e.memset(rinit, 0.0), reads=rg + r_ld, writes=[r_rinit] + rg)
    P.op("dve", lambda e: e.memset(Sbuf, 0.0), reads=rg, writes=r_S)
    P.off = setup_mark
    P.barrier()

    xt = [P.tile([128, D], F32) for _ in range(4)]
    r_xt = [Res("xt") for _ in range(4)]
    hT2 = [P.tile([128, 8, NB], BF16) for _ in range(2)]
    r_hT2 = [Res("hT") for _ in range(2)]
    wsl = [P.tile([128, 8, 512], BF16) for _ in range(1)]
    r_wsl = [Res("w") for _ in range(1)]
    uT = P.tile([128, 4, NB], F32)
    uTb = P.tile([128, 4, NB], BF16)
    r_u = [Res("u%d" % c) for c in range(4)]
    m4 = [P.tile([128, 512], F32) for _ in range(4)]
    r_m = Res("m4")
    rr = [P.tile([128, 512], F32) for _ in range(2)]
    r_r = Res("r")
    yT = P.tile([128, 4, NB], F32)
    yTb = P.tile([128, 4, NB], BF16)
    r_yT = [Res("yT%d" % c) for c in range(4)]
    gt = [P.tile([128, NB], F32) for _ in range(2)]
    r_gt = Res("gt")
    yco = P.tile([128, 4, NB], BF16)
    r_yco = Res("yco")
    scr = {"ss": P.tile([128, 8], F32), "junk": P.tile([128, D], BF16),
           "hb": [P.tile([128, D], BF16) for _ in range(2)]}
    r_scr = {"ss": Res("ss"), "junk": Res("junk"), "hb": [Res("hb0"), Res("hb1")]}
    if dbg is not None:
        dbg.update(pp=pp.rearrange('p a b -> p (a b)'), pq=pq.rearrange('p a b -> p (a b)'), Sre=Sbuf[:, 0].rearrange('p a b -> p (a b)'), Sim=Sbuf[:, 1].rearrange('p a b -> p (a b)'), uTb0=uTb[:, 0, :], yT0=yT[:, 0, :], rr0=rr[0], rr1=rr[1], B0=BpadD[0][:, 0, 0, :], Bl=BpadD[DEC - 1][:, 0, 0, :], Cp=Cpad[:, 0, 0, :], K0=Kmat[:, 0, 0, :])
    ycT_v = ycT.rearrange("c p t -> p c t")
    GC = 2.0 * math.sqrt(2.0 / math.pi)

    def dview(ap2):
        return ap2.rearrange("p (k j) -> p j k", j=DEC)

    load_x_tiles(P, xin, r_xin, xt, r_xt, 0, "o1x%d")
    rmsnorm_tiles(P, C, xt, r_xt, g_bc, r_g, hT2[0], r_hT2[0], (0, 1), scr, r_scr)
    for blk in range(nblk):
        t0 = blk * NB
        if hook is not None:
            hook(blk)
        hT, r_hT = hT2[blk % 2], r_hT2[blk % 2]
        if blk + 1 < nblk:
            load_x_tiles(P, xin, r_xin, xt, r_xt, blk + 1, "o1x%d")
        ws = 0
        WS.load(P, "ou", wsl[ws], r_wsl[ws], "o1w%d" % ws)
        for c in range(4):
            pb = 2 + (c % 2)
            for k in range(8):
                P.op("pe", lambda e, hT=hT, pb=pb, ws=ws, k=k, c=c: e.matmul(
                    C.banks[pb][:], lhsT=wsl[ws][:, k, c * 128:(c + 1) * 128], rhs=hT[:, k, :],
                    start=(k == 0), stop=(k == 7)), reads=[r_wsl[ws], r_hT], writes=[C.rb[pb]])
            P.op("act", lambda e, pb=pb, c=c: e.copy(out=uT[:, c, :], in_=C.banks[pb][:]), reads=[C.rb[pb]], writes=[r_u[c]])
            P.op("dve", lambda e, c=c: e.tensor_copy(out=uTb[:, c, :], in_=uT[:, c, :]), reads=[r_u[c]], writes=[r_u[c]])

        def issue_w(bt):
            pa, pbk = (4, 5) if bt % 2 == 0 else (6, 7)
            for q in range(PB):
                pr = bt * PB + q
                ch = pr // 4
                for (pbank, i) in ((pa, 0), (pbk, 1)):
                    for j in range(DEC):
                        P.op("pe", lambda e, pbank=pbank, i=i, pr=pr, ch=ch, j=j, q=q: e.matmul(
                            C.banks[pbank][:, q * KD:(q + 1) * KD], lhsT=BpadD[j][:, i, pr, :], rhs=dview(uTb[:, ch, :])[:, j, :],
                            start=(j == 0), stop=(j == DEC - 1)), reads=rg + [r_u[ch]], writes=[C.rb[pbank]])

        issue_w(0)
        for bt in range(NBT):
            pa, pbk = (4, 5) if bt % 2 == 0 else (6, 7)
            if bt + 1 < NBT:
                issue_w(bt + 1)
            p0 = bt * PB
            cb_ = ctD[:, p0:p0 + PB, :].rearrange("p a k -> p (a k)")
            sb_ = stD[:, p0:p0 + PB, :].rearrange("p a k -> p (a k)")
            bre, bim = C.banks[pa][:], C.banks[pbk][:]
            P.op("dve", lambda e, cb_=cb_, bre=bre: e.tensor_tensor(out=m4[0], in0=bre, in1=cb_, op=ALU.mult), reads=rg + [C.rb[pa]], writes=[r_m])
            P.op("dve", lambda e, sb_=sb_, bim=bim: e.tensor_tensor(out=m4[1], in0=bim, in1=sb_, op=ALU.mult), reads=rg + [C.rb[pbk]], writes=[r_m])
            P.op("dve", lambda e, cb_=cb_, bim=bim: e.tensor_tensor(out=m4[2], in0=bim, in1=cb_, op=ALU.mult), reads=rg + [C.rb[pbk]], writes=[r_m])
            P.op("dve", lambda e, sb_=sb_, bre=bre: e.tensor_tensor(out=m4[3], in0=bre, in1=sb_, op=ALU.mult), reads=rg + [C.rb[pa]], writes=[r_m])
            P.op("dve", lambda e: e.tensor_tensor(out=m4[0], in0=m4[0], in1=m4[1], op=ALU.add), reads=[r_m], writes=[r_m])
            P.op("dve", lambda e: e.tensor_tensor(out=m4[2], in0=m4[2], in1=m4[3], op=ALU.subtract), reads=[r_m], writes=[r_m])
            for q in range(PB):
                pr = p0 + q
                rho_b = pq[:, 16, pr:pr + 1].broadcast_to([128, KD])
                for i in range(2):
                    P.op("dve", lambda e, i=i, pr=pr, q=q, rho_b=rho_b: e.tensor_tensor_scan(
                        out=rr[i][:, q * KD:(q + 1) * KD], data0=rho_b, data1=m4[2 * i][:, q * KD:(q + 1) * KD],
                        initial=rinit[:, i, pr:pr + 1], op0=ALU.mult, op1=ALU.add),
                        reads=rg + [r_m, r_rinit], writes=[r_r])
            c5 = pp[:, 12, p0:p0 + PB]
            s5 = pp[:, 13, p0:p0 + PB]
            l_re = rr[0].rearrange("p (a k) -> p a k", a=PB)[:, :, KD - 1]
            l_im = rr[1].rearrange("p (a k) -> p a k", a=PB)[:, :, KD - 1]
            tA = pp[:, 14, p0:p0 + PB]
            tB = pp[:, 15, p0:p0 + PB]
            P.op("dve", lambda e, s5=s5, l_im=l_im, tA=tA: e.tensor_tensor(out=tA, in0=l_im, in1=s5, op=ALU.mult), reads=rg + [r_r], writes=rg)
            P.op("dve", lambda e, s5=s5, l_re=l_re, tB=tB: e.tensor_tensor(out=tB, in0=l_re, in1=s5, op=ALU.mult), reads=rg + [r_r], writes=rg)
            P.op("dve", lambda e, c5=c5, l_re=l_re, p0=p0: e.tensor_tensor(out=rinit[:, 0, p0:p0 + PB], in0=l_re, in1=c5, op=ALU.mult),
                 reads=rg + [r_r], writes=[r_rinit])
            P.op("dve", lambda e, c5=c5, l_im=l_im, p0=p0: e.tensor_tensor(out=rinit[:, 1, p0:p0 + PB], in0=l_im, in1=c5, op=ALU.mult),
                 reads=rg + [r_r], writes=[r_rinit])
            P.op("dve", lambda e, tA=tA, p0=p0: e.tensor_tensor(out=rinit[:, 0, p0:p0 + PB], in0=rinit[:, 0, p0:p0 + PB], in1=tA, op=ALU.subtract),
                 reads=rg + [r_rinit], writes=[r_rinit])
            P.op("dve", lambda e, tB=tB, p0=p0: e.tensor_tensor(out=rinit[:, 1, p0:p0 + PB], in0=rinit[:, 1, p0:p0 + PB], in1=tB, op=ALU.add),
                 reads=rg + [r_rinit], writes=[r_rinit])
            if blk > 0:
                for i in range(2):
                    P.op("act", lambda e, i=i, p0=p0: e.copy(out=Sbuf[:, i, p0:p0 + PB, 0], in_=Sbuf[:, i, p0:p0 + PB, KD]),
                         reads=[r_S[bt]], writes=[r_S[bt]])
            P.op("dve", lambda e, cb_=cb_: e.tensor_tensor(out=m4[0], in0=rr[0], in1=cb_, op=ALU.mult), reads=rg + [r_r, r_m], writes=[r_m])
            P.op("dve", lambda e, sb_=sb_: e.tensor_tensor(out=m4[1], in0=rr[1], in1=sb_, op=ALU.mult), reads=rg + [r_r], writes=[r_m])
            P.op("dve", lambda e, sb_=sb_: e.tensor_tensor(out=m4[2], in0=rr[0], in1=sb_, op=ALU.mult), reads=rg + [r_r], writes=[r_m])
            P.op("dve", lambda e, cb_=cb_: e.tensor_tensor(out=m4[3], in0=rr[1], in1=cb_, op=ALU.mult), reads=rg + [r_r], writes=[r_m])
            P.op("dve", lambda e, p0=p0: e.tensor_tensor(out=Sbuf[:, 0, p0:p0 + PB, 1:KD + 1], in0=m4[0].rearrange("p (a k) -> p a k", a=PB),
                                                         in1=m4[1].rearrange("p (a k) -> p a k", a=PB), op=ALU.subtract),
                 reads=[r_m], writes=[r_S[bt]])
            P.op("dve", lambda e, p0=p0: e.tensor_tensor(out=Sbuf[:, 1, p0:p0 + PB, 1:KD + 1], in0=m4[2].rearrange("p (a k) -> p a k", a=PB),
                                                         in1=m4[3].rearrange("p (a k) -> p a k", a=PB), op=ALU.add),
                 reads=[r_m], writes=[r_S[bt]])
            if (p0 + PB) % 4 == 0:
                ch = (p0 + PB) // 4 - 1
                yb_ = 2 + (ch % 2)
                bts = sorted(set((4 * ch + x) // PB for x in range(4)))
                for j in range(DEC):
                    n = 0
                    ntot = (j + 1 if j < DEC - 1 else 0) + 8
                    if j < DEC - 1:
                        for i in range(j + 1):
                            P.op("pe", lambda e, yb_=yb_, ch=ch, j=j, i=i, n=n, ntot=ntot: e.matmul(
                                C.banks[yb_][:, j * KD:(j + 1) * KD], lhsT=Kmat[:, ch, j - i, :], rhs=dview(uTb[:, ch, :])[:, i, :],
                                start=(n == 0), stop=(n == ntot - 1)), reads=rg + [r_u[ch]], writes=[C.rb[yb_]])
                            n += 1
                    for pr in range(4 * ch, 4 * ch + 4):
                        for i in range(2):
                            if j < DEC - 1:
                                lhs = CLpad[j][:, i, pr, :]
                                rhs = Sbuf[:, i, pr, 0:KD]
                            else:
                                lhs = Cpad[:, i, pr, :]
                                rhs = Sbuf[:, i, pr, 1:KD + 1]
                            P.op("pe", lambda e, yb_=yb_, j=j, lhs=lhs, rhs=rhs, n=n, ntot=ntot: e.matmul(
                                C.banks[yb_][:, j * KD:(j + 1) * KD], lhsT=lhs, rhs=rhs, start=(n == 0), stop=(n == ntot - 1)),
                                reads=rg + [r_S[x] for x in bts], writes=[C.rb[yb_]])
                            n += 1
                last_pair = True
                if last_pair:
                    P.op("dve", lambda e, yb_=yb_, ch=ch: e.scalar_tensor_tensor(
                        out=dview(yT[:, ch, :]), in0=dview(uT[:, ch, :]), scalar=dcol[:, ch:ch + 1],
                        in1=C.banks[yb_][:].rearrange("p (j k) -> p j k", j=DEC), op0=ALU.mult, op1=ALU.add),
                        reads=rg + [C.rb[yb_], r_u[ch]], writes=[r_yT[ch]])
                    P.op("act", lambda e, ch=ch: e.activation(out=gt[0], in_=yT[:, ch, :], func=AF.Square), reads=rg + [r_yT[ch]], writes=[r_gt])
                    P.op("dve", lambda e: e.tensor_scalar(out=gt[0], in0=gt[0], scalar1=0.044715, scalar2=1.0, op0=ALU.mult, op1=ALU.add),
                         reads=[r_gt], writes=[r_gt])
                    P.op("dve", lambda e, ch=ch: e.tensor_tensor(out=gt[0], in0=gt[0], in1=yT[:, ch, :], op=ALU.mult),
                         reads=[r_gt, r_yT[ch]], writes=[r_gt])
                    P.op("act", lambda e: e.activation(out=gt[1], in_=gt[0], func=AF.Sigmoid, scale=GC), reads=[r_gt], writes=[r_gt])
                    P.op("dve", lambda e, ch=ch: e.tensor_tensor(out=yT[:, ch, :], in0=yT[:, ch, :], in1=gt[1], op=ALU.mult),
                         reads=[r_gt, r_yT[ch]], writes=[r_yT[ch]])
                    P.op("act", lambda e, ch=ch: e.copy(out=yTb[:, ch, :], in_=yT[:, ch, :]), reads=[r_yT[ch]], writes=[r_yT[ch]])
        if blk + 1 < nblk:
            rmsnorm_tiles(P, C, xt, r_xt, g_bc, r_g, hT2[(blk + 1) % 2], r_hT2[(blk + 1) % 2], (0, 1), scr, r_scr)
        for co in range(4):
            pb = co % 2
            for ci in range(4):
                P.op("pe", lambda e, pb=pb, ci=ci, co=co: e.matmul(
                    C.banks[pb][:], lhsT=gw[:, ci, co * 128:(co + 1) * 128], rhs=yTb[:, ci, :], start=(ci == 0), stop=(ci == 3)),
                    reads=rg + [r_yT[ci]], writes=[C.rb[pb]])
            P.op("act", lambda e, pb=pb, co=co: e.activation(out=gt[1], in_=C.banks[pb][:], func=AF.Sigmoid,
                                                            bias=gbcol[:, co:co + 1], scale=1.0),
                 reads=rg + [C.rb[pb], r_gt], writes=[r_gt])
            P.op("dve", lambda e, co=co: e.tensor_tensor(out=yco[:, co, :], in0=yT[:, co, :], in1=gt[1], op=ALU.mult),
                 reads=[r_gt, r_yT[co]], writes=[r_yco])
        P.op("sp", lambda e, t0=t0: e.dma_start(out=ycT_v[:, :, t0:t0 + NB], in_=yco), reads=[r_yco], writes=[r_yc[blk]],
             dma_key="o1o")


ODD_SHAPES = {"od_norm": [1, D], "od_w_in": [1, D, 4608], "od_lam_re": [1, 32, 64], "od_lam_im": [1, 32, 64],
              "od_b_re": [1, 32, 64, 16], "od_b_im": [1, 32, 64, 16], "od_c_re": [1, 32, 16, 64], "od_c_im": [1, 32, 16, 64],
              "od_log_dt": [1, 32], "od_s5_d": [1, 512], "od_glu_w": [1, 512, 512], "od_glu_b": [1, 512],
              "od_gn_g": [1, D], "od_gn_b": [1, D], "od_w_out": [1, 1536, D]}
ODD_NAMES = list(ODD_SHAPES.keys())


def build_o1_only(nblk=NBLK, DEC=4):
    nc = bass.Bass("TRN2", target_bir_lowering=False)
    xin = nc.dram_tensor("x", [S, D], F32, kind="ExternalInput").ap()
    Wd = declare(nc, ODD_SHAPES)
    W = {k: v[0] for k, v in Wd.items()}
    iota = nc.dram_tensor("c_iota", [128, NB], F32, kind="ExternalInput").ap()
    ycT = nc.dram_tensor("ycT", [4, 128, S], BF16, kind="ExternalOutput").ap()
    P = Prog(nc, ARENA)
    C = Ctx()
    setup_common(P, C)
    r_in = [Res("xin") for _ in range(32)]
    r_yc = [Res("yc") for _ in range(NBLK)]
    dbg = {}
    WS = WStore(nc)
    define_panels(WS, Wo=W)
    WS.emit_group(P, "D")
    phase_o1(P, C, WS, xin, r_in, W, iota, ycT, r_yc, nblk=nblk, dbg=dbg, DEC=DEC)
    P.barrier()
    r_d = Res("dbg")
    for k, ap in dbg.items():
        o = nc.dram_tensor("dbg_" + k, [ap.shape[0], ap.shape[1]], F32, kind="ExternalOutput").ap()
        P.op("pool", lambda e, o=o, ap=ap: e.dma_start(out=o, in_=ap), writes=[r_d], dma_key="dbg")
    P.op("sp", None, reads=r_yc[:nblk] + [r_d])
    P.emit()
    return nc, P


RET_GAMMA = (1.0 - np.exp(np.linspace(math.log(1.0 / 32), math.log(1.0 / 512), 4, dtype=np.float32))).astype(np.float32)


def host_consts():
    c = {}
    c["c_iota"] = np.broadcast_to(np.arange(NB, dtype=np.float32)[None, :], (128, NB)).copy()
    inv_freq = (np.float32(10000.0) ** (-np.arange(0, 256, 2, dtype=np.float32) / np.float32(256))).astype(np.float32)
    ang = (np.arange(S, dtype=np.float32)[None, :] * inv_freq[:, None]).astype(np.float32)
    c["c_cos"] = np.cos(ang).astype(np.float32)
    c["c_sin"] = np.sin(ang).astype(np.float32)
    lg = np.log(RET_GAMMA.astype(np.float64))
    idx = np.arange(128, dtype=np.float64)
    diff = idx[None, :] - idx[:, None]
    dm = np.where(diff >= 0, np.exp(lg[:, None, None] * np.maximum(diff, 0.0)[None]), 0.0) / 16.0
    c["c_dmat"] = np.ascontiguousarray(dm.transpose(1, 0, 2)).astype(np.float32).reshape(128, 512)
    kd = np.exp(lg[None, :] * (127.0 - idx)[:, None]) / 16.0
    qd = np.exp(lg[None, :] * (idx + 1.0)[:, None])
    c["c_kq"] = np.concatenate([kd, qd], axis=1).astype(np.float32)
    return c


CONST_SHAPES = {"c_iota": [128, NB], "c_cos": [128, S], "c_sin": [128, S], "c_dmat": [128, 512], "c_kq": [128, 8]}


def phase_o2(P, C, WS, xin, r_xin, W, K_, ycT, r_yc, xout, r_xout, nblk=NBLK, hook=None):
    P.reset()
    QC0, KC0, VC0, GC0 = 512, 1536, 2560, 3584
    r_g = Res("g")
    rg = [r_g]
    key = "o2g"
    g_bc = P.tile([128, D], F32)
    gng = P.tile([128, D], F32)
    gnb = P.tile([128, D], F32)
    load_bcast_vec(P, g_bc, W["od_norm"], r_g, key)
    load_bcast_vec(P, gng, W["od_gn_g"], r_g, key)
    load_bcast_vec(P, gnb, W["od_gn_b"], r_g, key)
    dmat = P.tile([128, 4, 128], F32)
    kq = P.tile([128, 8], F32)
    P.op("sp", lambda e: e.dma_start(out=dmat, in_=K_["c_dmat"].rearrange("p (h l) -> p h l", h=4)), writes=rg, dma_key=key)
    P.op("sp", lambda e: e.dma_start(out=kq, in_=K_["c_kq"]), writes=rg, dma_key=key)
    cg = [float(np.float64(RET_GAMMA[h]) ** 128) for h in range(4)]
    prev = P.tile([128, 4, 512], F32)
    prev_bf = P.tile([128, 4, 512], BF16)
    r_prev = [Res("prev%d" % h) for h in range(4)]
    r_prevbf = [Res("prevbf%d" % h) for h in range(4)]
    P.op("dve", lambda e: e.memset(prev, 0.0), writes=r_prev)
    P.op("dve", lambda e: e.memset(prev_bf, 0.0), writes=r_prevbf)
    xt = [P.tile([128, D], F32) for _ in range(4)]
    r_xt = [Res("xt") for _ in range(4)]
    hT2 = [P.tile([128, 8, NB], BF16) for _ in range(2)]
    r_hT2 = [Res("hT") for _ in range(2)]
    NXR = 8
    xres = [P.tile([128, 512], F32) for _ in range(NXR)]
    r_xres = [Res("xres") for _ in range(NXR)]
    xrc = 0
    NW = 4
    wsl = [P.tile([128, 8, 512], BF16) for _ in range(NW)]
    r_wsl = [Res("w") for _ in range(NW)]
    cs = P.tile([128, NB], F32)
    sn = P.tile([128, NB], F32)
    r_cs = Res("cs")
    qT = P.tile([128, 8, NB], BF16)
    kT = P.tile([128, 8, NB], BF16)
    r_qT = [Res("qT%d" % h) for h in range(4)]
    r_kT = [Res("kT%d" % h) for h in range(4)]
    v_tm = P.tile([128, 4, D], BF16)
    r_v = [Res("v%d" % i) for i in range(4)]
    sgate = [P.tile([128, D], F32) for _ in range(4)]
    r_sg = [Res("sg%d" % i) for i in range(4)]
    mixT2 = [P.tile([128, 12, NB], BF16) for _ in range(2)]
    r_mix_c2 = [Res("mixc") for _ in range(2)]
    r_mix_d2 = [[Res("mixd%d" % q) for q in range(4)] for _ in range(2)]
    pending_wout = None
    mm_ = [P.tile([128, NB], F32) for _ in range(4)]
    r_mm = [Res("mm%d" % i) for i in range(2)]
    sc_bf = P.tile([128, 4, 128], BF16)
    r_sc = Res("sc")
    ktd = P.tile([128, D], BF16)
    r_ktd = Res("ktd")
    tmp = P.tile([128, D], F32)
    r_tmp = Res("tmp")
    ov = P.tile([128, D], F32)
    r_o = Res("o")
    st8 = P.tile([128, 4, 4], F32)
    r_st = Res("st8")
    yd_tm = P.tile([128, D], BF16)
    r_yd = Res("yd")
    scr = {"ss": P.tile([128, 8], F32), "junk": P.tile([128, D], BF16),
           "hb": [P.tile([128, D], BF16) for _ in range(2)]}
    r_scr = {"ss": Res("ss"), "junk": Res("junk"), "hb": [Res("hb0"), Res("hb1")]}
    wcnt = 0
    rcnt = 0

    def next_slot():
        nonlocal wcnt
        ws = wcnt % NW
        wcnt += 1
        return ws

    Win = W["od_w_in"]
    ycT_v = ycT.rearrange("c p t -> p c t")
    pcnt = 0
    load_x_tiles(P, xin, r_xin, xt, r_xt, 0, "o2x%d")
    rmsnorm_tiles(P, C, xt, r_xt, g_bc, r_g, hT2[0], r_hT2[0], (0, 1), scr, r_scr)
    for blk in range(nblk):
        t0 = blk * NB
        if hook is not None:
            hook(blk)
        hT, r_hT = hT2[blk % 2], r_hT2[blk % 2]
        mixT, r_mix_c, r_mix_d = mixT2[blk % 2], r_mix_c2[blk % 2], r_mix_d2[blk % 2]
        if blk + 1 < nblk:
            load_x_tiles(P, xin, r_xin, xt, r_xt, blk + 1, "o2x%d")
        P.op("sp", lambda e, t0=t0: e.dma_start(out=cs, in_=K_["c_cos"][:, t0:t0 + NB]), writes=[r_cs], dma_key="o2t")
        P.op("sp", lambda e, t0=t0: e.dma_start(out=sn, in_=K_["c_sin"][:, t0:t0 + NB]), writes=[r_cs], dma_key="o2t")
        P.op("sp", lambda e, t0=t0, mixT=mixT: e.dma_start(out=mixT[:, 0:4, :], in_=ycT_v[:, :, t0:t0 + NB]),
             reads=[r_yc[blk]], writes=[r_mix_c], dma_key="o2yc")
        for (dstT, r_dst, c0, wnm) in ((qT, r_qT, QC0, "oq"), (kT, r_kT, KC0, "ok")):
            for pn in range(2):
                ws = next_slot()
                WS.load(P, "%s%d" % (wnm, pn), wsl[ws], r_wsl[ws], "o2w%d" % ws)
                for hh in range(2):
                    h = 2 * pn + hh
                    pbase = 2 if (rcnt % 2 == 0) else 4
                    rcnt += 1
                    for half in range(2):
                        cc = hh * 2 + half
                        pb = pbase + half
                        for k in range(8):
                            P.op("pe", lambda e, hT=hT, pb=pb, ws=ws, k=k, cc=cc: e.matmul(
                                C.banks[pb][:], lhsT=wsl[ws][:, k, cc * 128:(cc + 1) * 128], rhs=hT[:, k, :],
                                start=(k == 0), stop=(k == 7)), reads=[r_wsl[ws], r_hT], writes=[C.rb[pb]])
                    A_, B_ = C.banks[pbase][:], C.banks[pbase + 1][:]
                    P.op("dve", lambda e, A_=A_: e.tensor_tensor(out=mm_[0], in0=A_, in1=cs, op=ALU.mult),
                         reads=[C.rb[pbase], r_cs], writes=[r_mm[0]])
                    P.op("dve", lambda e, B_=B_: e.tensor_tensor(out=mm_[1], in0=B_, in1=sn, op=ALU.mult),
                         reads=[C.rb[pbase + 1], r_cs], writes=[r_mm[0]])
                    P.op("dve", lambda e, A_=A_: e.tensor_tensor(out=mm_[2], in0=A_, in1=sn, op=ALU.mult),
                         reads=[C.rb[pbase], r_cs], writes=[r_mm[1]])
                    P.op("dve", lambda e, B_=B_: e.tensor_tensor(out=mm_[3], in0=B_, in1=cs, op=ALU.mult),
                         reads=[C.rb[pbase + 1], r_cs], writes=[r_mm[1]])
                    P.op("dve", lambda e, h=h, dstT=dstT: e.tensor_tensor(out=dstT[:, 2 * h, :], in0=mm_[0], in1=mm_[1], op=ALU.subtract),
                         reads=[r_mm[0]], writes=[r_dst[h]])
                    P.op("dve", lambda e, h=h, dstT=dstT: e.tensor_tensor(out=dstT[:, 2 * h + 1, :], in0=mm_[2], in1=mm_[3], op=ALU.add),
                         reads=[r_mm[1]], writes=[r_dst[h]])
        for (c0, isg) in ((VC0, False), (GC0, True)):
            for pn in range(2):
                ws = next_slot()
                WS.load(P, "%s%d" % ("og" if isg else "ov", pn), wsl[ws], r_wsl[ws], "o2w%d" % ws)
                for i in range(4):
                    pb = 2 + (pcnt % 2)
                    pcnt += 1
                    for k in range(8):
                        P.op("pe", lambda e, hT=hT, pb=pb, ws=ws, k=k, i=i: e.matmul(
                            C.banks[pb][:], lhsT=hT[:, k, i * 128:(i + 1) * 128], rhs=wsl[ws][:, k, :],
                            start=(k == 0), stop=(k == 7)), reads=[r_wsl[ws], r_hT], writes=[C.rb[pb]])
                    if isg:
                        P.op("act", lambda e, pb=pb, i=i, pn=pn: e.activation(out=sgate[i][:, pn * 512:(pn + 1) * 512],
                                                                               in_=C.banks[pb][:], func=AF.Silu),
                             reads=[C.rb[pb]], writes=[r_sg[i]])
                    else:
                        P.op("act", lambda e, pb=pb, i=i, pn=pn: e.copy(out=v_tm[:, i, pn * 512:(pn + 1) * 512], in_=C.banks[pb][:]),
                             reads=[C.rb[pb]], writes=[r_v[i]])
        pend = []
        if pending_wout is not None:
            pending_wout["start"]()
        for q in range(4):
            tq = q * 128
            for h in range(4):
                for dc in range(2):
                    P.op("pe", lambda e, h=h, dc=dc, tq=tq: e.matmul(
                        C.banks[4][:, h * 128:(h + 1) * 128], lhsT=kT[:, 2 * h + dc, tq:tq + 128], rhs=qT[:, 2 * h + dc, tq:tq + 128],
                        start=(dc == 0), stop=(dc == 1)), reads=[r_kT[h], r_qT[h]], writes=[C.rb[4]])
            P.op("dve", lambda e: e.tensor_tensor(out=sc_bf, in0=C.banks[4][:].rearrange("p (h l) -> p h l", h=4), in1=dmat, op=ALU.mult),
                 reads=rg + [C.rb[4]], writes=[r_sc])
            tb0 = bank_bf(C, 0)
            for c in range(8):
                P.op("pe", lambda e, c=c, tq=tq, tb0=tb0: e.transpose(out=tb0[:, c * 128:(c + 1) * 128], in_=kT[:, c, tq:tq + 128],
                                                                      identity=C.ident),
                     reads=[r_kT[c // 2], C.r_const], writes=[C.rb[0]])
            if pending_wout is not None:
                pending_wout["piece"](q // 2, 2 * (q % 2), 1)
            P.op("dve", lambda e, tb0=tb0: e.tensor_tensor(out=ktd.rearrange("p (h d) -> p h d", h=4),
                                                           in0=tb0.rearrange("p (h d) -> p h d", h=4),
                                                           in1=bc_mid(kq[:, 0:4], 256), op=ALU.mult),
                 reads=rg + [C.rb[0]], writes=[r_ktd])
            for h in range(4):
                pb = 6 + h // 2
                P.op("pe", lambda e, pb=pb, h=h, q=q: e.matmul(
                    C.banks[pb][:, (h % 2) * 256:(h % 2 + 1) * 256], lhsT=sc_bf[:, h, :], rhs=v_tm[:, q, h * 256:(h + 1) * 256],
                    start=True, stop=True), reads=[r_sc, r_v[q]], writes=[C.rb[pb]])
            for h in range(4):
                pb = 2 + h // 2
                for dc in range(2):
                    P.op("pe", lambda e, pb=pb, h=h, dc=dc, tq=tq: e.matmul(
                        C.banks[pb][:, (h % 2) * 256:(h % 2 + 1) * 256], lhsT=qT[:, 2 * h + dc, tq:tq + 128],
                        rhs=prev_bf[:, h, dc * 256:(dc + 1) * 256], start=(dc == 0), stop=(dc == 1)),
                        reads=[r_qT[h], r_prevbf[h]], writes=[C.rb[pb]])
            for h in range(4):
                pb = 2 + h // 2
                P.op("act", lambda e, pb=pb, h=h: e.activation(
                    out=tmp[:, h * 256:(h + 1) * 256], in_=C.banks[pb][:, (h % 2) * 256:(h % 2 + 1) * 256], func=AF.Identity,
                    scale=kq[:, 4 + h:5 + h]), reads=rg + [C.rb[pb]], writes=[r_tmp])
            for hf in range(2):
                P.op("dve", lambda e, hf=hf: e.tensor_tensor(out=ov[:, hf * 512:(hf + 1) * 512], in0=tmp[:, hf * 512:(hf + 1) * 512],
                                                             in1=C.banks[6 + hf][:], op=ALU.add),
                     reads=[r_tmp, C.rb[6 + hf]], writes=[r_o])
            for h in range(4):
                for dc in range(2):
                    P.op("pe", lambda e, h=h, dc=dc, q=q: e.matmul(
                        C.banks[5][:, dc * 256:(dc + 1) * 256], lhsT=ktd[:, h * 256 + dc * 128:h * 256 + (dc + 1) * 128],
                        rhs=v_tm[:, q, h * 256:(h + 1) * 256], start=True, stop=True),
                        reads=[r_ktd, r_v[q]], writes=[C.rb[5]])
                P.op("dve", lambda e, h=h: e.scalar_tensor_tensor(out=prev[:, h, :], in0=prev[:, h, :], scalar=cg[h], in1=C.banks[5][:],
                                                                  op0=ALU.mult, op1=ALU.add),
                     reads=[C.rb[5], r_prev[h]], writes=[r_prev[h]])
                P.op("act", lambda e, h=h: e.copy(out=prev_bf[:, h, :], in_=prev[:, h, :]), reads=[r_prev[h]], writes=[r_prevbf[h]])
            if pending_wout is not None:
                pending_wout["piece"](q // 2, 2 * (q % 2) + 1, 1)
            while pend:
                pend.pop(0)()
            for h in range(4):
                hs = slice(h * 256, (h + 1) * 256)
                P.op("act", lambda e, h=h, hs=hs: e.activation(out=tmp[:, hs], in_=ov[:, hs], func=AF.Identity, accum_out=st8[:, 0, h:h + 1]),
                     reads=[r_o], writes=[r_tmp, r_st])
                P.op("act", lambda e, h=h, hs=hs: e.activation(out=tmp[:, hs], in_=ov[:, hs], func=AF.Square, accum_out=st8[:, 1, h:h + 1]),
                     reads=[r_o], writes=[r_tmp, r_st])
            P.op("dve", lambda e: e.tensor_scalar(out=st8[:, 2, :], in0=st8[:, 0, :], scalar1=1.0 / 256, scalar2=None, op0=ALU.mult),
                 reads=[r_st], writes=[r_st])
            P.op("dve", lambda e: e.tensor_tensor(out=st8[:, 0, :], in0=st8[:, 2, :], in1=st8[:, 2, :], op=ALU.mult), reads=[r_st], writes=[r_st])
            P.op("dve", lambda e: e.scalar_tensor_tensor(out=st8[:, 3, :], in0=st8[:, 1, :], scalar=1.0 / 256, in1=st8[:, 0, :],
                                                         op0=ALU.mult, op1=ALU.subtract), reads=[r_st], writes=[r_st])
            P.op("act", lambda e: e.activation(out=st8[:, 3, :], in_=st8[:, 3, :], func=AF.Ln, bias=EPS, scale=1.0), reads=[r_st], writes=[r_st])
            P.op("act", lambda e: e.activation(out=st8[:, 3, :], in_=st8[:, 3, :], func=AF.Exp, scale=-0.5), reads=[r_st], writes=[r_st])
            for h in range(4):
                hs = slice(h * 256, (h + 1) * 256)
                P.op("dve", lambda e, h=h, hs=hs: e.tensor_scalar(out=ov[:, hs], in0=ov[:, hs], scalar1=st8[:, 2, h:h + 1],
                                                                   scalar2=st8[:, 3, h:h + 1], op0=ALU.subtract, op1=ALU.mult),
                     reads=[r_st, r_o], writes=[r_o])
            P.op("dve", lambda e: e.tensor_tensor(out=ov, in0=ov, in1=gng, op=ALU.mult), reads=rg + [r_o], writes=[r_o])
            P.op("dve", lambda e: e.tensor_tensor(out=ov, in0=ov, in1=gnb, op=ALU.add), reads=rg + [r_o], writes=[r_o])
            P.op("dve", lambda e, q=q: e.tensor_tensor(out=yd_tm, in0=ov, in1=sgate[q], op=ALU.mult), reads=[r_o, r_sg[q]], writes=[r_yd])
            tb1 = bank_bf(C, 1)

            def emit_yd(q=q, tq=tq, tb1=tb1, mixT=mixT, r_mix_d=r_mix_d):
                for c in range(8):
                    P.op("pe", lambda e, c=c, tb1=tb1: e.transpose(out=tb1[:, c * 128:(c + 1) * 128], in_=yd_tm[:, c * 128:(c + 1) * 128],
                                                                   identity=C.ident), reads=[r_yd, C.r_const], writes=[C.rb[1]])
                P.op("act", lambda e, tb1=tb1, tq=tq, mixT=mixT: e.copy(out=mixT[:, 4:12, tq:tq + 128], in_=tb1.rearrange("p (k n) -> p k n", k=8)),
                     reads=[C.rb[1]], writes=[r_mix_d[q]])
            pend.append(emit_yd)
        while pend:
            pend.pop(0)()
        if blk + 1 < nblk:
            rmsnorm_tiles(P, C, xt, r_xt, g_bc, r_g, hT2[(blk + 1) % 2], r_hT2[(blk + 1) % 2], (0, 1), scr, r_scr)
        def make_wout(blk=blk, t0=t0, mixT=mixT, r_mix_c=r_mix_c, r_mix_d=r_mix_d):
            xr_of = {}
            slots = {}

            def issue_xres(half, i):
                nonlocal xrc
                k_ = xrc % NXR
                xrc += 1
                xr_of[(half, i)] = k_
                P.op("sp", lambda e, k_=k_, i=i, half=half: e.dma_start(
                    out=xres[k_], in_=xin[t0 + i * 128:t0 + (i + 1) * 128, half * 512:(half + 1) * 512]),
                    reads=[r_xin[blk * 4 + i]], writes=[r_xres[k_]], dma_key="o2r" + str(k_))

            def start():
                nonlocal wcnt
                for hi in [(half, i) for half in range(2) for i in range(4)][:NXR]:
                    issue_xres(*hi)
                for half in range(2):
                    while wcnt % NW not in (0, 2):
                        wcnt += 1
                    ws = next_slot()
                    ws2 = next_slot()
                    WS.load(P, "oo%d0" % half, wsl[ws], r_wsl[ws], "o2w%d" % ws)
                    WS.load(P, "oo%d1" % half, wsl[ws2], r_wsl[ws2], "o2w%d" % ws2)
                    slots[half] = (ws, ws2)

            def piece(half, i, pb):
                ws, ws2 = slots[half]
                for c in range(12):
                    wsx = ws if c < 8 else ws2
                    P.op("pe", lambda e, pb=pb, c=c, i=i, wsx=wsx: e.matmul(
                        C.banks[pb][:], lhsT=mixT[:, c, i * 128:(i + 1) * 128], rhs=wsl[wsx][:, c % 8, :],
                        start=(c == 0), stop=(c == 11)),
                        reads=[r_mix_d[i], r_mix_c, r_wsl[wsx]], writes=[C.rb[pb]])
                if (half, i) not in xr_of:
                    issue_xres(half, i)
                k_ = xr_of[(half, i)]
                xr_ = xres[k_]
                r_xr = r_xres[k_]
                P.op("dve", lambda e, pb=pb, xr_=xr_: e.tensor_tensor(out=xr_, in0=xr_, in1=C.banks[pb][:], op=ALU.add),
                     reads=[C.rb[pb], r_xr], writes=[r_xr])
                P.op("sp", lambda e, i=i, half=half, xr_=xr_: e.dma_start(
                    out=xout[t0 + i * 128:t0 + (i + 1) * 128, half * 512:(half + 1) * 512], in_=xr_),
                    reads=[r_xr], writes=[r_xout[blk * 4 + i]], dma_key="o2o")

            return {"start": start, "piece": piece}

        pending_wout = make_wout()
    if pending_wout is not None:
        pending_wout["start"]()
        n_ = 0
        for half in range(2):
            for i in range(4):
                pending_wout["piece"](half, i, 2 + (n_ % 2))
                n_ += 1


def build_odd_only(nblk=NBLK):
    nc = bass.Bass("TRN2", target_bir_lowering=False)
    xin = nc.dram_tensor("x", [S, D], F32, kind="ExternalInput").ap()
    Wd = declare(nc, ODD_SHAPES)
    W = {k: v[0] for k, v in Wd.items()}
    K_ = declare(nc, CONST_SHAPES)
    ycT = nc.dram_tensor("ycT", [4, 128, S], BF16, kind="Internal").ap()
    out = nc.dram_tensor("out", [S, D], F32, kind="ExternalOutput").ap()
    P = Prog(nc, ARENA)
    C = Ctx()
    setup_common(P, C)
    r_in = [Res("xin") for _ in range(32)]
    r_yc = [Res("yc") for _ in range(NBLK)]
    r_out = [Res("xout") for _ in range(32)]
    WS = WStore(nc)
    define_panels(WS, Wo=W)
    WS.emit_group(P, "D")
    phase_o1(P, C, WS, xin, r_in, W, K_["c_iota"], ycT, r_yc, nblk=nblk)
    P.barrier()
    phase_o2(P, C, WS, xin, r_in, W, K_, ycT, r_yc, out, r_out, nblk=nblk)
    P.op("sp", None, reads=r_out[:nblk * 4])
    P.emit()
    return nc, P


FFN_SHAPES = {"ffn_norm": [2, D], "ffn_w_gate": [2, D, FF], "ffn_w_up": [2, D, FF], "ffn_w_down": [2, FF, D],
              "final_norm": [D]}


def build_full(nblk=NBLK):
    nc = bass.Bass("TRN2", target_bir_lowering=False)
    xin = nc.dram_tensor("x", [S, D], F32, kind="ExternalInput").ap()
    We = {k: v[0] for k, v in declare(nc, EVEN_SHAPES).items()}
    Wo = {k: v[0] for k, v in declare(nc, ODD_SHAPES).items()}
    Wf = declare(nc, FFN_SHAPES)
    K_ = declare(nc, CONST_SHAPES)
    ybT = nc.dram_tensor("ybT", [8, 128, S], BF16, kind="Internal").ap()
    ycT = nc.dram_tensor("ycT", [4, 128, S], BF16, kind="Internal").ap()
    xa = nc.dram_tensor("xa", [S, D], F32, kind="Internal").ap()
    xb = nc.dram_tensor("xb", [S, D], F32, kind="Internal").ap()
    out = nc.dram_tensor("out", [S, D], F32, kind="ExternalOutput").ap()
    P = Prog(nc, ARENA)
    C = Ctx()
    setup_common(P, C)

    def rl(n, name):
        return [Res(name) for _ in range(n)]
    r_x, r_xa, r_xb, r_out = rl(32, "x"), rl(32, "xa"), rl(32, "xb"), rl(32, "out")
    r_yb, r_yc = rl(NBLK, "yb"), rl(NBLK, "yc")
    WS = WStore(nc)
    define_panels(WS, We=We, Wo=Wo, Wf=Wf)
    WS.emit_group(P, "A")
    WS.emit_group(P, "B")
    phase_e1(P, C, WS, xin, r_x, We, ybT, r_yb, nblk=nblk)
    P.barrier()
    phase_e2(P, C, WS, xin, r_x, We, ybT, r_yb, xa, r_xa, nblk=nblk, after_first_loads=lambda: WS.emit_group(P, "C"))
    P.barrier()
    WS.emit_group(P, "D")
    phase_ffn(P, C, WS, 0, xa, r_xa, xb, r_xb, Wf["ffn_norm"][0], nblk=nblk, tag="f")
    P.barrier()
    WS.emit_group(P, "E")
    phase_o1(P, C, WS, xb, r_xb, Wo, K_["c_iota"], ycT, r_yc, nblk=nblk)
    P.barrier()
    phase_o2(P, C, WS, xb, r_xb, Wo, K_, ycT, r_yc, xa, r_xa, nblk=nblk)
    P.barrier()
    phase_ffn(P, C, WS, 1, xa, r_xa, out, r_out, Wf["ffn_norm"][1], g_final=Wf["final_norm"], nblk=nblk, tag="f")
    P.op("sp", None, reads=r_out[:nblk * 4])
    P.emit()
    return nc, P


_CACHE = {}


def kernel(**inputs):
    n = 8
    if "nc" not in _CACHE:
        _CACHE["nc"] = build_full()[0]
        _CACHE["consts"] = host_consts()
    nc = _CACHE["nc"]
    consts = _CACHE["consts"]
    x = np.ascontiguousarray(np.asarray(inputs["x"], dtype=np.float32))
    shared = {}
    for k in list(EVEN_SHAPES) + list(ODD_SHAPES) + list(FFN_SHAPES):
        shared[k] = np.ascontiguousarray(np.asarray(inputs[k], dtype=np.float32))
    shared.update(consts)
    in_maps = [dict(x=x[c], **shared) for c in range(n)]
    res = run_bass_kernel_spmd(nc, in_maps, core_ids=list(range(n)))
    return np.stack([np.asarray(r["out"], dtype=np.float32) for r in res.results], axis=0)
```
